# Optimizing a Trainium2 kernel written in Bass

```python
import jax, jax.numpy as jnp
from jax import lax
import numpy as np

D_MODEL = 1024
BATCH = 4
SEQ = 4096
DEPTH = 2

POOL_WINDOWS = (2, 4, 8, 16)
POOL_WIDTH = 512
POOL_GROUP = POOL_WIDTH // len(POOL_WINDOWS)
N_HEADS = 8
HEAD_DIM = 64
ATTN_WIDTH = N_HEADS * HEAD_DIM
MOBA_BLOCK = 256
MOBA_TOP_K = 3
Q_CHUNK = 64
ROPE_THETA = 10000.0
N_BRANCH = 2
IN_WIDTH = POOL_WIDTH + 3 * ATTN_WIDTH + N_BRANCH * D_MODEL
D_FF = 2816
EPS = 1e-6
NEG = -1e30

kernel_name = "hybrid_pool_moba_macaron"


def rms_norm(x, g):
    xf = x.astype(jnp.float32)
    y = xf * lax.rsqrt(jnp.mean(xf * xf, axis=-1, keepdims=True) + EPS)
    return (y * g.astype(jnp.float32)).astype(x.dtype)


def swiglu(h, w_gate_up, w_down):
    a, b = jnp.split(h @ w_gate_up, 2, axis=-1)
    return (jax.nn.silu(a) * b) @ w_down


def rope(x):
    S, Dh = x.shape[1], x.shape[-1]
    half = Dh // 2
    inv_freq = 1.0 / (ROPE_THETA ** (jnp.arange(half, dtype=jnp.float32) * (2.0 / Dh)))
    ang = jnp.arange(S, dtype=jnp.float32)[:, None] * inv_freq[None, :]
    cos = jnp.cos(ang)[None, :, None, :]
    sin = jnp.sin(ang)[None, :, None, :]
    xf = x.astype(jnp.float32)
    x1, x2 = xf[..., :half], xf[..., half:]
    out = jnp.concatenate([x1 * cos - x2 * sin, x2 * cos + x1 * sin], axis=-1)
    return out.astype(x.dtype)


def multiscale_pool(u, w_grp, scale):
    B, S, _ = u.shape
    G = len(POOL_WINDOWS)
    uf = u.astype(jnp.float32).reshape(B, S, G, POOL_GROUP)
    csum = jnp.concatenate([jnp.zeros((B, 1, G, POOL_GROUP), jnp.float32),
                            jnp.cumsum(uf, axis=1)], axis=1)
    t = jnp.arange(S)
    means = []
    for g, w in enumerate(POOL_WINDOWS):
        lo = jnp.maximum(t + 1 - w, 0)
        cnt = (t + 1 - lo).astype(jnp.float32)
        means.append((csum[:, 1:, g] - csum[:, lo, g]) / cnt[None, :, None])
    pooled = jnp.stack(means, axis=2)
    d = (pooled - uf).astype(u.dtype)
    y = jnp.einsum('bsgc,gcd->bsgd', d, w_grp).reshape(B, S, POOL_WIDTH)
    return y * scale


def moba_attention(q, k, v):
    B, S, H, Dh = q.shape
    nb = -(-S // MOBA_BLOCK)
    pad = nb * MOBA_BLOCK - S
    kp = jnp.pad(k, ((0, 0), (0, pad), (0, 0), (0, 0)))
    vp = jnp.pad(v, ((0, 0), (0, pad), (0, 0), (0, 0)))
    kb = kp.reshape(B, nb, MOBA_BLOCK, H, Dh).transpose(0, 3, 1, 2, 4)
    vb = vp.reshape(B, nb, MOBA_BLOCK, H, Dh).transpose(0, 3, 1, 2, 4)
    kbar = jnp.mean(kb.astype(jnp.float32), axis=3)
    nc = S // Q_CHUNK
    qc = q.reshape(B, nc, Q_CHUNK, H, Dh).transpose(1, 0, 3, 2, 4)
    k_eff = min(MOBA_TOP_K, nb)
    scale = Dh ** -0.5
    bi = jnp.arange(B)[:, None, None, None]
    hi = jnp.arange(H)[None, :, None, None]
    blk_ids = jnp.arange(nb)

    def one_chunk(args):
        qi, ci = args
        q0 = ci * Q_CHUNK
        qblk = q0 // MOBA_BLOCK
        qpos = q0 + jnp.arange(Q_CHUNK)
        gate = jnp.einsum('bhqd,bhnd->bhqn', qi.astype(jnp.float32), kbar)
        gate = jnp.where(blk_ids < qblk, gate, NEG)
        _, sel = lax.top_k(gate, k_eff)
        valid = sel < qblk
        ksel = kb[bi, hi, sel]
        vsel = vb[bi, hi, sel]
        kown = lax.dynamic_index_in_dim(kb, qblk, axis=2, keepdims=False)
        vown = lax.dynamic_index_in_dim(vb, qblk, axis=2, keepdims=False)
        s_sel = jnp.einsum('bhqd,bhqknd->bhqkn', qi, ksel).astype(jnp.float32) * scale
        s_sel = jnp.where(valid[..., None], s_sel, NEG).reshape(B, H, Q_CHUNK, k_eff * MOBA_BLOCK)
        kpos = qblk * MOBA_BLOCK + jnp.arange(MOBA_BLOCK)
        s_own = jnp.einsum('bhqd,bhnd->bhqn', qi, kown).astype(jnp.float32) * scale
        s_own = jnp.where(kpos[None, :] <= qpos[:, None], s_own, NEG)
        p = jax.nn.softmax(jnp.concatenate([s_sel, s_own], axis=-1), axis=-1).astype(v.dtype)
        p_sel = p[..., :k_eff * MOBA_BLOCK].reshape(B, H, Q_CHUNK, k_eff, MOBA_BLOCK)
        p_own = p[..., k_eff * MOBA_BLOCK:]
        return (jnp.einsum('bhqkn,bhqknd->bhqd', p_sel, vsel)
                + jnp.einsum('bhqn,bhnd->bhqd', p_own, vown))

    out = lax.map(one_chunk, (qc, jnp.arange(nc)))
    return out.transpose(1, 0, 3, 2, 4).reshape(B, S, H * Dh)


def hybrid_layer(x, ffn1_norm, ffn1_w_gate_up, ffn1_w_down, mix_norm, w_in, b_gate,
                 pool_w, pool_scale, q_norm, k_norm, w_branch_pool, w_branch_attn, w_out,
                 ffn2_norm, ffn2_w_gate_up, ffn2_w_down):
    B, S, D = x.shape
    x = x + 0.5 * swiglu(rms_norm(x, ffn1_norm), ffn1_w_gate_up, ffn1_w_down)
    h = rms_norm(x, mix_norm)
    z = h @ w_in
    o1 = POOL_WIDTH
    o2 = o1 + ATTN_WIDTH
    o3 = o2 + ATTN_WIDTH
    o4 = o3 + ATTN_WIDTH
    u, q, k, v, gl = z[..., :o1], z[..., o1:o2], z[..., o2:o3], z[..., o3:o4], z[..., o4:]
    y_pool = multiscale_pool(u, pool_w, pool_scale)
    q = rope(rms_norm(q.reshape(B, S, N_HEADS, HEAD_DIM), q_norm))
    k = rope(rms_norm(k.reshape(B, S, N_HEADS, HEAD_DIM), k_norm))
    v = v.reshape(B, S, N_HEADS, HEAD_DIM)
    y_attn = moba_attention(q, k, v)
    gates = jax.nn.sigmoid(gl + b_gate).reshape(B, S, N_BRANCH, D)
    merged = gates[:, :, 0] * (y_pool @ w_branch_pool) + gates[:, :, 1] * (y_attn @ w_branch_attn)
    x = x + merged @ w_out
    x = x + 0.5 * swiglu(rms_norm(x, ffn2_norm), ffn2_w_gate_up, ffn2_w_down)
    return x


def setup_inputs(seed: int = 0) -> dict:
    key = jax.random.key(seed)
    ks = jax.random.split(key, 18)
    f32 = jnp.float32

    def w(k, shape, fan_in):
        return jax.random.normal(k, shape, f32) * (fan_in ** -0.5)

    def gain(k, shape):
        return 1.0 + 0.1 * jax.random.normal(k, shape, f32)

    L = DEPTH
    return {
        "x": jax.random.normal(ks[0], (BATCH, SEQ, D_MODEL), f32),
        "ffn1_norm": gain(ks[1], (L, D_MODEL)),
        "ffn1_w_gate_up": w(ks[2], (L, D_MODEL, 2 * D_FF), D_MODEL),
        "ffn1_w_down": w(ks[3], (L, D_FF, D_MODEL), D_FF),
        "mix_norm": gain(ks[4], (L, D_MODEL)),
        "w_in": w(ks[5], (L, D_MODEL, IN_WIDTH), D_MODEL),
        "b_gate": 0.1 * jax.random.normal(ks[6], (L, N_BRANCH * D_MODEL), f32),
        "pool_w": w(ks[7], (L, len(POOL_WINDOWS), POOL_GROUP, POOL_GROUP), POOL_GROUP),
        "pool_scale": gain(ks[8], (L, POOL_WIDTH)),
        "q_norm": gain(ks[9], (L, HEAD_DIM)),
        "k_norm": gain(ks[10], (L, HEAD_DIM)),
        "w_branch_pool": w(ks[11], (L, POOL_WIDTH, D_MODEL), POOL_WIDTH),
        "w_branch_attn": w(ks[12], (L, ATTN_WIDTH, D_MODEL), ATTN_WIDTH),
        "w_out": w(ks[13], (L, D_MODEL, D_MODEL), D_MODEL),
        "ffn2_norm": gain(ks[14], (L, D_MODEL)),
        "ffn2_w_gate_up": w(ks[15], (L, D_MODEL, 2 * D_FF), D_MODEL),
        "ffn2_w_down": w(ks[16], (L, D_FF, D_MODEL), D_FF),
    }


def reference(x, ffn1_norm, ffn1_w_gate_up, ffn1_w_down, mix_norm, w_in, b_gate,
              pool_w, pool_scale, q_norm, k_norm, w_branch_pool, w_branch_attn, w_out,
              ffn2_norm, ffn2_w_gate_up, ffn2_w_down):
    for l in range(DEPTH):
        x = hybrid_layer(x, ffn1_norm[l], ffn1_w_gate_up[l], ffn1_w_down[l], mix_norm[l],
                         w_in[l], b_gate[l], pool_w[l], pool_scale[l], q_norm[l], k_norm[l],
                         w_branch_pool[l], w_branch_attn[l], w_out[l],
                         ffn2_norm[l], ffn2_w_gate_up[l], ffn2_w_down[l])
    return x
```

```python
import numpy as np
import ml_dtypes
from contextlib import ExitStack
import concourse.bass as bass
import concourse.mybir as mybir
from concourse.bass_utils import run_bass_kernel_spmd

F32 = mybir.dt.float32
BF16 = mybir.dt.bfloat16
AF = mybir.ActivationFunctionType
ALU = mybir.AluOpType
AX = mybir.AxisListType

L = 2
D = 1024
KT = 8
T = 2048
NCH = 4
CW = 512
FF = 2816
NJ = 22
PARTS = [(0, 8), (8, 7), (15, 7)]
EPS = 1e-6
NS = 48
VROW = 1024
NCF = 770
NCB = 448
NEGB = -30000.0
MSTOP = 99
STOP = None

class Tok:
    __slots__ = ("sem", "val", "eng")
    def __init__(self, sem, val, eng):
        self.sem = sem; self.val = val; self.eng = eng


class Buf:
    __slots__ = ("w", "r", "name", "aux")
    def __init__(self, name=""):
        self.w = None; self.r = {}; self.name = name


ENGS = ("pe", "act", "dve", "pool", "sp")


class Prog:
    def __init__(self, esem, rings, ccsem):
        self.esem = esem
        self.rings = rings
        self.ccsem = ccsem
        self.ops = {e: [] for e in ENGS}
        self.cnt = {e: 0 for e in esem}
        self.known = {e: {} for e in ENGS}
        self.dma_tot = {}
        self.sem_of = {}
        for q in rings:
            for s in rings[q]:
                self.dma_tot[id(s)] = 0
                self.sem_of[id(s)] = s
        self.ring_i = {q: 0 for q in rings}
        self.cc_tot = 0

    def _need(self, eng, toks):
        waits = {}
        for t in toks:
            if t is None:
                continue
            if eng == "pe" and t.eng == "pe":
                continue
            k = id(t.sem)
            cur = waits.get(k)
            if cur is None or cur.val < t.val:
                waits[k] = t
        out = []
        kn = self.known[eng]
        for k, t in waits.items():
            if kn.get(k, 0) >= t.val:
                continue
            kn[k] = t.val
            out.append((t.sem, t.val))
        return out

    def _toks(self, reads, writes):
        toks = []
        for b in reads:
            toks.append(b.w)
        for b in writes:
            toks.append(b.w)
            toks.extend(b.r.values())
        return toks

    def _update(self, tok, reads, writes):
        for b in writes:
            b.w = tok; b.r = {}
        for b in reads:
            if b.w is tok:
                continue
            b.r[id(tok.sem)] = tok

    def op(self, eng, fn, reads=(), writes=()):
        waits = self._need(eng, self._toks(reads, writes))
        self.cnt[eng] += 1
        tok = Tok(self.esem[eng], self.cnt[eng], eng)
        self.ops[eng].append((waits, fn, tok.sem, 1))
        self._update(tok, reads, writes)
        return tok

    def dma(self, q, out_ap, in_ap, reads=(), writes=()):
        ring = self.rings[q]
        i = self.ring_i[q]
        self.ring_i[q] = (i + 1) % len(ring)
        sem = ring[i]
        toks = self._toks(reads, writes)
        if self.dma_tot[id(sem)] > 0:
            toks.append(Tok(sem, self.dma_tot[id(sem)], "dma"))
        waits = self._need(q, toks)
        self.dma_tot[id(sem)] += 16
        tok = Tok(sem, self.dma_tot[id(sem)], "dma")
        self.ops[q].append((waits, (lambda e, o=out_ap, s=in_ap: e.dma_start(out=o, in_=s)), sem, 16))
        self._update(tok, reads, writes)
        return tok

    def collective(self, in_ap, out_ap, reads=(), writes=()):
        waits = self._need("pool", self._toks(reads, writes))
        self.cc_tot += 1
        tok = Tok(self.ccsem, self.cc_tot, "cc")

        def fn(e, i=in_ap, o=out_ap):
            return e.collective_compute("AllGather", ALU.bypass,
                                        replica_groups=[[0, 1], [2, 3], [4, 5], [6, 7]],
                                        ins=[i], outs=[o])
        self.ops["pool"].append((waits, fn, self.ccsem, 1))
        self.ops["pool"].append(([(self.ccsem, self.cc_tot)], None, None, 0))
        self.known["pool"][id(self.ccsem)] = self.cc_tot
        self._update(tok, reads, writes)
        return tok

    def barrier(self):
        toks = [Tok(self.esem[e], self.cnt[e], e) for e in self.esem if self.cnt[e] > 0]
        for k, tot in self.dma_tot.items():
            if tot > 0:
                toks.append(Tok(self.sem_of[k], tot, "dma"))
        if self.cc_tot > 0:
            toks.append(Tok(self.ccsem, self.cc_tot, "cc"))
        for e in ENGS:
            waits = {}
            for t in toks:
                if t.eng == e:
                    continue
                waits[id(t.sem)] = t
            out = []
            kn = self.known[e]
            for k, t in waits.items():
                if kn.get(k, 0) >= t.val:
                    continue
                kn[k] = t.val
                out.append((t.sem, t.val))
            if out:
                self.ops[e].append((out, None, None, 0))

    def replay(self, eng, e):
        for waits, fn, sem, amt in self.ops[eng]:
            for (s, v) in waits:
                e.wait_ge(s, v)
            if fn is not None:
                fn(e).then_inc(sem, amt)


class Ring:
    def __init__(self, aps, name, bufs=None):
        self.aps = aps
        self.bufs = bufs if bufs is not None else [Buf(f"{name}{i}") for i in range(len(aps))]
        self.i = 0

    def next(self):
        i = self.i
        self.i = (i + 1) % len(self.aps)
        return self.aps[i], self.bufs[i]


def build_program():
    nc = bass.Bass("TRN2", target_bir_lowering=False)
    dt = nc.dram_tensor
    xT = dt("xT", [D, T], F32, kind="ExternalInput")
    outT = dt("outT", [D, T], F32, kind="ExternalOutput")
    gu_d = dt("gu", [L, 2, NJ, 128, 2048], F32, kind="ExternalInput")
    dn_d = dt("dn", [L, 2, 3, 8, 128, 1024], F32, kind="ExternalInput")
    win_d = dt("win", [L, 32, 128, 1024], F32, kind="ExternalInput")
    wv_d = dt("wv", [L, 2, 128, 2048], F32, kind="ExternalInput")
    wbp_d = dt("wbp", [L, 8, 128, 512], F32, kind="ExternalInput")
    wba_d = dt("wba", [L, 8, 128, 512], F32, kind="ExternalInput")
    wout_d = dt("wout", [L, 8, 128, 1024], F32, kind="ExternalInput")
    pw_d = dt("pw", [L, 128, 512], F32, kind="ExternalInput")
    sm_d = dt("smalls", [128, L * NS], F32, kind="ExternalInput")
    cf_d = dt("cf32", [128, NCF], F32, kind="ExternalInput")
    cb_d = dt("cb16", [128, NCB], BF16, kind="ExternalInput")
    cs_d = dt("cs", [64, 2 * T], F32, kind="ExternalInput")
    E_d = dt("E", [16, 4096], BF16, kind="ExternalInput")
    xk_own_f = dt("xk_own", [4096, 128], F32)
    xk_all_f = dt("xk_all", [8192, 128], F32)
    xv_own_f = [dt(f"xv{i}_own", [4096, 128], F32) for i in range(2)]
    xv_all_f = [dt(f"xv{i}_all", [8192, 128], F32) for i in range(2)]
    xs_own_f = dt("xs_own", [4096, 128], F32)
    xs_all_f = dt("xs_all", [8192, 128], F32)
    xk_own = xk_own_f.bitcast(BF16).reshape([512, T])
    xk_all = xk_all_f.bitcast(BF16).reshape([1024, T])
    xv_own = [t.bitcast(BF16).reshape([1024, VROW]) for t in xv_own_f]
    xv_all = [t.bitcast(BF16).reshape([2048, VROW]) for t in xv_all_f]
    xs_own = xs_own_f
    xs_all = xs_all_f

    with ExitStack() as st:
        def sb(name, shape, dtype):
            return st.enter_context(nc.sbuf_tensor(name, shape, dtype))

        x_sb = sb("x_sb", [128, KT * T], F32)
        h_sb = sb("h_sb", [128, KT * T], BF16)
        R1 = sb("R1", [128, 8 * T], BF16)
        R2 = sb("R2", [128, 8256], F32)
        ktb = [sb(f"ktb{i}", [80, 1024], BF16) for i in range(2)]
        vb = [sb(f"vb{i}", [128, 8, 128], BF16) for i in range(2)]
        stg = [sb(f"stg{i}", [128, 2048], BF16) for i in range(3)]
        tf_t = [sb(f"tf{i}", [128, 528], F32) for i in range(4)]
        pt_all = sb("pt_all", [128, 6 * CW], BF16)
        pt_t = [pt_all[:, i * CW:(i + 1) * CW] for i in range(6)]
        qa_t = [sb(f"qa{i}", [80, CW], BF16) for i in range(2)]
        vst_t = [sb(f"vst{i}", [128, 386], BF16) for i in range(2)]
        kst_t = [sb(f"kst{i}", [64, CW], BF16) for i in range(2)]
        biasm = sb("biasm", [128, 4, 80], BF16)
        gs_t = sb("gs_t", [128, 64], F32)
        sel_t = sb("sel_t", [128, 64], F32)
        top8 = sb("top8", [128, 4, 8], F32)
        xs_sb = sb("xs_sb", [128, 128], F32)
        xsp_sb = sb("xsp_sb", [128, 128], F32)
        kb_all = sb("kb_all", [64, 8, 16], F32)
        sm_sb = sb("sm_sb", [128, L * NS], F32)
        cf_sb = sb("cf_sb", [128, NCF], F32)
        cb_sb = sb("cb_sb", [128, NCB], BF16)
        pp = [st.enter_context(nc.psum_tensor(f"pp{i}", [128, 2 * CW], F32)) for i in range(4)]
        psb = [pp[i // 2][:, (i % 2) * CW:(i % 2 + 1) * CW] for i in range(8)]

        sem = lambda n: st.enter_context(nc.semaphore(n))
        esem = {"pe": sem("s_pe"), "act": sem("s_act"), "dve": sem("s_dve")}
        rings = {"sp": [sem(f"s_sp{i}") for i in range(8)],
                 "pool": [sem(f"s_pl{i}") for i in range(8)]}
        ccsem = sem("s_cc")
        P = Prog(esem, rings, ccsem)

        def xv_(kt, c): return x_sb[:, kt * T + c * CW: kt * T + (c + 1) * CW]
        def hv_(kt, c): return h_sb[:, kt * T + c * CW: kt * T + (c + 1) * CW]
        xB = [[Buf() for _ in range(NCH)] for _ in range(KT)]
        hB = [[Buf() for _ in range(NCH)] for _ in range(KT)]
        def hff_(jj, c): return R1[:, jj * T + c * CW: jj * T + (c + 1) * CW]
        hffB = [[Buf() for _ in range(NCH)] for _ in range(8)]
        def yp_(g, c): return R1[:, g * T + c * CW: g * T + (c + 1) * CW]
        ypB = [[Buf() for _ in range(NCH)] for _ in range(4)]
        def ya_(p, c, r0, r1): return R1[r0:r1, (4 + p) * T + c * CW: (4 + p) * T + (c + 1) * CW]
        yaB = [[Buf() for _ in range(NCH)] for _ in range(4)]
        R2b = R2[:, 0:8192].bitcast(BF16)
        def mg_(m, c): return R2b[:, m * T + c * CW: m * T + (c + 1) * CW]
        mgB = [[Buf() for _ in range(NCH)] for _ in range(8)]
        cosB = Buf("cos"); sinB = Buf("sin")
        def cos_(c): return R2[0:64, c * CW:(c + 1) * CW]
        def sin_(c): return R2[0:64, T + c * CW: T + (c + 1) * CW]
        UB0 = 4096
        def ub_(i, a, b): return R2[:, UB0 + i * 2080 + a: UB0 + i * 2080 + b]
        ubB = [Buf("ub0"), Buf("ub1")]

        stgR = Ring([s[:] for s in stg], "stg")
        tfR = Ring([s[:] for s in tf_t], "tf")
        ptR = Ring(pt_t, "pt")
        sqR = ptR
        qaR = Ring(qa_t, "qa")
        vstR = Ring(vst_t, "vst")
        kstR = Ring(kst_t, "kst")
        ktR = Ring(ktb, "ktb")
        ktEB = [Buf("ktE0"), Buf("ktE1")]
        vbR = Ring(vb, "vb")
        psBufs = [Buf(f"psum{i}") for i in range(8)]
        psS = Ring(psb[0:3], "psS", psBufs[0:3])
        psO = Ring(psb[3:5], "psO", psBufs[3:5])
        psG = Ring(psb[5:8], "psG", psBufs[5:8])
        psM = Ring(psb[6:8], "psM", psBufs[6:8])
        psO = Ring(psb[4:6], "psO", psBufs[4:6])
        psA = Ring(psb[0:8], "psA", psBufs[0:8])
        rstdB = tfR.bufs[0]; pwB = Buf("pw")
        rstd_ap = tf_t[0][:, 0:CW]
        smB = Buf("sm"); cfB = Buf("cf"); cbB = Buf("cb")
        biasB = Buf("biasm"); gsB = Buf("gs"); selB = Buf("sel"); top8B = Buf("top8")
        xsB = Buf("xs"); xspB = Buf("xsp"); kbB = Buf("kb_all")
        xkB = [[Buf() for _ in range(NCH)] for _ in range(8)]
        xvB = [[Buf() for _ in range(2)] for _ in range(16)]
        xsoB = Buf("xs_own")
        xkaB = Buf("xk_all"); xvaB = [Buf("xv0_all"), Buf("xv1_all")]; xsaB = Buf("xs_all")

        ident = cb_sb[:, 0:128]
        tri = cb_sb[:, 128:256]
        onesm = cb_sb[:, 256:384]
        ones64 = cb_sb[0:64, 384:448]
        rotT = cf_sb[0:64, 0:64]
        def GM_(c): return cf_sb[:, 64 + c * 64: 64 + (c + 1) * 64]
        def OWN_(c): return cf_sb[:, 320 + c * 64: 320 + (c + 1) * 64]
        def corr_(g): return cf_sb[:, 576 + g * 16: 576 + (g + 1) * 16]
        hasprev = cf_sb[:, 640:641]
        onesf = cf_sb[:, 641:769]
        eps_ap = cf_sb[:, 769:770]
        def smc(l, col, rows=128): return sm_sb[0:rows, l * NS + col: l * NS + col + 1]

        mm_count = [0]

        def mm(out, lhsT, rhs, start, stop, reads, wbuf):
            P.op("pe", lambda e, o=out, a=lhsT, b=rhs, s=start, t=stop:
                 e.matmul(o, a, b, start=s, stop=t), reads=reads, writes=[wbuf])

        def load_w(src_ap, ncols):
            ap, b = stgR.next()
            P.dma("pool", ap[:, 0:ncols], src_ap, reads=[], writes=[b])
            return ap, b

        P.dma("sp", sm_sb[:], sm_d[:, :], writes=[smB])
        P.dma("sp", cf_sb[:], cf_d[:, :], writes=[cfB])
        P.dma("sp", cb_sb[:], cb_d[:, :], writes=[cbB])
        for kt in range(KT):
            P.dma("sp", x_sb[:, kt * T:(kt + 1) * T], xT[kt * 128:(kt + 1) * 128, :], writes=xB[kt])
        P.op("dve", lambda e: e.memset(biasm[:], 0.0), writes=[biasB])
        for i in range(2):
            a, b = vstR.next()
            P.op("dve", lambda e, a=a: e.memset(a[:], 0.0), writes=[b])
            for pl in range(2):
                P.op("dve", lambda e, a=a, pl=pl: e.memset(a[:, pl * 193 + 64: pl * 193 + 66], 1.0), writes=[b])
        P.op("dve", lambda e: e.memset(xs_sb[:], 0.0), writes=[xsB])

        def emit_norm(l, gcol):
            for c in range(NCH):
                pa, pb_ = psG.next()
                for kt in range(KT):
                    sq, sqb = sqR.next()
                    P.op("act", lambda e, o=sq, i=xv_(kt, c): e.activation(out=o, in_=i, func=AF.Square),
                         reads=[xB[kt][c]], writes=[sqb])
                    mm(pa[:], onesm, sq, kt == 0, kt == KT - 1, [sqb, cbB], pb_)
                P.op("act", lambda e, i=pa: e.activation(out=rstd_ap, in_=i[:], func=AF.Ln, bias=eps_ap),
                     reads=[pb_, cfB], writes=[rstdB])
                P.op("act", lambda e: e.activation(out=rstd_ap, in_=rstd_ap, func=AF.Exp, scale=-0.5),
                     reads=[], writes=[rstdB])
                for kt in range(KT):
                    P.op("dve", lambda e, o=hv_(kt, c), i=xv_(kt, c), g=smc(l, gcol + kt):
                         e.scalar_tensor_tensor(o, i, g, rstd_ap, ALU.mult, ALU.mult),
                         reads=[xB[kt][c], rstdB, smB], writes=[hB[kt][c]])

        def emit_ffn(l, which):
            emit_norm(l, 0 if which == 0 else 16)
            for part, (j0, nk) in enumerate(PARTS):
                for jj in range(nk):
                    j = j0 + jj
                    w, wb_ = load_w(gu_d[l, which, j], 2048)
                    for c in range(NCH):
                        pa, pab = psG.next()
                        for kt in range(KT):
                            mm(pa[:], w[:, kt * 256: kt * 256 + 128], hv_(kt, c), kt == 0, kt == KT - 1,
                               [wb_, hB[kt][c]], pab)
                        pb2, pbb = psG.next()
                        for kt in range(KT):
                            mm(pb2[:], w[:, kt * 256 + 128: kt * 256 + 256], hv_(kt, c), kt == 0, kt == KT - 1,
                               [wb_, hB[kt][c]], pbb)
                        tf, tfb = tfR.next()
                        P.op("act", lambda e, o=tf, i=pa: e.activation(out=o[:, 0:CW], in_=i[:], func=AF.Silu),
                             reads=[pab], writes=[tfb])
                        P.op("dve", lambda e, o=hff_(jj, c), a=tf, b=pb2:
                             e.tensor_tensor(o, a[:, 0:CW], b[:], ALU.mult),
                             reads=[tfb, pbb], writes=[hffB[jj][c]])
                for m in range(8):
                    w, wb_ = load_w(dn_d[l, which, part, m][:, 0:nk * 128], nk * 128)
                    for c in range(NCH):
                        po, pob = psG.next()
                        for kk in range(nk):
                            mm(po[:], w[:, kk * 128:(kk + 1) * 128], hff_(kk, c), kk == 0, kk == nk - 1,
                               [wb_, hffB[kk][c]], pob)
                        P.op("dve", lambda e, o=xv_(m, c), i=po:
                             e.scalar_tensor_tensor(o, i[:], 0.5, o, ALU.mult, ALU.add),
                             reads=[pob], writes=[xB[m][c]])

        def normrope_g(l, pin, pinb, gcol, c, sqring, out, psr=None):
            psr = psr or psG
            sq, sqb = sqring.next()
            P.op("act", lambda e, o=sq, i=pin: e.activation(out=o[0:64, :], in_=i[0:64, :], func=AF.Square),
                 reads=[pinb], writes=[sqb])
            yield
            ps_, psb_ = psr.next()
            mm(ps_[0:64, :], ones64, sq[0:64, :], True, True, [sqb, cbB], psb_)
            yield
            t2, t2b = tfR.next()
            P.op("act", lambda e, o=t2, i=ps_: e.activation(out=o[0:64, 0:CW], in_=i[0:64, :], func=AF.Ln, bias=eps_ap[0:64, :]),
                 reads=[psb_, cfB], writes=[t2b])
            P.op("act", lambda e, o=t2: e.activation(out=o[0:64, 0:CW], in_=o[0:64, 0:CW], func=AF.Exp, scale=-0.5),
                 reads=[], writes=[t2b])
            yield
            t3, t3b = tfR.next()
            P.op("dve", lambda e, o=t3, i=pin, r=t2, g=smc(l, gcol, 64):
                 e.scalar_tensor_tensor(o[0:64, 0:CW], i[0:64, :], g, r[0:64, 0:CW], ALU.mult, ALU.mult),
                 reads=[pinb, t2b, smB], writes=[t3b])
            yield
            pr, prb = psr.next()
            mm(pr[0:64, :], rotT, t3[0:64, 0:CW], True, True, [t3b, cfB], prb)
            yield
            P.op("dve", lambda e, o=t2, i=pr, s=sin_(c): e.tensor_tensor(o[0:64, 0:CW], i[0:64, :], s, ALU.mult),
                 reads=[prb, sinB], writes=[t2b])
            P.op("dve", lambda e, o=t3, s=cos_(c): e.tensor_tensor(o[0:64, 0:CW], o[0:64, 0:CW], s, ALU.mult),
                 reads=[cosB], writes=[t3b])
            yield
            P.op("dve", lambda e, o=t3, a=t2: e.tensor_tensor(o[0:64, 0:CW], o[0:64, 0:CW], a[0:64, 0:CW], ALU.add),
                 reads=[t2b], writes=[t3b])
            out["t"] = t3; out["b"] = t3b

        def normrope(l, pin, pinb, gcol, c):
            out = {}
            for _ in normrope_g(l, pin, pinb, gcol, c, sqR, out):
                pass
            return out["t"], out["b"]

        def emit_mixer(l):
            emit_norm(l, 8)
            P.barrier()
            P.dma("sp", R2[0:64, 0:T], cs_d[:, 0:T], writes=[cosB])
            P.dma("sp", R2[0:64, T:2 * T], cs_d[:, T:2 * T], writes=[sinB])
            def kchain_g(head, c, w, wb_):
                hh = head % 2
                pk, pkb = psA.next()
                for kt in range(KT):
                    mm(pk[0:64, :], w[:, kt * 128 + hh * 64: kt * 128 + hh * 64 + 64], hv_(kt, c),
                       kt == 0, kt == KT - 1, [wb_, hB[kt][c]], pkb)
                yield
                o2 = {}
                yield from normrope_g(l, pk, pkb, 45, c, sqR, o2, psA)
                kf, kfb = o2["t"], o2["b"]
                yield
                ks, ksb = kstR.next()
                P.op("act", lambda e, o=ks, i=kf: e.activation(out=o[:], in_=i[0:64, 0:CW], func=AF.Copy),
                     reads=[kfb], writes=[ksb])
                P.dma("sp", xk_own[head * 64:(head + 1) * 64, c * CW:(c + 1) * CW], ks[:],
                      reads=[ksb], writes=[xkB[head][c]])
                P.op("dve", lambda e, i=kf, o=xs_sb[0:64, head * 8 + 2 * c: head * 8 + 2 * c + 2]:
                     e.reduce_sum(o, i[0:64, 0:CW].rearrange("p (a b) -> p a b", a=2), AX.X),
                     reads=[kfb], writes=[xsB])

            for tp in range(4):
                w, wb_ = load_w(win_d[l, 8 + tp], 1024)
                todo = [kchain_g(2 * tp + hh, c, w, wb_) for hh in range(2) for c in range(NCH)]
                active = []
                while todo or active:
                    while todo and len(active) < 2:
                        active.append(todo.pop(0))
                    for g_ in list(active):
                        try:
                            next(g_)
                        except StopIteration:
                            active.remove(g_)
            if MSTOP <= 1:
                P.barrier(); return
            for vh in range(2):
                w, wb_ = load_w(wv_d[l, vh], 2048)
                for tt in range(16):
                    c, o4 = tt // 4, (tt % 4) * 128
                    pv, pvb = psG.next()
                    for kt in range(KT):
                        mm(pv[:, 0:256], hv_(kt, c)[:, o4:o4 + 128], w[:, kt * 256:(kt + 1) * 256],
                           kt == 0, kt == KT - 1, [wb_, hB[kt][c]], pvb)
                    vs, vsb = vstR.next()
                    for pl in range(2):
                        P.op("act", lambda e, o=vs, i=pv, pl=pl:
                             e.activation(out=o[:, pl * 193: pl * 193 + 64], in_=i[:, pl * 128: pl * 128 + 64], func=AF.Copy),
                             reads=[pvb], writes=[vsb])
                        P.op("act", lambda e, o=vs, i=pv, pl=pl:
                             e.activation(out=o[:, pl * 193 + 129: pl * 193 + 193], in_=i[:, pl * 128 + 64: pl * 128 + 128], func=AF.Copy),
                             reads=[pvb], writes=[vsb])
                    P.dma("sp", xv_own[tt // 8][(tt % 8) * 128:(tt % 8 + 1) * 128, vh * 386:(vh + 1) * 386], vs[:],
                          reads=[vsb], writes=[xvB[tt][vh]])
            if MSTOP <= 2:
                P.barrier(); return
            for g in range(4):
                w, wb_ = load_w(win_d[l, g], 1024)
                pu, pub = psG.next()
                for kt in range(KT):
                    mm(pu[:, 0:16], w[:, kt * 128:(kt + 1) * 128], hv_(kt, 3)[:, CW - 16:CW], kt == 0, kt == KT - 1,
                       [wb_, hB[kt][3]], pub)
                P.op("act", lambda e, i=pu, o=xs_sb[:, 64 + g * 16: 64 + (g + 1) * 16]:
                     e.activation(out=o, in_=i[:, 0:16], func=AF.Copy), reads=[pub], writes=[xsB])
            P.op("dve", lambda e: e.tensor_scalar(xs_sb[0:64, 0:64], xs_sb[0:64, 0:64], 1.0 / 256.0, None, ALU.mult),
                 reads=[], writes=[xsB])
            P.dma("sp", xs_own[0:128, :], xs_sb[:], reads=[xsB], writes=[xsoB])
            if MSTOP <= 3:
                P.barrier(); return
            P.collective(xk_own_f.ap().opt(), xk_all_f.ap().opt(),
                         reads=[b for r in xkB for b in r], writes=[xkaB])
            for i in range(2):
                P.collective(xv_own_f[i].ap().opt(), xv_all_f[i].ap().opt(),
                             reads=[b for r in xvB[8 * i:8 * i + 8] for b in r], writes=[xvaB[i]])
            P.collective(xs_own_f.ap().opt(), xs_all_f.ap().opt(), reads=[xsoB], writes=[xsaB])
            P.dma("sp", xsp_sb[:], xs_all[0:128, :], reads=[xsaB], writes=[xspB])
            P.op("dve", lambda e: e.tensor_copy(kb_all[:, :, 0:8], xsp_sb[0:64, 0:64].rearrange("p (a b) -> p a b", a=8)),
                 reads=[xspB], writes=[kbB])
            P.op("dve", lambda e: e.tensor_copy(kb_all[:, :, 8:16], xs_sb[0:64, 0:64].rearrange("p (a b) -> p a b", a=8)),
                 reads=[xsB], writes=[kbB])
            if MSTOP <= 4:
                P.barrier(); return
            pwb = vbR.bufs[0]
            P.dma("pool", vb[0][:, 0:4, :], pw_d[l].rearrange("p (a b) -> p a b", a=4), reads=[], writes=[pwb])
            for g in range(4):
                wwin = 2 ** (g + 1)
                w, wb_ = load_w(win_d[l, g], 1024)
                ubuf = ubB[g % 2]
                ui = g % 2
                P.op("dve", lambda e, o=ub_(ui, 0, 16), i=xsp_sb[:, 64 + g * 16: 64 + (g + 1) * 16]:
                     e.tensor_scalar(o, i, hasprev, None, ALU.mult), reads=[xspB, cfB], writes=[ubuf])
                for c in range(NCH):
                    pu, pub = psG.next()
                    for kt in range(KT):
                        mm(pu[:], w[:, kt * 128:(kt + 1) * 128], hv_(kt, c), kt == 0, kt == KT - 1,
                           [wb_, hB[kt][c]], pub)
                    P.op("act", lambda e, i=pu, o=ub_(ui, 16 + c * CW, 16 + (c + 1) * CW):
                         e.activation(out=o, in_=i[:], func=AF.Copy), reads=[pub], writes=[ubuf])
                for c in range(NCH):
                    b0 = c * CW
                    cur = (lambda ui_, b0_: (lambda a, b_: ub_(ui_, b0_ + a, b0_ + b_)))(ui, b0)
                    curb = ubuf
                    sh = 1
                    while sh < wwin:
                        nt, ntb = tfR.next()
                        P.op("dve", lambda e, o=nt, hi=cur(sh, 528), lo=cur(0, 528 - sh), sh=sh:
                             e.tensor_tensor(o[:, sh:528], hi, lo, ALU.add), reads=[curb], writes=[ntb])
                        cur = (lambda nt_: (lambda a, b_: nt_[:, a:b_]))(nt)
                        curb = ntb
                        sh *= 2
                    if c == 0:
                        P.op("dve", lambda e, o=cur(16, 32), cr=corr_(g): e.tensor_tensor(o, o, cr, ALU.mult),
                             reads=[cfB], writes=[curb])
                    dtile, db = ptR.next()
                    P.op("dve", lambda e, o=dtile, s=cur(16, 528), u=ub_(ui, 16 + c * CW, 16 + (c + 1) * CW), sc=1.0 / wwin:
                         e.scalar_tensor_tensor(o, s, sc, u, ALU.mult, ALU.subtract),
                         reads=[curb, ubuf], writes=[db])
                    py, pyb = psG.next()
                    mm(py[:], vb[0][:, g, :], dtile, True, True, [pwb, db], pyb)
                    P.op("act", lambda e, i=py, o=yp_(g, c), s=smc(l, 40 + g):
                         e.activation(out=o, in_=i[:], func=AF.Copy, scale=s), reads=[pyb, smB], writes=[ypB[g][c]])
            if MSTOP <= 5:
                P.barrier(); return
            wqs = {}

            psS2_i = [0]
            ptP_i = [0]
            def prologue_g(c, head, out):
                pair, hh = head // 2, head % 2
                if hh == 0:
                    wqs["w"] = load_w(win_d[l, 4 + pair], 1024)
                wq, wqb = wqs["w"]
                pq, pqb = psM.next()
                for kt in range(KT):
                    mm(pq[0:64, :], wq[:, kt * 128 + hh * 64: kt * 128 + hh * 64 + 64], hv_(kt, c),
                       kt == 0, kt == KT - 1, [wqb, hB[kt][c]], pqb)
                yield
                o2 = {}
                yield from normrope_g(l, pq, pqb, 44, c, kstR, o2, psM)
                qf, qfb = o2["t"], o2["b"]
                yield
                qa, qab = qaR.next()
                P.op("dve", lambda e, o=qa, i=qf: e.tensor_copy(o[0:64, :], i[0:64, 0:CW]),
                     reads=[qfb], writes=[qab])
                pg, pgb = psM.next()
                for qt in range(4):
                    mm(pg[:, qt * 16:(qt + 1) * 16], qf[0:64, qt * 128:(qt + 1) * 128], kb_all[:, head, :],
                       True, True, [qfb, kbB], pgb)
                yield
                P.op("dve", lambda e, i=pg, g=GM_(c): e.tensor_tensor(gs_t[:], i[:, 0:64], g, ALU.add),
                     reads=[pgb, cfB], writes=[gsB])
                for qt in range(4):
                    P.op("dve", lambda e, qt=qt: e.max(top8[:, qt, :], gs_t[:, qt * 16:(qt + 1) * 16]),
                         reads=[gsB], writes=[top8B])
                yield
                for qt in range(4):
                    P.op("dve", lambda e, qt=qt: e.tensor_scalar(sel_t[:, qt * 16:(qt + 1) * 16],
                                                                gs_t[:, qt * 16:(qt + 1) * 16],
                                                                top8[:, qt, 2:3], None, ALU.is_ge),
                         reads=[gsB, top8B], writes=[selB])
                P.op("dve", lambda e: e.scalar_tensor_tensor(sel_t[:], gs_t[:], -1e29, sel_t[:], ALU.is_gt, ALU.mult),
                     reads=[gsB], writes=[selB])
                P.op("dve", lambda e, o=OWN_(c): e.tensor_tensor(sel_t[:], sel_t[:], o, ALU.max),
                     reads=[cfB], writes=[selB])
                P.op("dve", lambda e: e.tensor_scalar(biasm[:, :, 64:80], sel_t[:].rearrange("p (a b) -> p a b", a=4),
                                                      -1.0, -NEGB, ALU.add, ALU.mult),
                     reads=[selB], writes=[biasB])
                yield
                pb_, pbb = psM.next()
                for qt in range(4):
                    mm(pb_[0:80, qt * 128:(qt + 1) * 128], biasm[:, qt, :], ident, True, True, [biasB, cbB], pbb)
                yield
                P.op("dve", lambda e, o=qa, i=pb_: e.tensor_copy(o[64:80, :], i[64:80, :]),
                     reads=[pbb], writes=[qab])
                out["qa"] = qa; out["qab"] = qab

            def attend(c, head, qa, qab, nxt):
                    pair, hh = head // 2, head % 2
                    pieces = []
                    for pc in range(2):
                        pieces.append((xk_all, xv_all[pc], xkaB, xvaB[pc], pc * 1024, pc * 1024, 8, None))
                    n_own = 4 * (c + 1)
                    for op_ in range(2):
                        nt_ = min(8, n_own - op_ * 8)
                        if nt_ > 0:
                            pieces.append((xk_own, xv_own[op_], None, None, op_ * 1024, T + op_ * 1024, nt_, op_ * 8))
                    po, pob = psO.next()
                    VW = 65 if hh == 0 else 128
                    vc0 = pair * 193 + (0 if hh == 0 else 65)
                    total_tiles = sum(p[6] for p in pieces)
                    tiles = []
                    first = {}
                    for pi, pcs in enumerate(pieces):
                        own0, ntl = pcs[7], pcs[6]
                        for i in range(ntl):
                            q0, diag = 0, False
                            if own0 is not None and own0 + i >= 4 * c:
                                q0, diag = (own0 + i - 4 * c) * 128, True
                            tiles.append((pi, i, q0, diag))
                    kslot, vslot = {}, {}

                    def load_k(pi, head=head):
                        sk, sv, skb, svb, k0, s0, ntl, own0 = pieces[pi]
                        nk = ntl * 128
                        ktEbuf = ktEB[ktR.i]
                        kt_t, ktbuf = ktR.next()
                        kreads = [xkB[head][cc] for cc in range(NCH)] if skb is None else [skb]
                        P.dma("sp", kt_t[0:64, 0:nk], sk[head * 64:(head + 1) * 64, k0:k0 + nk], reads=kreads, writes=[ktbuf])
                        P.dma("sp", kt_t[64:80, 0:nk], E_d[:, s0:s0 + nk], reads=[], writes=[ktEbuf])
                        kslot[pi] = (kt_t, ktbuf, ktEbuf)

                    def load_v(pi, pair=pair, VW=VW, vc0=vc0):
                        sk, sv, skb, svb, k0, s0, ntl, own0 = pieces[pi]
                        nk = ntl * 128
                        vb_t, vbbuf = vbR.next()
                        vreads = [xvB[tt][pair // 2] for tt in range(k0 // 128, k0 // 128 + ntl)] if svb is None else [svb]
                        P.dma("sp", vb_t[:, 0:ntl, 0:VW],
                              sv[0:nk, vc0:vc0 + VW].rearrange("(kt p) c -> p kt c", p=128),
                              reads=vreads, writes=[vbbuf])
                        vslot[pi] = (vb_t, vbbuf)

                    for pi in range(min(2, len(pieces))):
                        load_k(pi); load_v(pi)
                    groups, cur_g = [], []
                    for t_i, tl in enumerate(tiles):
                        if tl[3]:
                            if cur_g:
                                groups.append(cur_g); cur_g = []
                            groups.append([t_i])
                        else:
                            cur_g.append(t_i)
                            if len(cur_g) == 2:
                                groups.append(cur_g); cur_g = []
                    if cur_g:
                        groups.append(cur_g)
                    LA = 2
                    stash = {}
                    n_t = len(tiles)
                    n_g = len(groups)
                    for g_i in range(n_g + LA):
                        if g_i < n_g:
                            grp = groups[g_i]
                            sl = psS2_i[0]
                            psS2_i[0] = (sl + 1) % 2
                            ps2 = pp[sl]
                            ps2b = [psBufs[2 * sl], psBufs[2 * sl + 1]]
                            pl = ptP_i[0]
                            ptP_i[0] = (pl + 1) % 3
                            pt2 = pt_all[:, pl * 2 * CW:(pl + 1) * 2 * CW]
                            pt2b = [ptR.bufs[2 * pl], ptR.bufs[2 * pl + 1]]
                            for k_, t in enumerate(grp):
                                if nxt is not None and g_i >= 1:
                                    next(nxt, None)
                                pi, i, q0, diag = tiles[t]
                                if i == 0 and pi >= 1 and pi + 1 < len(pieces):
                                    load_k(pi + 1)
                                kt_t, ktbuf, ktEbuf = kslot[pi]
                                mm(ps2[:, k_ * CW + q0:(k_ + 1) * CW], kt_t[0:80, i * 128:(i + 1) * 128], qa[0:80, q0:CW], True, True,
                                   [ktbuf, ktEbuf, qab], ps2b[k_])
                            q0g = tiles[grp[0]][2]
                            hi_ = len(grp) * CW
                            P.op("act", lambda e, o=pt2, i_=ps2, q0=q0g, hi_=hi_:
                                 e.activation(out=o[:, q0:hi_], in_=i_[:, q0:hi_], func=AF.Exp, scale=0.125),
                                 reads=ps2b[0:len(grp)], writes=pt2b[0:len(grp)])
                            if tiles[grp[0]][3]:
                                P.op("dve", lambda e, o=pt2, q0=q0g:
                                     e.tensor_tensor(o[:, q0:q0 + 128], o[:, q0:q0 + 128], tri, ALU.mult),
                                     reads=[cbB], writes=[pt2b[0]])
                            stash[g_i] = (pt2, pt2b)
                        gp = g_i - LA
                        if gp >= 0:
                            pt2, pt2b = stash.pop(gp)
                            for k_, tp in enumerate(groups[gp]):
                                pi, i, q0, diag = tiles[tp]
                                if i == 0 and pi >= 1 and pi + 1 < len(pieces):
                                    load_v(pi + 1)
                                vb_t, vbbuf = vslot[pi]
                                P.op("pe", lambda e, o=po, v=vb_t, i=i, VW=VW, p_=pt2, q0=q0, k_=k_, s=(tp == 0), t_=(tp == n_t - 1):
                                     e.matmul(o[0:VW, q0:CW], v[:, i, 0:VW], p_[:, k_ * CW + q0:(k_ + 1) * CW], start=s, stop=t_, skip_group_check=True),
                                     reads=[vbbuf, pt2b[k_]], writes=[pob])
                    dr = 64 if hh == 0 else 0
                    r0, r1 = (0, 64) if hh == 0 else (64, 128)
                    if nxt is not None:
                        for _ in nxt:
                            pass
                    rd, rdb = tfR.next()
                    P.op("act", lambda e, o=rd, i=po, dr=dr: e.activation(out=o[dr:dr + 1, 0:CW], in_=i[dr:dr + 1, :], func=AF.Ln),
                         reads=[pob], writes=[rdb])
                    P.op("act", lambda e, o=rd, dr=dr: e.activation(out=o[dr:dr + 1, 0:CW], in_=o[dr:dr + 1, 0:CW], func=AF.Exp, scale=-1.0),
                         reads=[], writes=[rdb])
                    pbc, pbcb = psM.next()
                    mm(pbc[0:r1, :], onesf[dr:dr + 1, 0:r1], rd[dr:dr + 1, 0:CW], True, True, [rdb, cfB], pbcb)
                    on, onb = tfR.next()
                    P.op("dve", lambda e, o=on, i=po, r0=r0, r1=r1: e.tensor_copy(o[r0:r1, 0:CW], i[r0:r1, :]),
                         reads=[pob], writes=[onb])
                    P.op("dve", lambda e, o=ya_(pair, c, r0, r1), a=on, b=pbc, r0=r0, r1=r1:
                         e.tensor_tensor(o, a[r0:r1, 0:CW], b[r0:r1, :], ALU.mult),
                         reads=[onb, pbcb], writes=[yaB[pair][c]])

            order = [(c, head) for c in range(NCH) for head in range(8)]
            st_cur = {}
            for _ in prologue_g(order[0][0], order[0][1], st_cur):
                pass
            for idx, (c, head) in enumerate(order):
                st_nxt = {}
                nxt = prologue_g(order[idx + 1][0], order[idx + 1][1], st_nxt) if idx + 1 < len(order) else None
                attend(c, head, st_cur["qa"], st_cur["qab"], nxt)
                st_cur = st_nxt
            if MSTOP <= 6:
                P.barrier(); return
            P.barrier()
            for m in range(8):
                w0, w0b = load_w(win_d[l, 16 + m], 1024)
                w1, w1b = load_w(win_d[l, 24 + m], 1024)
                wpa, wpab = stgR.next()
                P.dma("pool", wpa[:, 0:512], wbp_d[l, m], reads=[], writes=[wpab])
                P.dma("pool", wpa[:, 512:1024], wba_d[l, m], reads=[], writes=[wpab])
                for c in range(NCH):
                    g0, g0b = psG.next()
                    for kt in range(KT):
                        mm(g0[:], w0[:, kt * 128:(kt + 1) * 128], hv_(kt, c), kt == 0, kt == KT - 1, [w0b, hB[kt][c]], g0b)
                    ta, tab = tfR.next()
                    P.op("act", lambda e, o=ta, i=g0, b=smc(l, 24 + m): e.activation(out=o[:, 0:CW], in_=i[:], func=AF.Sigmoid, bias=b),
                         reads=[g0b, smB], writes=[tab])
                    g1, g1b = psG.next()
                    for kt in range(KT):
                        mm(g1[:], w1[:, kt * 128:(kt + 1) * 128], hv_(kt, c), kt == 0, kt == KT - 1, [w1b, hB[kt][c]], g1b)
                    tb, tbb = tfR.next()
                    P.op("act", lambda e, o=tb, i=g1, b=smc(l, 32 + m): e.activation(out=o[:, 0:CW], in_=i[:], func=AF.Sigmoid, bias=b),
                         reads=[g1b, smB], writes=[tbb])
                    bp, bpb = psG.next()
                    for kk in range(4):
                        mm(bp[:], wpa[:, kk * 128:(kk + 1) * 128], yp_(kk, c), kk == 0, kk == 3, [wpab, ypB[kk][c]], bpb)
                    P.op("dve", lambda e, a=ta, i=bp: e.tensor_tensor(a[:, 0:CW], a[:, 0:CW], i[:], ALU.mult),
                         reads=[bpb], writes=[tab])
                    ba, bab = psG.next()
                    for kk in range(4):
                        mm(ba[:], wpa[:, 512 + kk * 128: 512 + (kk + 1) * 128], ya_(kk, c, 0, 128), kk == 0, kk == 3, [wpab, yaB[kk][c]], bab)
                    P.op("dve", lambda e, a=tb, i=ba: e.tensor_tensor(a[:, 0:CW], a[:, 0:CW], i[:], ALU.mult),
                         reads=[bab], writes=[tbb])
                    P.op("dve", lambda e, o=mg_(m, c), a=ta, b=tb: e.tensor_tensor(o, a[:, 0:CW], b[:, 0:CW], ALU.add),
                         reads=[tab, tbb], writes=[mgB[m][c]])
            for m2 in range(8):
                w, wb_ = load_w(wout_d[l, m2], 1024)
                for c in range(NCH):
                    po, pob = psG.next()
                    for m in range(8):
                        mm(po[:], w[:, m * 128:(m + 1) * 128], mg_(m, c), m == 0, m == 7, [wb_, mgB[m][c]], pob)
                    P.op("dve", lambda e, o=xv_(m2, c), i=po: e.tensor_tensor(o, i[:], o, ALU.add),
                         reads=[pob], writes=[xB[m2][c]])
            P.barrier()

        ph = 0
        for l in range(L):
            for f in (lambda: emit_ffn(l, 0), lambda: emit_mixer(l), lambda: emit_ffn(l, 1)):
                if STOP is None or ph < STOP:
                    f()
                ph += 1
            P.barrier()
        for kt in range(KT):
            P.dma("sp", outT[kt * 128:(kt + 1) * 128, :], x_sb[:, kt * T:(kt + 1) * T], reads=xB[kt], writes=[])
        P.barrier()

        with nc.Block() as block:
            @block.tensor
            def _(e):
                P.replay("pe", e)

            @block.scalar
            def _(e):
                P.replay("act", e)

            @block.vector
            def _(e):
                P.replay("dve", e)

            @block.gpsimd
            def _(e):
                P.replay("pool", e)

            @block.sync
            def _(e):
                P.replay("sp", e)
    return nc


def _prep_shared(inp):
    f = np.float32
    gu = np.empty((L, 2, NJ, 128, 2048), f)
    dn = np.zeros((L, 2, 3, 8, 128, 1024), f)
    for wi, (ngu, ndn) in enumerate((("ffn1_w_gate_up", "ffn1_w_down"), ("ffn2_w_gate_up", "ffn2_w_down"))):
        W = np.asarray(inp[ngu], f)
        A = W[:, :, :FF].reshape(L, KT, 128, NJ, 128)
        B = W[:, :, FF:].reshape(L, KT, 128, NJ, 128)
        t = np.stack([A, B], axis=4)
        gu[:, wi] = t.transpose(0, 3, 2, 1, 4, 5).reshape(L, NJ, 128, 2048)
        Wd = np.asarray(inp[ndn], f).reshape(L, NJ, 128, 8, 128)
        for part, (j0, nk) in enumerate(PARTS):
            t = Wd[:, j0:j0 + nk].transpose(0, 3, 2, 1, 4)
            dn[:, wi, part, :, :, :nk * 128] = t.reshape(L, 8, 128, nk * 128)
    Win = np.asarray(inp["w_in"], f)
    win = Win.reshape(L, KT, 128, 32, 128).transpose(0, 3, 2, 1, 4).reshape(L, 32, 128, 1024)
    Wv = Win[:, :, 1536:2048].reshape(L, KT, 128, 2, 256)
    wv = Wv.transpose(0, 3, 2, 1, 4).reshape(L, 2, 128, 2048)
    def br(name):
        W = np.asarray(inp[name], f).reshape(L, 4, 128, 8, 128)
        return W.transpose(0, 3, 2, 1, 4).reshape(L, 8, 128, 512)
    wout = np.asarray(inp["w_out"], f).reshape(L, KT, 128, 8, 128).transpose(0, 3, 2, 1, 4).reshape(L, 8, 128, 1024)
    pw = np.asarray(inp["pool_w"], f).transpose(0, 2, 1, 3).reshape(L, 128, 512)
    sm = np.zeros((128, L * NS), f)
    for l in range(L):
        o = l * NS
        sm[:, o + 0:o + 8] = np.asarray(inp["ffn1_norm"], f)[l].reshape(8, 128).T
        sm[:, o + 8:o + 16] = np.asarray(inp["mix_norm"], f)[l].reshape(8, 128).T
        sm[:, o + 16:o + 24] = np.asarray(inp["ffn2_norm"], f)[l].reshape(8, 128).T
        sm[:, o + 24:o + 40] = np.asarray(inp["b_gate"], f)[l].reshape(16, 128).T
        sm[:, o + 40:o + 44] = np.asarray(inp["pool_scale"], f)[l].reshape(4, 128).T
        sm[0:64, o + 44] = np.asarray(inp["q_norm"], f)[l]
        sm[0:64, o + 45] = np.asarray(inp["k_norm"], f)[l]
    return dict(gu=np.ascontiguousarray(gu), dn=dn, win=np.ascontiguousarray(win), wv=np.ascontiguousarray(wv),
                wbp=np.ascontiguousarray(br("w_branch_pool")), wba=np.ascontiguousarray(br("w_branch_attn")),
                wout=np.ascontiguousarray(wout), pw=np.ascontiguousarray(pw), smalls=sm)


def _consts(half):
    f = np.float32
    cf = np.zeros((128, NCF), f)
    for m in range(64):
        if m < 32:
            cf[m + 32, m] = -1.0
        else:
            cf[m - 32, m] = 1.0
    GM = np.full((16, 16), -1e30, f)
    OWN = np.zeros((16, 16), f)
    for qt in range(16):
        sbq = 8 + qt // 2
        lo = 0 if half == 1 else 8
        GM[qt, lo:sbq] = 0.0
        OWN[qt, sbq] = 1.0
    cf[:, 64:320] = GM.reshape(1, 256)
    cf[:, 320:576] = OWN.reshape(1, 256)
    corr = np.ones((4, 16), f)
    if half == 0:
        for g in range(4):
            w = 2 ** (g + 1)
            for t in range(16):
                corr[g, t] = w / min(t + 1, w)
    cf[:, 576:640] = corr.reshape(1, 64)
    cf[:, 640] = float(half)
    cf[:, 641:769] = 1.0
    cf[:, 769] = EPS
    cb = np.zeros((128, NCB), f)
    cb[:, 0:128] = np.eye(128, dtype=f)
    cb[:, 128:256] = np.triu(np.ones((128, 128), f))
    cb[:, 256:384] = 1.0 / 1024.0
    cb[0:64, 384:448] = 1.0 / 64.0
    hd = 32
    inv_freq = (1.0 / (np.float32(10000.0) ** (np.arange(hd, dtype=f) * f(2.0 / 64)))).astype(f)
    pos = (np.arange(T, dtype=f) + f(half * T)).astype(f)
    ang = (pos[:, None] * inv_freq[None, :]).astype(f)
    cosv = np.cos(ang).astype(f).T
    sinv = np.sin(ang).astype(f).T
    cs = np.zeros((64, 2 * T), f)
    cs[0:32, 0:T] = cosv; cs[32:64, 0:T] = cosv
    cs[0:32, T:] = sinv; cs[32:64, T:] = sinv
    E = np.zeros((16, 4096), f)
    for j in range(16):
        E[j, j * 256:(j + 1) * 256] = 1.0
    return dict(cf32=cf, cb16=cb.astype(ml_dtypes.bfloat16), cs=cs, E=E.astype(ml_dtypes.bfloat16))


def kernel(**inputs):
    x = np.asarray(inputs["x"], np.float32)
    shared = _prep_shared(inputs)
    consts = [_consts(0), _consts(1)]
    in_maps = []
    for core in range(8):
        b, half = core // 2, core % 2
        m = dict(shared)
        m.update(consts[half])
        m["xT"] = np.ascontiguousarray(x[b, half * T:(half + 1) * T, :].T)
        in_maps.append(m)
    nc = build_program()
    res = run_bass_kernel_spmd(nc, in_maps, core_ids=list(range(8)))
    out = np.empty_like(x)
    for core in range(8):
        b, half = core // 2, core % 2
        out[b, half * T:(half + 1) * T, :] = np.asarray(res.results[core]["outT"], np.float32).T
    return out
```

```python
import numpy as np
import ml_dtypes
from contextlib import ExitStack
import concourse.bass as bass
import concourse.mybir as mybir
from concourse.bass_utils import run_bass_kernel_spmd

F32 = mybir.dt.float32
BF16 = mybir.dt.bfloat16
AF = mybir.ActivationFunctionType
ALU = mybir.AluOpType
AX = mybir.AxisListType

L = 2
D = 1024
KT = 8
T = 2048
NCH = 4
CW = 512
FF = 2816
NJ = 22
PARTS = [(0, 8), (8, 7), (15, 7)]
EPS = 1e-6
NS = 48
VROW = 1024
NCF = 770
NCB = 448
NEGB = -30000.0
MSTOP = 99
STOP = None

class Tok:
    __slots__ = ("sem", "val", "eng")
    def __init__(self, sem, val, eng):
        self.sem = sem; self.val = val; self.eng = eng


class Buf:
    __slots__ = ("w", "r", "name", "aux")
    def __init__(self, name=""):
        self.w = None; self.r = {}; self.name = name


ENGS = ("pe", "act", "dve", "pool", "sp")


class Prog:
    def __init__(self, esem, rings, ccsem):
        self.esem = esem
        self.rings = rings
        self.ccsem = ccsem
        self.ops = {e: [] for e in ENGS}
        self.cnt = {e: 0 for e in esem}
        self.known = {e: {} for e in ENGS}
        self.dma_tot = {}
        self.sem_of = {}
        for q in rings:
            for s in rings[q]:
                self.dma_tot[id(s)] = 0
                self.sem_of[id(s)] = s
        self.ring_i = {q: 0 for q in rings}
        self.cc_tot = 0

    def _need(self, eng, toks):
        waits = {}
        for t in toks:
            if t is None:
                continue
            if eng == "pe" and t.eng == "pe":
                continue
            k = id(t.sem)
            cur = waits.get(k)
            if cur is None or cur.val < t.val:
                waits[k] = t
        out = []
        kn = self.known[eng]
        for k, t in waits.items():
            if kn.get(k, 0) >= t.val:
                continue
            kn[k] = t.val
            out.append((t.sem, t.val))
        return out

    def _toks(self, reads, writes):
        toks = []
        for b in reads:
            toks.append(b.w)
        for b in writes:
            toks.append(b.w)
            toks.extend(b.r.values())
        return toks

    def _update(self, tok, reads, writes):
        for b in writes:
            b.w = tok; b.r = {}
        for b in reads:
            if b.w is tok:
                continue
            b.r[id(tok.sem)] = tok

    def op(self, eng, fn, reads=(), writes=()):
        waits = self._need(eng, self._toks(reads, writes))
        self.cnt[eng] += 1
        tok = Tok(self.esem[eng], self.cnt[eng], eng)
        self.ops[eng].append((waits, fn, tok.sem, 1))
        self._update(tok, reads, writes)
        return tok

    def dma(self, q, out_ap, in_ap, reads=(), writes=()):
        ring = self.rings[q]
        i = self.ring_i[q]
        self.ring_i[q] = (i + 1) % len(ring)
        sem = ring[i]
        toks = self._toks(reads, writes)
        if self.dma_tot[id(sem)] > 0:
            toks.append(Tok(sem, self.dma_tot[id(sem)], "dma"))
        waits = self._need(q, toks)
        self.dma_tot[id(sem)] += 16
        tok = Tok(sem, self.dma_tot[id(sem)], "dma")
        self.ops[q].append((waits, (lambda e, o=out_ap, s=in_ap: e.dma_start(out=o, in_=s)), sem, 16))
        self._update(tok, reads, writes)
        return tok

    def collective(self, in_ap, out_ap, reads=(), writes=()):
        waits = self._need("pool", self._toks(reads, writes))
        self.cc_tot += 1
        tok = Tok(self.ccsem, self.cc_tot, "cc")

        def fn(e, i=in_ap, o=out_ap):
            return e.collective_compute("AllGather", ALU.bypass,
                                        replica_groups=[[0, 1], [2, 3], [4, 5], [6, 7]],
                                        ins=[i], outs=[o])
        self.ops["pool"].append((waits, fn, self.ccsem, 1))
        self.ops["pool"].append(([(self.ccsem, self.cc_tot)], None, None, 0))
        self.known["pool"][id(self.ccsem)] = self.cc_tot
        self._update(tok, reads, writes)
        return tok

    def barrier(self):
        toks = [Tok(self.esem[e], self.cnt[e], e) for e in self.esem if self.cnt[e] > 0]
        for k, tot in self.dma_tot.items():
            if tot > 0:
                toks.append(Tok(self.sem_of[k], tot, "dma"))
        if self.cc_tot > 0:
            toks.append(Tok(self.ccsem, self.cc_tot, "cc"))
        for e in ENGS:
            waits = {}
            for t in toks:
                if t.eng == e:
                    continue
                waits[id(t.sem)] = t
            out = []
            kn = self.known[e]
            for k, t in waits.items():
                if kn.get(k, 0) >= t.val:
                    continue
                kn[k] = t.val
                out.append((t.sem, t.val))
            if out:
                self.ops[e].append((out, None, None, 0))

    def replay(self, eng, e):
        for waits, fn, sem, amt in self.ops[eng]:
            for (s, v) in waits:
                e.wait_ge(s, v)
            if fn is not None:
                fn(e).then_inc(sem, amt)


class Ring:
    def __init__(self, aps, name, bufs=None):
        self.aps = aps
        self.bufs = bufs if bufs is not None else [Buf(f"{name}{i}") for i in range(len(aps))]
        self.i = 0

    def next(self):
        i = self.i
        self.i = (i + 1) % len(self.aps)
        return self.aps[i], self.bufs[i]


def build_program():
    nc = bass.Bass("TRN2", target_bir_lowering=False)
    dt = nc.dram_tensor
    xT = dt("xT", [D, T], F32, kind="ExternalInput")
    outT = dt("outT", [D, T], F32, kind="ExternalOutput")
    gu_d = dt("gu", [L, 2, NJ, 128, 2048], F32, kind="ExternalInput")
    dn_d = dt("dn", [L, 2, 3, 8, 128, 1024], F32, kind="ExternalInput")
    win_d = dt("win", [L, 32, 128, 1024], F32, kind="ExternalInput")
    wv_d = dt("wv", [L, 2, 128, 2048], F32, kind="ExternalInput")
    wbp_d = dt("wbp", [L, 8, 128, 512], F32, kind="ExternalInput")
    wba_d = dt("wba", [L, 8, 128, 512], F32, kind="ExternalInput")
    wout_d = dt("wout", [L, 8, 128, 1024], F32, kind="ExternalInput")
    pw_d = dt("pw", [L, 128, 512], F32, kind="ExternalInput")
    sm_d = dt("smalls", [128, L * NS], F32, kind="ExternalInput")
    cf_d = dt("cf32", [128, NCF], F32, kind="ExternalInput")
    cb_d = dt("cb16", [128, NCB], BF16, kind="ExternalInput")
    cs_d = dt("cs", [64, 2 * T], F32, kind="ExternalInput")
    E_d = dt("E", [16, 4096], BF16, kind="ExternalInput")
    xk_own_f = dt("xk_own", [4096, 128], F32)
    xk_all_f = dt("xk_all", [8192, 128], F32)
    xv_own_f = [dt(f"xv{i}_own", [4096, 128], F32) for i in range(2)]
    xv_all_f = [dt(f"xv{i}_all", [8192, 128], F32) for i in range(2)]
    xs_own_f = dt("xs_own", [4096, 128], F32)
    xs_all_f = dt("xs_all", [8192, 128], F32)
    xk_own = xk_own_f.bitcast(BF16).reshape([512, T])
    xk_all = xk_all_f.bitcast(BF16).reshape([1024, T])
    xv_own = [t.bitcast(BF16).reshape([1024, VROW]) for t in xv_own_f]
    xv_all = [t.bitcast(BF16).reshape([2048, VROW]) for t in xv_all_f]
    xs_own = xs_own_f
    xs_all = xs_all_f

    with ExitStack() as st:
        def sb(name, shape, dtype):
            return st.enter_context(nc.sbuf_tensor(name, shape, dtype))

        x_sb = sb("x_sb", [128, KT * T], F32)
        h_sb = sb("h_sb", [128, KT * T], BF16)
        R1 = sb("R1", [128, 8 * T], BF16)
        R2 = sb("R2", [128, 8256], F32)
        ktb = [sb(f"ktb{i}", [80, 1024], BF16) for i in range(2)]
        vb = [sb(f"vb{i}", [128, 8, 128], BF16) for i in range(2)]
        stg = [sb(f"stg{i}", [128, 2048], BF16) for i in range(3)]
        pw_sb = sb("pw_sb", [128, 512], BF16)
        tf_t = [sb(f"tf{i}", [128, 528], F32) for i in range(4)]
        pt_t = [sb(f"pt{i}", [128, CW], BF16) for i in range(5)]
        qa_t = [sb(f"qa{i}", [80, CW], BF16) for i in range(2)]
        vst_t = [sb(f"vst{i}", [128, 386], BF16) for i in range(2)]
        kst_t = [sb(f"kst{i}", [64, CW], BF16) for i in range(2)]
        biasm = sb("biasm", [128, 4, 80], BF16)
        gs_t = sb("gs_t", [128, 64], F32)
        sel_t = sb("sel_t", [128, 64], F32)
        top8 = sb("top8", [128, 4, 8], F32)
        xs_sb = sb("xs_sb", [128, 128], F32)
        xsp_sb = sb("xsp_sb", [128, 128], F32)
        kb_all = sb("kb_all", [64, 8, 16], F32)
        sm_sb = sb("sm_sb", [128, L * NS], F32)
        cf_sb = sb("cf_sb", [128, NCF], F32)
        cb_sb = sb("cb_sb", [128, NCB], BF16)
        psb = [st.enter_context(nc.psum_tensor(f"ps{i}", [128, CW], F32)) for i in range(8)]

        sem = lambda n: st.enter_context(nc.semaphore(n))
        esem = {"pe": sem("s_pe"), "act": sem("s_act"), "dve": sem("s_dve")}
        rings = {"sp": [sem(f"s_sp{i}") for i in range(8)],
                 "pool": [sem(f"s_pl{i}") for i in range(8)]}
        ccsem = sem("s_cc")
        P = Prog(esem, rings, ccsem)

        def xv_(kt, c): return x_sb[:, kt * T + c * CW: kt * T + (c + 1) * CW]
        def hv_(kt, c): return h_sb[:, kt * T + c * CW: kt * T + (c + 1) * CW]
        xB = [[Buf() for _ in range(NCH)] for _ in range(KT)]
        hB = [[Buf() for _ in range(NCH)] for _ in range(KT)]
        def hff_(jj, c): return R1[:, jj * T + c * CW: jj * T + (c + 1) * CW]
        hffB = [[Buf() for _ in range(NCH)] for _ in range(8)]
        def yp_(g, c): return R1[:, g * T + c * CW: g * T + (c + 1) * CW]
        ypB = [[Buf() for _ in range(NCH)] for _ in range(4)]
        def ya_(p, c, r0, r1): return R1[r0:r1, (4 + p) * T + c * CW: (4 + p) * T + (c + 1) * CW]
        yaB = [[Buf() for _ in range(NCH)] for _ in range(4)]
        R2b = R2[:, 0:8192].bitcast(BF16)
        def mg_(m, c): return R2b[:, m * T + c * CW: m * T + (c + 1) * CW]
        mgB = [[Buf() for _ in range(NCH)] for _ in range(8)]
        cosB = Buf("cos"); sinB = Buf("sin")
        def cos_(c): return R2[0:64, c * CW:(c + 1) * CW]
        def sin_(c): return R2[0:64, T + c * CW: T + (c + 1) * CW]
        UB0 = 4096
        def ub_(i, a, b): return R2[:, UB0 + i * 2080 + a: UB0 + i * 2080 + b]
        ubB = [Buf("ub0"), Buf("ub1")]

        stgR = Ring([s[:] for s in stg], "stg")
        tfR = Ring([s[:] for s in tf_t], "tf")
        ptR = Ring([s[:] for s in pt_t], "pt")
        sqR = ptR
        qaR = Ring(qa_t, "qa")
        vstR = Ring(vst_t, "vst")
        kstR = Ring(kst_t, "kst")
        ktR = Ring(ktb, "ktb")
        ktEB = [Buf("ktE0"), Buf("ktE1")]
        vbR = Ring(vb, "vb")
        psBufs = [Buf(f"psum{i}") for i in range(8)]
        psS = Ring(psb[0:3], "psS", psBufs[0:3])
        psO = Ring(psb[3:5], "psO", psBufs[3:5])
        psG = Ring(psb[5:8], "psG", psBufs[5:8])
        psA = Ring(psb[0:8], "psA", psBufs[0:8])
        rstdB = tfR.bufs[0]; pwB = Buf("pw")
        rstd_ap = tf_t[0][:, 0:CW]
        smB = Buf("sm"); cfB = Buf("cf"); cbB = Buf("cb")
        biasB = Buf("biasm"); gsB = Buf("gs"); selB = Buf("sel"); top8B = Buf("top8")
        xsB = Buf("xs"); xspB = Buf("xsp"); kbB = Buf("kb_all")
        xkB = [[Buf() for _ in range(NCH)] for _ in range(8)]
        xvB = [[Buf() for _ in range(2)] for _ in range(16)]
        xsoB = Buf("xs_own")
        xkaB = Buf("xk_all"); xvaB = [Buf("xv0_all"), Buf("xv1_all")]; xsaB = Buf("xs_all")

        ident = cb_sb[:, 0:128]
        tri = cb_sb[:, 128:256]
        onesm = cb_sb[:, 256:384]
        ones64 = cb_sb[0:64, 384:448]
        rotT = cf_sb[0:64, 0:64]
        def GM_(c): return cf_sb[:, 64 + c * 64: 64 + (c + 1) * 64]
        def OWN_(c): return cf_sb[:, 320 + c * 64: 320 + (c + 1) * 64]
        def corr_(g): return cf_sb[:, 576 + g * 16: 576 + (g + 1) * 16]
        hasprev = cf_sb[:, 640:641]
        onesf = cf_sb[:, 641:769]
        eps_ap = cf_sb[:, 769:770]
        def smc(l, col, rows=128): return sm_sb[0:rows, l * NS + col: l * NS + col + 1]

        mm_count = [0]

        def mm(out, lhsT, rhs, start, stop, reads, wbuf):
            P.op("pe", lambda e, o=out, a=lhsT, b=rhs, s=start, t=stop:
                 e.matmul(o, a, b, start=s, stop=t), reads=reads, writes=[wbuf])

        def load_w(src_ap, ncols):
            ap, b = stgR.next()
            P.dma("pool", ap[:, 0:ncols], src_ap, reads=[], writes=[b])
            return ap, b

        P.dma("sp", sm_sb[:], sm_d[:, :], writes=[smB])
        P.dma("sp", cf_sb[:], cf_d[:, :], writes=[cfB])
        P.dma("sp", cb_sb[:], cb_d[:, :], writes=[cbB])
        for kt in range(KT):
            P.dma("sp", x_sb[:, kt * T:(kt + 1) * T], xT[kt * 128:(kt + 1) * 128, :], writes=xB[kt])
        P.op("dve", lambda e: e.memset(biasm[:], 0.0), writes=[biasB])
        for i in range(2):
            a, b = vstR.next()
            P.op("dve", lambda e, a=a: e.memset(a[:], 0.0), writes=[b])
            for pl in range(2):
                P.op("dve", lambda e, a=a, pl=pl: e.memset(a[:, pl * 193 + 64: pl * 193 + 66], 1.0), writes=[b])
        P.op("dve", lambda e: e.memset(xs_sb[:], 0.0), writes=[xsB])

        def emit_norm(l, gcol):
            for c in range(NCH):
                pa, pb_ = psG.next()
                for kt in range(KT):
                    sq, sqb = sqR.next()
                    P.op("act", lambda e, o=sq, i=xv_(kt, c): e.activation(out=o, in_=i, func=AF.Square),
                         reads=[xB[kt][c]], writes=[sqb])
                    mm(pa[:], onesm, sq, kt == 0, kt == KT - 1, [sqb, cbB], pb_)
                P.op("act", lambda e, i=pa: e.activation(out=rstd_ap, in_=i[:], func=AF.Ln, bias=eps_ap),
                     reads=[pb_, cfB], writes=[rstdB])
                P.op("act", lambda e: e.activation(out=rstd_ap, in_=rstd_ap, func=AF.Exp, scale=-0.5),
                     reads=[], writes=[rstdB])
                for kt in range(KT):
                    P.op("dve", lambda e, o=hv_(kt, c), i=xv_(kt, c), g=smc(l, gcol + kt):
                         e.scalar_tensor_tensor(o, i, g, rstd_ap, ALU.mult, ALU.mult),
                         reads=[xB[kt][c], rstdB, smB], writes=[hB[kt][c]])

        def emit_ffn(l, which):
            emit_norm(l, 0 if which == 0 else 16)
            for part, (j0, nk) in enumerate(PARTS):
                for jj in range(nk):
                    j = j0 + jj
                    w, wb_ = load_w(gu_d[l, which, j], 2048)
                    for c in range(NCH):
                        pa, pab = psG.next()
                        for kt in range(KT):
                            mm(pa[:], w[:, kt * 256: kt * 256 + 128], hv_(kt, c), kt == 0, kt == KT - 1,
                               [wb_, hB[kt][c]], pab)
                        pb2, pbb = psG.next()
                        for kt in range(KT):
                            mm(pb2[:], w[:, kt * 256 + 128: kt * 256 + 256], hv_(kt, c), kt == 0, kt == KT - 1,
                               [wb_, hB[kt][c]], pbb)
                        tf, tfb = tfR.next()
                        P.op("act", lambda e, o=tf, i=pa: e.activation(out=o[:, 0:CW], in_=i[:], func=AF.Silu),
                             reads=[pab], writes=[tfb])
                        P.op("dve", lambda e, o=hff_(jj, c), a=tf, b=pb2:
                             e.tensor_tensor(o, a[:, 0:CW], b[:], ALU.mult),
                             reads=[tfb, pbb], writes=[hffB[jj][c]])
                for m in range(8):
                    w, wb_ = load_w(dn_d[l, which, part, m][:, 0:nk * 128], nk * 128)
                    for c in range(NCH):
                        po, pob = psG.next()
                        for kk in range(nk):
                            mm(po[:], w[:, kk * 128:(kk + 1) * 128], hff_(kk, c), kk == 0, kk == nk - 1,
                               [wb_, hffB[kk][c]], pob)
                        P.op("dve", lambda e, o=xv_(m, c), i=po:
                             e.scalar_tensor_tensor(o, i[:], 0.5, o, ALU.mult, ALU.add),
                             reads=[pob], writes=[xB[m][c]])

        def normrope_g(l, pin, pinb, gcol, c, sqring, out, psr=None):
            psr = psr or psG
            sq, sqb = sqring.next()
            P.op("act", lambda e, o=sq, i=pin: e.activation(out=o[0:64, :], in_=i[0:64, :], func=AF.Square),
                 reads=[pinb], writes=[sqb])
            yield
            ps_, psb_ = psr.next()
            mm(ps_[0:64, :], ones64, sq[0:64, :], True, True, [sqb, cbB], psb_)
            yield
            t2, t2b = tfR.next()
            P.op("act", lambda e, o=t2, i=ps_: e.activation(out=o[0:64, 0:CW], in_=i[0:64, :], func=AF.Ln, bias=eps_ap[0:64, :]),
                 reads=[psb_, cfB], writes=[t2b])
            P.op("act", lambda e, o=t2: e.activation(out=o[0:64, 0:CW], in_=o[0:64, 0:CW], func=AF.Exp, scale=-0.5),
                 reads=[], writes=[t2b])
            yield
            t3, t3b = tfR.next()
            P.op("dve", lambda e, o=t3, i=pin, r=t2, g=smc(l, gcol, 64):
                 e.scalar_tensor_tensor(o[0:64, 0:CW], i[0:64, :], g, r[0:64, 0:CW], ALU.mult, ALU.mult),
                 reads=[pinb, t2b, smB], writes=[t3b])
            yield
            pr, prb = psr.next()
            mm(pr[0:64, :], rotT, t3[0:64, 0:CW], True, True, [t3b, cfB], prb)
            yield
            P.op("dve", lambda e, o=t2, i=pr, s=sin_(c): e.tensor_tensor(o[0:64, 0:CW], i[0:64, :], s, ALU.mult),
                 reads=[prb, sinB], writes=[t2b])
            P.op("dve", lambda e, o=t3, s=cos_(c): e.tensor_tensor(o[0:64, 0:CW], o[0:64, 0:CW], s, ALU.mult),
                 reads=[cosB], writes=[t3b])
            yield
            P.op("dve", lambda e, o=t3, a=t2: e.tensor_tensor(o[0:64, 0:CW], o[0:64, 0:CW], a[0:64, 0:CW], ALU.add),
                 reads=[t2b], writes=[t3b])
            out["t"] = t3; out["b"] = t3b

        def normrope(l, pin, pinb, gcol, c):
            out = {}
            for _ in normrope_g(l, pin, pinb, gcol, c, sqR, out):
                pass
            return out["t"], out["b"]

        def emit_mixer(l):
            emit_norm(l, 8)
            P.barrier()
            P.dma("sp", R2[0:64, 0:T], cs_d[:, 0:T], writes=[cosB])
            P.dma("sp", R2[0:64, T:2 * T], cs_d[:, T:2 * T], writes=[sinB])
            def kchain_g(head, c, w, wb_):
                hh = head % 2
                pk, pkb = psA.next()
                for kt in range(KT):
                    mm(pk[0:64, :], w[:, kt * 128 + hh * 64: kt * 128 + hh * 64 + 64], hv_(kt, c),
                       kt == 0, kt == KT - 1, [wb_, hB[kt][c]], pkb)
                yield
                o2 = {}
                yield from normrope_g(l, pk, pkb, 45, c, sqR, o2, psA)
                kf, kfb = o2["t"], o2["b"]
                yield
                ks, ksb = kstR.next()
                P.op("act", lambda e, o=ks, i=kf: e.activation(out=o[:], in_=i[0:64, 0:CW], func=AF.Copy),
                     reads=[kfb], writes=[ksb])
                P.dma("sp", xk_own[head * 64:(head + 1) * 64, c * CW:(c + 1) * CW], ks[:],
                      reads=[ksb], writes=[xkB[head][c]])
                P.op("dve", lambda e, i=kf, o=xs_sb[0:64, head * 8 + 2 * c: head * 8 + 2 * c + 2]:
                     e.reduce_sum(o, i[0:64, 0:CW].rearrange("p (a b) -> p a b", a=2), AX.X),
                     reads=[kfb], writes=[xsB])

            for tp in range(4):
                w, wb_ = load_w(win_d[l, 8 + tp], 1024)
                todo = [kchain_g(2 * tp + hh, c, w, wb_) for hh in range(2) for c in range(NCH)]
                active = []
                while todo or active:
                    while todo and len(active) < 2:
                        active.append(todo.pop(0))
                    for g_ in list(active):
                        try:
                            next(g_)
                        except StopIteration:
                            active.remove(g_)
            if MSTOP <= 1:
                P.barrier(); return
            for vh in range(2):
                w, wb_ = load_w(wv_d[l, vh], 2048)
                for tt in range(16):
                    c, o4 = tt // 4, (tt % 4) * 128
                    pv, pvb = psG.next()
                    for kt in range(KT):
                        mm(pv[:, 0:256], hv_(kt, c)[:, o4:o4 + 128], w[:, kt * 256:(kt + 1) * 256],
                           kt == 0, kt == KT - 1, [wb_, hB[kt][c]], pvb)
                    vs, vsb = vstR.next()
                    for pl in range(2):
                        P.op("act", lambda e, o=vs, i=pv, pl=pl:
                             e.activation(out=o[:, pl * 193: pl * 193 + 64], in_=i[:, pl * 128: pl * 128 + 64], func=AF.Copy),
                             reads=[pvb], writes=[vsb])
                        P.op("act", lambda e, o=vs, i=pv, pl=pl:
                             e.activation(out=o[:, pl * 193 + 129: pl * 193 + 193], in_=i[:, pl * 128 + 64: pl * 128 + 128], func=AF.Copy),
                             reads=[pvb], writes=[vsb])
                    P.dma("sp", xv_own[tt // 8][(tt % 8) * 128:(tt % 8 + 1) * 128, vh * 386:(vh + 1) * 386], vs[:],
                          reads=[vsb], writes=[xvB[tt][vh]])
            if MSTOP <= 2:
                P.barrier(); return
            for g in range(4):
                w, wb_ = load_w(win_d[l, g], 1024)
                pu, pub = psG.next()
                for kt in range(KT):
                    mm(pu[:, 0:16], w[:, kt * 128:(kt + 1) * 128], hv_(kt, 3)[:, CW - 16:CW], kt == 0, kt == KT - 1,
                       [wb_, hB[kt][3]], pub)
                P.op("act", lambda e, i=pu, o=xs_sb[:, 64 + g * 16: 64 + (g + 1) * 16]:
                     e.activation(out=o, in_=i[:, 0:16], func=AF.Copy), reads=[pub], writes=[xsB])
            P.op("dve", lambda e: e.tensor_scalar(xs_sb[0:64, 0:64], xs_sb[0:64, 0:64], 1.0 / 256.0, None, ALU.mult),
                 reads=[], writes=[xsB])
            P.dma("sp", xs_own[0:128, :], xs_sb[:], reads=[xsB], writes=[xsoB])
            if MSTOP <= 3:
                P.barrier(); return
            P.collective(xs_own_f.ap().opt(), xs_all_f.ap().opt(), reads=[xsoB], writes=[xsaB])
            P.dma("sp", xsp_sb[:], xs_all[0:128, :], reads=[xsaB], writes=[xspB])
            pwt, pwb = pw_sb[:], pwB
            P.dma("pool", pwt, pw_d[l], reads=[], writes=[pwB])
            pre_w = [load_w(win_d[l, g], 1024) for g in range(3)]
            P.collective(xk_own_f.ap().opt(), xk_all_f.ap().opt(),
                         reads=[b for r in xkB for b in r], writes=[xkaB])
            for i in range(2):
                P.collective(xv_own_f[i].ap().opt(), xv_all_f[i].ap().opt(),
                             reads=[b for r in xvB[8 * i:8 * i + 8] for b in r], writes=[xvaB[i]])
            P.op("dve", lambda e: e.tensor_copy(kb_all[:, :, 0:8], xsp_sb[0:64, 0:64].rearrange("p (a b) -> p a b", a=8)),
                 reads=[xspB], writes=[kbB])
            P.op("dve", lambda e: e.tensor_copy(kb_all[:, :, 8:16], xs_sb[0:64, 0:64].rearrange("p (a b) -> p a b", a=8)),
                 reads=[xsB], writes=[kbB])
            if MSTOP <= 4:
                P.barrier(); return
            for g in range(4):
                wwin = 2 ** (g + 1)
                w, wb_ = pre_w[g] if g < 3 else load_w(win_d[l, g], 1024)
                ubuf = ubB[g % 2]
                ui = g % 2
                P.op("dve", lambda e, o=ub_(ui, 0, 16), i=xsp_sb[:, 64 + g * 16: 64 + (g + 1) * 16]:
                     e.tensor_scalar(o, i, hasprev, None, ALU.mult), reads=[xspB, cfB], writes=[ubuf])
                for c in range(NCH):
                    pu, pub = psG.next()
                    for kt in range(KT):
                        mm(pu[:], w[:, kt * 128:(kt + 1) * 128], hv_(kt, c), kt == 0, kt == KT - 1,
                           [wb_, hB[kt][c]], pub)
                    P.op("act", lambda e, i=pu, o=ub_(ui, 16 + c * CW, 16 + (c + 1) * CW):
                         e.activation(out=o, in_=i[:], func=AF.Copy), reads=[pub], writes=[ubuf])
                for c in range(NCH):
                    b0 = c * CW
                    cur = (lambda ui_, b0_: (lambda a, b_: ub_(ui_, b0_ + a, b0_ + b_)))(ui, b0)
                    curb = ubuf
                    sh = 1
                    while sh < wwin:
                        nt, ntb = tfR.next()
                        P.op("dve", lambda e, o=nt, hi=cur(sh, 528), lo=cur(0, 528 - sh), sh=sh:
                             e.tensor_tensor(o[:, sh:528], hi, lo, ALU.add), reads=[curb], writes=[ntb])
                        cur = (lambda nt_: (lambda a, b_: nt_[:, a:b_]))(nt)
                        curb = ntb
                        sh *= 2
                    if c == 0:
                        P.op("dve", lambda e, o=cur(16, 32), cr=corr_(g): e.tensor_tensor(o, o, cr, ALU.mult),
                             reads=[cfB], writes=[curb])
                    dtile, db = ptR.next()
                    P.op("dve", lambda e, o=dtile, s=cur(16, 528), u=ub_(ui, 16 + c * CW, 16 + (c + 1) * CW), sc=1.0 / wwin:
                         e.scalar_tensor_tensor(o, s, sc, u, ALU.mult, ALU.subtract),
                         reads=[curb, ubuf], writes=[db])
                    py, pyb = psG.next()
                    mm(py[:], pwt[:, g * 128:(g + 1) * 128], dtile, True, True, [pwb, db], pyb)
                    P.op("act", lambda e, i=py, o=yp_(g, c), s=smc(l, 40 + g):
                         e.activation(out=o, in_=i[:], func=AF.Copy, scale=s), reads=[pyb, smB], writes=[ypB[g][c]])
            if MSTOP <= 5:
                P.barrier(); return
            wqs = {}

            def prologue_g(c, head, out):
                pair, hh = head // 2, head % 2
                if hh == 0:
                    wqs["w"] = load_w(win_d[l, 4 + pair], 1024)
                wq, wqb = wqs["w"]
                pq, pqb = psG.next()
                for kt in range(KT):
                    mm(pq[0:64, :], wq[:, kt * 128 + hh * 64: kt * 128 + hh * 64 + 64], hv_(kt, c),
                       kt == 0, kt == KT - 1, [wqb, hB[kt][c]], pqb)
                yield
                o2 = {}
                yield from normrope_g(l, pq, pqb, 44, c, kstR, o2)
                qf, qfb = o2["t"], o2["b"]
                yield
                qa, qab = qaR.next()
                P.op("act", lambda e, o=qa, i=qf: e.activation(out=o[0:64, :], in_=i[0:64, 0:CW], func=AF.Copy),
                     reads=[qfb], writes=[qab])
                pg, pgb = psG.next()
                for qt in range(4):
                    mm(pg[:, qt * 16:(qt + 1) * 16], qf[0:64, qt * 128:(qt + 1) * 128], kb_all[:, head, :],
                       True, True, [qfb, kbB], pgb)
                yield
                P.op("dve", lambda e, i=pg, g=GM_(c): e.tensor_tensor(gs_t[:], i[:, 0:64], g, ALU.add),
                     reads=[pgb, cfB], writes=[gsB])
                for qt in range(4):
                    P.op("dve", lambda e, qt=qt: e.max(top8[:, qt, :], gs_t[:, qt * 16:(qt + 1) * 16]),
                         reads=[gsB], writes=[top8B])
                yield
                for qt in range(4):
                    P.op("dve", lambda e, qt=qt: e.tensor_scalar(sel_t[:, qt * 16:(qt + 1) * 16],
                                                                gs_t[:, qt * 16:(qt + 1) * 16],
                                                                top8[:, qt, 2:3], None, ALU.is_ge),
                         reads=[gsB, top8B], writes=[selB])
                P.op("dve", lambda e: e.scalar_tensor_tensor(sel_t[:], gs_t[:], -1e29, sel_t[:], ALU.is_gt, ALU.mult),
                     reads=[gsB], writes=[selB])
                P.op("dve", lambda e, o=OWN_(c): e.tensor_tensor(sel_t[:], sel_t[:], o, ALU.max),
                     reads=[cfB], writes=[selB])
                P.op("dve", lambda e: e.tensor_scalar(biasm[:, :, 64:80], sel_t[:].rearrange("p (a b) -> p a b", a=4),
                                                      -1.0, -NEGB, ALU.add, ALU.mult),
                     reads=[selB], writes=[biasB])
                yield
                pb_, pbb = psG.next()
                for qt in range(4):
                    mm(pb_[0:80, qt * 128:(qt + 1) * 128], biasm[:, qt, :], ident, True, True, [biasB, cbB], pbb)
                yield
                P.op("act", lambda e, o=qa, i=pb_: e.activation(out=o[64:80, :], in_=i[64:80, :], func=AF.Copy),
                     reads=[pbb], writes=[qab])
                out["qa"] = qa; out["qab"] = qab

            def attend(c, head, qa, qab, nxt):
                    pair, hh = head // 2, head % 2
                    pieces = []
                    for pc in range(2):
                        pieces.append((xk_all, xv_all[pc], xkaB, xvaB[pc], pc * 1024, pc * 1024, 8, None))
                    n_own = 4 * (c + 1)
                    for op_ in range(2):
                        nt_ = min(8, n_own - op_ * 8)
                        if nt_ > 0:
                            pieces.append((xk_own, xv_own[op_], None, None, op_ * 1024, T + op_ * 1024, nt_, op_ * 8))
                    po, pob = psO.next()
                    VW = 65 if hh == 0 else 128
                    vc0 = pair * 193 + (0 if hh == 0 else 65)
                    total_tiles = sum(p[6] for p in pieces)
                    tiles = []
                    first = {}
                    for pi, pcs in enumerate(pieces):
                        own0, ntl = pcs[7], pcs[6]
                        for i in range(ntl):
                            q0, diag = 0, False
                            if own0 is not None and own0 + i >= 4 * c:
                                q0, diag = (own0 + i - 4 * c) * 128, True
                            tiles.append((pi, i, q0, diag))
                    kslot, vslot = {}, {}

                    def load_k(pi, head=head):
                        sk, sv, skb, svb, k0, s0, ntl, own0 = pieces[pi]
                        nk = ntl * 128
                        ktEbuf = ktEB[ktR.i]
                        kt_t, ktbuf = ktR.next()
                        kreads = [xkB[head][cc] for cc in range(NCH)] if skb is None else [skb]
                        P.dma("sp", kt_t[0:64, 0:nk], sk[head * 64:(head + 1) * 64, k0:k0 + nk], reads=kreads, writes=[ktbuf])
                        P.dma("sp", kt_t[64:80, 0:nk], E_d[:, s0:s0 + nk], reads=[], writes=[ktEbuf])
                        kslot[pi] = (kt_t, ktbuf, ktEbuf)

                    def load_v(pi, pair=pair, VW=VW, vc0=vc0):
                        sk, sv, skb, svb, k0, s0, ntl, own0 = pieces[pi]
                        nk = ntl * 128
                        vb_t, vbbuf = vbR.next()
                        vreads = [xvB[tt][pair // 2] for tt in range(k0 // 128, k0 // 128 + ntl)] if svb is None else [svb]
                        P.dma("sp", vb_t[:, 0:ntl, 0:VW],
                              sv[0:nk, vc0:vc0 + VW].rearrange("(kt p) c -> p kt c", p=128),
                              reads=vreads, writes=[vbbuf])
                        vslot[pi] = (vb_t, vbbuf)

                    for pi in range(min(2, len(pieces))):
                        load_k(pi); load_v(pi)
                    LA = 3
                    stash = {}
                    n_t = len(tiles)
                    for t in range(n_t + LA):
                        if nxt is not None and t >= 3:
                            next(nxt, None)
                        if t < n_t:
                            pi, i, q0, diag = tiles[t]
                            if i == 0 and pi >= 1 and pi + 1 < len(pieces):
                                load_k(pi + 1)
                            kt_t, ktbuf, ktEbuf = kslot[pi]
                            pss, pssb = psS.next()
                            mm(pss[:, q0:CW], kt_t[0:80, i * 128:(i + 1) * 128], qa[0:80, q0:CW], True, True,
                               [ktbuf, ktEbuf, qab], pssb)
                            pt, ptb = ptR.next()
                            P.op("act", lambda e, o=pt, i_=pss, q0=q0:
                                 e.activation(out=o[:, q0:CW], in_=i_[:, q0:CW], func=AF.Exp, scale=0.125),
                                 reads=[pssb], writes=[ptb])
                            if diag:
                                P.op("dve", lambda e, o=pt, q0=q0:
                                     e.tensor_tensor(o[:, q0:q0 + 128], o[:, q0:q0 + 128], tri, ALU.mult),
                                     reads=[cbB], writes=[ptb])
                            stash[t] = (pt, ptb)
                        tp = t - LA
                        if tp >= 0:
                            pi, i, q0, diag = tiles[tp]
                            if i == 0 and pi >= 1 and pi + 1 < len(pieces):
                                load_v(pi + 1)
                            vb_t, vbbuf = vslot[pi]
                            pt, ptb = stash.pop(tp)
                            P.op("pe", lambda e, o=po, v=vb_t, i=i, VW=VW, p_=pt, q0=q0, s=(tp == 0), t_=(tp == n_t - 1):
                                 e.matmul(o[0:VW, q0:CW], v[:, i, 0:VW], p_[:, q0:CW], start=s, stop=t_, skip_group_check=True),
                                 reads=[vbbuf, ptb], writes=[pob])
                    dr = 64 if hh == 0 else 0
                    r0, r1 = (0, 64) if hh == 0 else (64, 128)
                    if nxt is not None:
                        for _ in nxt:
                            pass
                    rd, rdb = tfR.next()
                    P.op("act", lambda e, o=rd, i=po, dr=dr: e.activation(out=o[dr:dr + 1, 0:CW], in_=i[dr:dr + 1, :], func=AF.Ln),
                         reads=[pob], writes=[rdb])
                    P.op("act", lambda e, o=rd, dr=dr: e.activation(out=o[dr:dr + 1, 0:CW], in_=o[dr:dr + 1, 0:CW], func=AF.Exp, scale=-1.0),
                         reads=[], writes=[rdb])
                    pbc, pbcb = psG.next()
                    mm(pbc[0:r1, :], onesf[dr:dr + 1, 0:r1], rd[dr:dr + 1, 0:CW], True, True, [rdb, cfB], pbcb)
                    on, onb = tfR.next()
                    P.op("act", lambda e, o=on, i=po, r0=r0, r1=r1: e.activation(out=o[r0:r1, 0:CW], in_=i[r0:r1, :], func=AF.Copy),
                         reads=[pob], writes=[onb])
                    P.op("dve", lambda e, o=ya_(pair, c, r0, r1), a=on, b=pbc, r0=r0, r1=r1:
                         e.tensor_tensor(o, a[r0:r1, 0:CW], b[r0:r1, :], ALU.mult),
                         reads=[onb, pbcb], writes=[yaB[pair][c]])

            order = [(c, head) for c in range(NCH) for head in range(8)]
            st_cur = {}
            for _ in prologue_g(order[0][0], order[0][1], st_cur):
                pass
            for idx, (c, head) in enumerate(order):
                st_nxt = {}
                nxt = prologue_g(order[idx + 1][0], order[idx + 1][1], st_nxt) if idx + 1 < len(order) else None
                attend(c, head, st_cur["qa"], st_cur["qab"], nxt)
                st_cur = st_nxt
            if MSTOP <= 6:
                P.barrier(); return
            P.barrier()
            for m in range(8):
                w0, w0b = load_w(win_d[l, 16 + m], 1024)
                w1, w1b = load_w(win_d[l, 24 + m], 1024)
                wpa, wpab = stgR.next()
                P.dma("pool", wpa[:, 0:512], wbp_d[l, m], reads=[], writes=[wpab])
                P.dma("pool", wpa[:, 512:1024], wba_d[l, m], reads=[], writes=[wpab])
                for c in range(NCH):
                    g0, g0b = psG.next()
                    for kt in range(KT):
                        mm(g0[:], w0[:, kt * 128:(kt + 1) * 128], hv_(kt, c), kt == 0, kt == KT - 1, [w0b, hB[kt][c]], g0b)
                    ta, tab = tfR.next()
                    P.op("act", lambda e, o=ta, i=g0, b=smc(l, 24 + m): e.activation(out=o[:, 0:CW], in_=i[:], func=AF.Sigmoid, bias=b),
                         reads=[g0b, smB], writes=[tab])
                    g1, g1b = psG.next()
                    for kt in range(KT):
                        mm(g1[:], w1[:, kt * 128:(kt + 1) * 128], hv_(kt, c), kt == 0, kt == KT - 1, [w1b, hB[kt][c]], g1b)
                    tb, tbb = tfR.next()
                    P.op("act", lambda e, o=tb, i=g1, b=smc(l, 32 + m): e.activation(out=o[:, 0:CW], in_=i[:], func=AF.Sigmoid, bias=b),
                         reads=[g1b, smB], writes=[tbb])
                    bp, bpb = psG.next()
                    for kk in range(4):
                        mm(bp[:], wpa[:, kk * 128:(kk + 1) * 128], yp_(kk, c), kk == 0, kk == 3, [wpab, ypB[kk][c]], bpb)
                    P.op("dve", lambda e, a=ta, i=bp: e.tensor_tensor(a[:, 0:CW], a[:, 0:CW], i[:], ALU.mult),
                         reads=[bpb], writes=[tab])
                    ba, bab = psG.next()
                    for kk in range(4):
                        mm(ba[:], wpa[:, 512 + kk * 128: 512 + (kk + 1) * 128], ya_(kk, c, 0, 128), kk == 0, kk == 3, [wpab, yaB[kk][c]], bab)
                    P.op("dve", lambda e, a=tb, i=ba: e.tensor_tensor(a[:, 0:CW], a[:, 0:CW], i[:], ALU.mult),
                         reads=[bab], writes=[tbb])
                    P.op("dve", lambda e, o=mg_(m, c), a=ta, b=tb: e.tensor_tensor(o, a[:, 0:CW], b[:, 0:CW], ALU.add),
                         reads=[tab, tbb], writes=[mgB[m][c]])
            for m2 in range(8):
                w, wb_ = load_w(wout_d[l, m2], 1024)
                for c in range(NCH):
                    po, pob = psG.next()
                    for m in range(8):
                        mm(po[:], w[:, m * 128:(m + 1) * 128], mg_(m, c), m == 0, m == 7, [wb_, mgB[m][c]], pob)
                    P.op("dve", lambda e, o=xv_(m2, c), i=po: e.tensor_tensor(o, i[:], o, ALU.add),
                         reads=[pob], writes=[xB[m2][c]])
            P.barrier()

        ph = 0
        for l in range(L):
            for f in (lambda: emit_ffn(l, 0), lambda: emit_mixer(l), lambda: emit_ffn(l, 1)):
                if STOP is None or ph < STOP:
                    f()
                ph += 1
            P.barrier()
        for kt in range(KT):
            P.dma("sp", outT[kt * 128:(kt + 1) * 128, :], x_sb[:, kt * T:(kt + 1) * T], reads=xB[kt], writes=[])
        P.barrier()

        with nc.Block() as block:
            @block.tensor
            def _(e):
                P.replay("pe", e)

            @block.scalar
            def _(e):
                P.replay("act", e)

            @block.vector
            def _(e):
                P.replay("dve", e)

            @block.gpsimd
            def _(e):
                P.replay("pool", e)

            @block.sync
            def _(e):
                P.replay("sp", e)
    return nc


def _prep_shared(inp):
    f = np.float32
    gu = np.empty((L, 2, NJ, 128, 2048), f)
    dn = np.zeros((L, 2, 3, 8, 128, 1024), f)
    for wi, (ngu, ndn) in enumerate((("ffn1_w_gate_up", "ffn1_w_down"), ("ffn2_w_gate_up", "ffn2_w_down"))):
        W = np.asarray(inp[ngu], f)
        A = W[:, :, :FF].reshape(L, KT, 128, NJ, 128)
        B = W[:, :, FF:].reshape(L, KT, 128, NJ, 128)
        t = np.stack([A, B], axis=4)
        gu[:, wi] = t.transpose(0, 3, 2, 1, 4, 5).reshape(L, NJ, 128, 2048)
        Wd = np.asarray(inp[ndn], f).reshape(L, NJ, 128, 8, 128)
        for part, (j0, nk) in enumerate(PARTS):
            t = Wd[:, j0:j0 + nk].transpose(0, 3, 2, 1, 4)
            dn[:, wi, part, :, :, :nk * 128] = t.reshape(L, 8, 128, nk * 128)
    Win = np.asarray(inp["w_in"], f)
    win = Win.reshape(L, KT, 128, 32, 128).transpose(0, 3, 2, 1, 4).reshape(L, 32, 128, 1024)
    Wv = Win[:, :, 1536:2048].reshape(L, KT, 128, 2, 256)
    wv = Wv.transpose(0, 3, 2, 1, 4).reshape(L, 2, 128, 2048)
    def br(name):
        W = np.asarray(inp[name], f).reshape(L, 4, 128, 8, 128)
        return W.transpose(0, 3, 2, 1, 4).reshape(L, 8, 128, 512)
    wout = np.asarray(inp["w_out"], f).reshape(L, KT, 128, 8, 128).transpose(0, 3, 2, 1, 4).reshape(L, 8, 128, 1024)
    pw = np.asarray(inp["pool_w"], f).transpose(0, 2, 1, 3).reshape(L, 128, 512)
    sm = np.zeros((128, L * NS), f)
    for l in range(L):
        o = l * NS
        sm[:, o + 0:o + 8] = np.asarray(inp["ffn1_norm"], f)[l].reshape(8, 128).T
        sm[:, o + 8:o + 16] = np.asarray(inp["mix_norm"], f)[l].reshape(8, 128).T
        sm[:, o + 16:o + 24] = np.asarray(inp["ffn2_norm"], f)[l].reshape(8, 128).T
        sm[:, o + 24:o + 40] = np.asarray(inp["b_gate"], f)[l].reshape(16, 128).T
        sm[:, o + 40:o + 44] = np.asarray(inp["pool_scale"], f)[l].reshape(4, 128).T
        sm[0:64, o + 44] = np.asarray(inp["q_norm"], f)[l]
        sm[0:64, o + 45] = np.asarray(inp["k_norm"], f)[l]
    return dict(gu=np.ascontiguousarray(gu), dn=dn, win=np.ascontiguousarray(win), wv=np.ascontiguousarray(wv),
                wbp=np.ascontiguousarray(br("w_branch_pool")), wba=np.ascontiguousarray(br("w_branch_attn")),
                wout=np.ascontiguousarray(wout), pw=np.ascontiguousarray(pw), smalls=sm)


def _consts(half):
    f = np.float32
    cf = np.zeros((128, NCF), f)
    for m in range(64):
        if m < 32:
            cf[m + 32, m] = -1.0
        else:
            cf[m - 32, m] = 1.0
    GM = np.full((16, 16), -1e30, f)
    OWN = np.zeros((16, 16), f)
    for qt in range(16):
        sbq = 8 + qt // 2
        lo = 0 if half == 1 else 8
        GM[qt, lo:sbq] = 0.0
        OWN[qt, sbq] = 1.0
    cf[:, 64:320] = GM.reshape(1, 256)
    cf[:, 320:576] = OWN.reshape(1, 256)
    corr = np.ones((4, 16), f)
    if half == 0:
        for g in range(4):
            w = 2 ** (g + 1)
            for t in range(16):
                corr[g, t] = w / min(t + 1, w)
    cf[:, 576:640] = corr.reshape(1, 64)
    cf[:, 640] = float(half)
    cf[:, 641:769] = 1.0
    cf[:, 769] = EPS
    cb = np.zeros((128, NCB), f)
    cb[:, 0:128] = np.eye(128, dtype=f)
    cb[:, 128:256] = np.triu(np.ones((128, 128), f))
    cb[:, 256:384] = 1.0 / 1024.0
    cb[0:64, 384:448] = 1.0 / 64.0
    hd = 32
    inv_freq = (1.0 / (np.float32(10000.0) ** (np.arange(hd, dtype=f) * f(2.0 / 64)))).astype(f)
    pos = (np.arange(T, dtype=f) + f(half * T)).astype(f)
    ang = (pos[:, None] * inv_freq[None, :]).astype(f)
    cosv = np.cos(ang).astype(f).T
    sinv = np.sin(ang).astype(f).T
    cs = np.zeros((64, 2 * T), f)
    cs[0:32, 0:T] = cosv; cs[32:64, 0:T] = cosv
    cs[0:32, T:] = sinv; cs[32:64, T:] = sinv
    E = np.zeros((16, 4096), f)
    for j in range(16):
        E[j, j * 256:(j + 1) * 256] = 1.0
    return dict(cf32=cf, cb16=cb.astype(ml_dtypes.bfloat16), cs=cs, E=E.astype(ml_dtypes.bfloat16))


def kernel(**inputs):
    x = np.asarray(inputs["x"], np.float32)
    shared = _prep_shared(inputs)
    consts = [_consts(0), _consts(1)]
    in_maps = []
    for core in range(8):
        b, half = core // 2, core % 2
        m = dict(shared)
        m.update(consts[half])
        m["xT"] = np.ascontiguousarray(x[b, half * T:(half + 1) * T, :].T)
        in_maps.append(m)
    nc = build_program()
    res = run_bass_kernel_spmd(nc, in_maps, core_ids=list(range(8)))
    out = np.empty_like(x)
    for core in range(8):
        b, half = core // 2, core % 2
        out[b, half * T:(half + 1) * T, :] = np.asarray(res.results[core]["outT"], np.float32).T
    return out
```

```python
import numpy as np
import ml_dtypes
from contextlib import ExitStack
import concourse.bass as bass
import concourse.mybir as mybir
from concourse.bass_utils import run_bass_kernel_spmd

F32 = mybir.dt.float32
BF16 = mybir.dt.bfloat16
AF = mybir.ActivationFunctionType
ALU = mybir.AluOpType
AX = mybir.AxisListType

L = 2
D = 1024
KT = 8
T = 2048
NCH = 4
CW = 512
FF = 2816
NJ = 22
PARTS = [(0, 8), (8, 7), (15, 7)]
EPS = 1e-6
NS = 48
VROW = 1024
NCF = 770
NCB = 448
NEGB = -30000.0
MSTOP = 99
STOP = None

class Tok:
    __slots__ = ("sem", "val", "eng")
    def __init__(self, sem, val, eng):
        self.sem = sem; self.val = val; self.eng = eng


class Buf:
    __slots__ = ("w", "r", "name", "aux")
    def __init__(self, name=""):
        self.w = None; self.r = {}; self.name = name


ENGS = ("pe", "act", "dve", "pool", "sp")


class Prog:
    def __init__(self, esem, rings, ccsem):
        self.esem = esem
        self.rings = rings
        self.ccsem = ccsem
        self.ops = {e: [] for e in ENGS}
        self.cnt = {e: 0 for e in esem}
        self.known = {e: {} for e in ENGS}
        self.dma_tot = {}
        self.sem_of = {}
        for q in rings:
            for s in rings[q]:
                self.dma_tot[id(s)] = 0
                self.sem_of[id(s)] = s
        self.ring_i = {q: 0 for q in rings}
        self.cc_tot = 0

    def _need(self, eng, toks):
        waits = {}
        for t in toks:
            if t is None:
                continue
            if eng == "pe" and t.eng == "pe":
                continue
            k = id(t.sem)
            cur = waits.get(k)
            if cur is None or cur.val < t.val:
                waits[k] = t
        out = []
        kn = self.known[eng]
        for k, t in waits.items():
            if kn.get(k, 0) >= t.val:
                continue
            kn[k] = t.val
            out.append((t.sem, t.val))
        return out

    def _toks(self, reads, writes):
        toks = []
        for b in reads:
            toks.append(b.w)
        for b in writes:
            toks.append(b.w)
            toks.extend(b.r.values())
        return toks

    def _update(self, tok, reads, writes):
        for b in writes:
            b.w = tok; b.r = {}
        for b in reads:
            if b.w is tok:
                continue
            b.r[id(tok.sem)] = tok

    def op(self, eng, fn, reads=(), writes=()):
        waits = self._need(eng, self._toks(reads, writes))
        self.cnt[eng] += 1
        tok = Tok(self.esem[eng], self.cnt[eng], eng)
        self.ops[eng].append((waits, fn, tok.sem, 1))
        self._update(tok, reads, writes)
        return tok

    def dma(self, q, out_ap, in_ap, reads=(), writes=()):
        ring = self.rings[q]
        i = self.ring_i[q]
        self.ring_i[q] = (i + 1) % len(ring)
        sem = ring[i]
        toks = self._toks(reads, writes)
        if self.dma_tot[id(sem)] > 0:
            toks.append(Tok(sem, self.dma_tot[id(sem)], "dma"))
        waits = self._need(q, toks)
        self.dma_tot[id(sem)] += 16
        tok = Tok(sem, self.dma_tot[id(sem)], "dma")
        self.ops[q].append((waits, (lambda e, o=out_ap, s=in_ap: e.dma_start(out=o, in_=s)), sem, 16))
        self._update(tok, reads, writes)
        return tok

    def collective(self, in_ap, out_ap, reads=(), writes=()):
        waits = self._need("pool", self._toks(reads, writes))
        self.cc_tot += 1
        tok = Tok(self.ccsem, self.cc_tot, "cc")

        def fn(e, i=in_ap, o=out_ap):
            return e.collective_compute("AllGather", ALU.bypass,
                                        replica_groups=[[0, 1], [2, 3], [4, 5], [6, 7]],
                                        ins=[i], outs=[o])
        self.ops["pool"].append((waits, fn, self.ccsem, 1))
        self.ops["pool"].append(([(self.ccsem, self.cc_tot)], None, None, 0))
        self.known["pool"][id(self.ccsem)] = self.cc_tot
        self._update(tok, reads, writes)
        return tok

    def barrier(self):
        toks = [Tok(self.esem[e], self.cnt[e], e) for e in self.esem if self.cnt[e] > 0]
        for k, tot in self.dma_tot.items():
            if tot > 0:
                toks.append(Tok(self.sem_of[k], tot, "dma"))
        if self.cc_tot > 0:
            toks.append(Tok(self.ccsem, self.cc_tot, "cc"))
        for e in ENGS:
            waits = {}
            for t in toks:
                if t.eng == e:
                    continue
                waits[id(t.sem)] = t
            out = []
            kn = self.known[e]
            for k, t in waits.items():
                if kn.get(k, 0) >= t.val:
                    continue
                kn[k] = t.val
                out.append((t.sem, t.val))
            if out:
                self.ops[e].append((out, None, None, 0))

    def replay(self, eng, e):
        for waits, fn, sem, amt in self.ops[eng]:
            for (s, v) in waits:
                e.wait_ge(s, v)
            if fn is not None:
                fn(e).then_inc(sem, amt)


class Ring:
    def __init__(self, aps, name, bufs=None):
        self.aps = aps
        self.bufs = bufs if bufs is not None else [Buf(f"{name}{i}") for i in range(len(aps))]
        self.i = 0

    def next(self):
        i = self.i
        self.i = (i + 1) % len(self.aps)
        return self.aps[i], self.bufs[i]


def build_program():
    nc = bass.Bass("TRN2", target_bir_lowering=False)
    dt = nc.dram_tensor
    xT = dt("xT", [D, T], F32, kind="ExternalInput")
    outT = dt("outT", [D, T], F32, kind="ExternalOutput")
    gu_d = dt("gu", [L, 2, NJ, 128, 2048], F32, kind="ExternalInput")
    dn_d = dt("dn", [L, 2, 3, 8, 128, 1024], F32, kind="ExternalInput")
    win_d = dt("win", [L, 32, 128, 1024], F32, kind="ExternalInput")
    wv_d = dt("wv", [L, 2, 128, 2048], F32, kind="ExternalInput")
    wbp_d = dt("wbp", [L, 8, 128, 512], F32, kind="ExternalInput")
    wba_d = dt("wba", [L, 8, 128, 512], F32, kind="ExternalInput")
    wout_d = dt("wout", [L, 8, 128, 1024], F32, kind="ExternalInput")
    pw_d = dt("pw", [L, 128, 512], F32, kind="ExternalInput")
    sm_d = dt("smalls", [128, L * NS], F32, kind="ExternalInput")
    cf_d = dt("cf32", [128, NCF], F32, kind="ExternalInput")
    cb_d = dt("cb16", [128, NCB], BF16, kind="ExternalInput")
    cs_d = dt("cs", [64, 2 * T], F32, kind="ExternalInput")
    E_d = dt("E", [16, 4096], BF16, kind="ExternalInput")
    xk_own_f = dt("xk_own", [4096, 128], F32)
    xk_all_f = dt("xk_all", [8192, 128], F32)
    xv_own_f = [dt(f"xv{i}_own", [4096, 128], F32) for i in range(2)]
    xv_all_f = [dt(f"xv{i}_all", [8192, 128], F32) for i in range(2)]
    xs_own_f = dt("xs_own", [4096, 128], F32)
    xs_all_f = dt("xs_all", [8192, 128], F32)
    xk_own = xk_own_f.bitcast(BF16).reshape([512, T])
    xk_all = xk_all_f.bitcast(BF16).reshape([1024, T])
    xv_own = [t.bitcast(BF16).reshape([1024, VROW]) for t in xv_own_f]
    xv_all = [t.bitcast(BF16).reshape([2048, VROW]) for t in xv_all_f]
    xs_own = xs_own_f
    xs_all = xs_all_f

    with ExitStack() as st:
        def sb(name, shape, dtype):
            return st.enter_context(nc.sbuf_tensor(name, shape, dtype))

        x_sb = sb("x_sb", [128, KT * T], F32)
        h_sb = sb("h_sb", [128, KT * T], BF16)
        R1 = sb("R1", [128, 8 * T], BF16)
        R2 = sb("R2", [128, 8256], F32)
        ktb = [sb(f"ktb{i}", [80, 1024], BF16) for i in range(2)]
        vb = [sb(f"vb{i}", [128, 8, 128], BF16) for i in range(2)]
        stg = [sb(f"stg{i}", [128, 2048], BF16) for i in range(3)]
        pw_sb = sb("pw_sb", [128, 512], BF16)
        tf_t = [sb(f"tf{i}", [128, 528], F32) for i in range(4)]
        pt_t = [sb(f"pt{i}", [128, CW], BF16) for i in range(5)]
        qa_t = [sb(f"qa{i}", [80, CW], BF16) for i in range(2)]
        vst_t = [sb(f"vst{i}", [128, 386], BF16) for i in range(2)]
        kst_t = [sb(f"kst{i}", [64, CW], BF16) for i in range(2)]
        biasm = sb("biasm", [128, 4, 80], BF16)
        gs_t = sb("gs_t", [128, 64], F32)
        sel_t = sb("sel_t", [128, 64], F32)
        top8 = sb("top8", [128, 4, 8], F32)
        xs_sb = sb("xs_sb", [128, 128], F32)
        xsp_sb = sb("xsp_sb", [128, 128], F32)
        kb_all = sb("kb_all", [64, 8, 16], F32)
        sm_sb = sb("sm_sb", [128, L * NS], F32)
        cf_sb = sb("cf_sb", [128, NCF], F32)
        cb_sb = sb("cb_sb", [128, NCB], BF16)
        psb = [st.enter_context(nc.psum_tensor(f"ps{i}", [128, CW], F32)) for i in range(8)]

        sem = lambda n: st.enter_context(nc.semaphore(n))
        esem = {"pe": sem("s_pe"), "act": sem("s_act"), "dve": sem("s_dve")}
        rings = {"sp": [sem(f"s_sp{i}") for i in range(8)],
                 "pool": [sem(f"s_pl{i}") for i in range(8)]}
        ccsem = sem("s_cc")
        P = Prog(esem, rings, ccsem)

        def xv_(kt, c): return x_sb[:, kt * T + c * CW: kt * T + (c + 1) * CW]
        def hv_(kt, c): return h_sb[:, kt * T + c * CW: kt * T + (c + 1) * CW]
        xB = [[Buf() for _ in range(NCH)] for _ in range(KT)]
        hB = [[Buf() for _ in range(NCH)] for _ in range(KT)]
        def hff_(jj, c): return R1[:, jj * T + c * CW: jj * T + (c + 1) * CW]
        hffB = [[Buf() for _ in range(NCH)] for _ in range(8)]
        def yp_(g, c): return R1[:, g * T + c * CW: g * T + (c + 1) * CW]
        ypB = [[Buf() for _ in range(NCH)] for _ in range(4)]
        def ya_(p, c, r0, r1): return R1[r0:r1, (4 + p) * T + c * CW: (4 + p) * T + (c + 1) * CW]
        yaB = [[Buf() for _ in range(NCH)] for _ in range(4)]
        R2b = R2[:, 0:8192].bitcast(BF16)
        def mg_(m, c): return R2b[:, m * T + c * CW: m * T + (c + 1) * CW]
        mgB = [[Buf() for _ in range(NCH)] for _ in range(8)]
        cosB = Buf("cos"); sinB = Buf("sin")
        def cos_(c): return R2[0:64, c * CW:(c + 1) * CW]
        def sin_(c): return R2[0:64, T + c * CW: T + (c + 1) * CW]
        UB0 = 4096
        def ub_(i, a, b): return R2[:, UB0 + i * 2080 + a: UB0 + i * 2080 + b]
        ubB = [Buf("ub0"), Buf("ub1")]

        stgR = Ring([s[:] for s in stg], "stg")
        tfR = Ring([s[:] for s in tf_t], "tf")
        ptR = Ring([s[:] for s in pt_t], "pt")
        sqR = ptR
        qaR = Ring(qa_t, "qa")
        vstR = Ring(vst_t, "vst")
        kstR = Ring(kst_t, "kst")
        ktR = Ring(ktb, "ktb")
        ktEB = [Buf("ktE0"), Buf("ktE1")]
        vbR = Ring(vb, "vb")
        psBufs = [Buf(f"psum{i}") for i in range(8)]
        psS = Ring(psb[0:3], "psS", psBufs[0:3])
        psO = Ring(psb[3:5], "psO", psBufs[3:5])
        psG = Ring(psb[5:8], "psG", psBufs[5:8])
        psA = Ring(psb[0:8], "psA", psBufs[0:8])
        rstdB = tfR.bufs[0]; pwB = Buf("pw")
        rstd_ap = tf_t[0][:, 0:CW]
        smB = Buf("sm"); cfB = Buf("cf"); cbB = Buf("cb")
        biasB = Buf("biasm"); gsB = Buf("gs"); selB = Buf("sel"); top8B = Buf("top8")
        xsB = Buf("xs"); xspB = Buf("xsp"); kbB = Buf("kb_all")
        xkB = [[Buf() for _ in range(NCH)] for _ in range(8)]
        xvB = [[Buf() for _ in range(2)] for _ in range(16)]
        xsoB = Buf("xs_own")
        xkaB = Buf("xk_all"); xvaB = [Buf("xv0_all"), Buf("xv1_all")]; xsaB = Buf("xs_all")

        ident = cb_sb[:, 0:128]
        tri = cb_sb[:, 128:256]
        onesm = cb_sb[:, 256:384]
        ones64 = cb_sb[0:64, 384:448]
        rotT = cf_sb[0:64, 0:64]
        def GM_(c): return cf_sb[:, 64 + c * 64: 64 + (c + 1) * 64]
        def OWN_(c): return cf_sb[:, 320 + c * 64: 320 + (c + 1) * 64]
        def corr_(g): return cf_sb[:, 576 + g * 16: 576 + (g + 1) * 16]
        hasprev = cf_sb[:, 640:641]
        onesf = cf_sb[:, 641:769]
        eps_ap = cf_sb[:, 769:770]
        def smc(l, col, rows=128): return sm_sb[0:rows, l * NS + col: l * NS + col + 1]

        mm_count = [0]

        def mm(out, lhsT, rhs, start, stop, reads, wbuf):
            P.op("pe", lambda e, o=out, a=lhsT, b=rhs, s=start, t=stop:
                 e.matmul(o, a, b, start=s, stop=t), reads=reads, writes=[wbuf])

        def load_w(src_ap, ncols):
            ap, b = stgR.next()
            P.dma("pool", ap[:, 0:ncols], src_ap, reads=[], writes=[b])
            return ap, b

        P.dma("sp", sm_sb[:], sm_d[:, :], writes=[smB])
        P.dma("sp", cf_sb[:], cf_d[:, :], writes=[cfB])
        P.dma("sp", cb_sb[:], cb_d[:, :], writes=[cbB])
        for kt in range(KT):
            P.dma("sp", x_sb[:, kt * T:(kt + 1) * T], xT[kt * 128:(kt + 1) * 128, :], writes=xB[kt])
        P.op("dve", lambda e: e.memset(biasm[:], 0.0), writes=[biasB])
        for i in range(2):
            a, b = vstR.next()
            P.op("dve", lambda e, a=a: e.memset(a[:], 0.0), writes=[b])
            for pl in range(2):
                P.op("dve", lambda e, a=a, pl=pl: e.memset(a[:, pl * 193 + 64: pl * 193 + 66], 1.0), writes=[b])
        P.op("dve", lambda e: e.memset(xs_sb[:], 0.0), writes=[xsB])

        def emit_norm(l, gcol):
            for c in range(NCH):
                pa, pb_ = psG.next()
                for kt in range(KT):
                    sq, sqb = sqR.next()
                    P.op("act", lambda e, o=sq, i=xv_(kt, c): e.activation(out=o, in_=i, func=AF.Square),
                         reads=[xB[kt][c]], writes=[sqb])
                    mm(pa[:], onesm, sq, kt == 0, kt == KT - 1, [sqb, cbB], pb_)
                P.op("act", lambda e, i=pa: e.activation(out=rstd_ap, in_=i[:], func=AF.Ln, bias=eps_ap),
                     reads=[pb_, cfB], writes=[rstdB])
                P.op("act", lambda e: e.activation(out=rstd_ap, in_=rstd_ap, func=AF.Exp, scale=-0.5),
                     reads=[], writes=[rstdB])
                for kt in range(KT):
                    P.op("dve", lambda e, o=hv_(kt, c), i=xv_(kt, c), g=smc(l, gcol + kt):
                         e.scalar_tensor_tensor(o, i, g, rstd_ap, ALU.mult, ALU.mult),
                         reads=[xB[kt][c], rstdB, smB], writes=[hB[kt][c]])

        def emit_ffn(l, which):
            emit_norm(l, 0 if which == 0 else 16)
            for part, (j0, nk) in enumerate(PARTS):
                for jj in range(nk):
                    j = j0 + jj
                    w, wb_ = load_w(gu_d[l, which, j], 2048)
                    for c in range(NCH):
                        pa, pab = psG.next()
                        for kt in range(KT):
                            mm(pa[:], w[:, kt * 256: kt * 256 + 128], hv_(kt, c), kt == 0, kt == KT - 1,
                               [wb_, hB[kt][c]], pab)
                        pb2, pbb = psG.next()
                        for kt in range(KT):
                            mm(pb2[:], w[:, kt * 256 + 128: kt * 256 + 256], hv_(kt, c), kt == 0, kt == KT - 1,
                               [wb_, hB[kt][c]], pbb)
                        tf, tfb = tfR.next()
                        P.op("act", lambda e, o=tf, i=pa: e.activation(out=o[:, 0:CW], in_=i[:], func=AF.Silu),
                             reads=[pab], writes=[tfb])
                        P.op("dve", lambda e, o=hff_(jj, c), a=tf, b=pb2:
                             e.tensor_tensor(o, a[:, 0:CW], b[:], ALU.mult),
                             reads=[tfb, pbb], writes=[hffB[jj][c]])
                for m in range(8):
                    w, wb_ = load_w(dn_d[l, which, part, m][:, 0:nk * 128], nk * 128)
                    for c in range(NCH):
                        po, pob = psG.next()
                        for kk in range(nk):
                            mm(po[:], w[:, kk * 128:(kk + 1) * 128], hff_(kk, c), kk == 0, kk == nk - 1,
                               [wb_, hffB[kk][c]], pob)
                        P.op("dve", lambda e, o=xv_(m, c), i=po:
                             e.scalar_tensor_tensor(o, i[:], 0.5, o, ALU.mult, ALU.add),
                             reads=[pob], writes=[xB[m][c]])

        def normrope_g(l, pin, pinb, gcol, c, sqring, out, psr=None):
            psr = psr or psG
            sq, sqb = sqring.next()
            P.op("act", lambda e, o=sq, i=pin: e.activation(out=o[0:64, :], in_=i[0:64, :], func=AF.Square),
                 reads=[pinb], writes=[sqb])
            yield
            ps_, psb_ = psr.next()
            mm(ps_[0:64, :], ones64, sq[0:64, :], True, True, [sqb, cbB], psb_)
            yield
            t2, t2b = tfR.next()
            P.op("act", lambda e, o=t2, i=ps_: e.activation(out=o[0:64, 0:CW], in_=i[0:64, :], func=AF.Ln, bias=eps_ap[0:64, :]),
                 reads=[psb_, cfB], writes=[t2b])
            P.op("act", lambda e, o=t2: e.activation(out=o[0:64, 0:CW], in_=o[0:64, 0:CW], func=AF.Exp, scale=-0.5),
                 reads=[], writes=[t2b])
            yield
            t3, t3b = tfR.next()
            P.op("dve", lambda e, o=t3, i=pin, r=t2, g=smc(l, gcol, 64):
                 e.scalar_tensor_tensor(o[0:64, 0:CW], i[0:64, :], g, r[0:64, 0:CW], ALU.mult, ALU.mult),
                 reads=[pinb, t2b, smB], writes=[t3b])
            yield
            pr, prb = psr.next()
            mm(pr[0:64, :], rotT, t3[0:64, 0:CW], True, True, [t3b, cfB], prb)
            yield
            P.op("dve", lambda e, o=t2, i=pr, s=sin_(c): e.tensor_tensor(o[0:64, 0:CW], i[0:64, :], s, ALU.mult),
                 reads=[prb, sinB], writes=[t2b])
            P.op("dve", lambda e, o=t3, s=cos_(c): e.tensor_tensor(o[0:64, 0:CW], o[0:64, 0:CW], s, ALU.mult),
                 reads=[cosB], writes=[t3b])
            yield
            P.op("dve", lambda e, o=t3, a=t2: e.tensor_tensor(o[0:64, 0:CW], o[0:64, 0:CW], a[0:64, 0:CW], ALU.add),
                 reads=[t2b], writes=[t3b])
            out["t"] = t3; out["b"] = t3b

        def normrope(l, pin, pinb, gcol, c):
            out = {}
            for _ in normrope_g(l, pin, pinb, gcol, c, sqR, out):
                pass
            return out["t"], out["b"]

        def emit_mixer(l):
            emit_norm(l, 8)
            P.barrier()
            P.dma("sp", R2[0:64, 0:T], cs_d[:, 0:T], writes=[cosB])
            P.dma("sp", R2[0:64, T:2 * T], cs_d[:, T:2 * T], writes=[sinB])
            def kchain_g(head, c, w, wb_):
                hh = head % 2
                pk, pkb = psA.next()
                for kt in range(KT):
                    mm(pk[0:64, :], w[:, kt * 128 + hh * 64: kt * 128 + hh * 64 + 64], hv_(kt, c),
                       kt == 0, kt == KT - 1, [wb_, hB[kt][c]], pkb)
                yield
                o2 = {}
                yield from normrope_g(l, pk, pkb, 45, c, sqR, o2, psA)
                kf, kfb = o2["t"], o2["b"]
                yield
                ks, ksb = kstR.next()
                P.op("act", lambda e, o=ks, i=kf: e.activation(out=o[:], in_=i[0:64, 0:CW], func=AF.Copy),
                     reads=[kfb], writes=[ksb])
                P.dma("sp", xk_own[head * 64:(head + 1) * 64, c * CW:(c + 1) * CW], ks[:],
                      reads=[ksb], writes=[xkB[head][c]])
                P.op("dve", lambda e, i=kf, o=xs_sb[0:64, head * 8 + 2 * c: head * 8 + 2 * c + 2]:
                     e.reduce_sum(o, i[0:64, 0:CW].rearrange("p (a b) -> p a b", a=2), AX.X),
                     reads=[kfb], writes=[xsB])

            for tp in range(4):
                w, wb_ = load_w(win_d[l, 8 + tp], 1024)
                todo = [kchain_g(2 * tp + hh, c, w, wb_) for hh in range(2) for c in range(NCH)]
                active = []
                while todo or active:
                    while todo and len(active) < 2:
                        active.append(todo.pop(0))
                    for g_ in list(active):
                        try:
                            next(g_)
                        except StopIteration:
                            active.remove(g_)
            if MSTOP <= 1:
                P.barrier(); return
            pre_v = [load_w(wv_d[l, vh], 2048) for vh in range(2)]
            P.collective(xk_own_f.ap().opt(), xk_all_f.ap().opt(),
                         reads=[b for r in xkB for b in r], writes=[xkaB])
            for vh in range(2):
                w, wb_ = pre_v[vh]
                for tt in range(16):
                    c, o4 = tt // 4, (tt % 4) * 128
                    pv, pvb = psG.next()
                    for kt in range(KT):
                        mm(pv[:, 0:256], hv_(kt, c)[:, o4:o4 + 128], w[:, kt * 256:(kt + 1) * 256],
                           kt == 0, kt == KT - 1, [wb_, hB[kt][c]], pvb)
                    vs, vsb = vstR.next()
                    for pl in range(2):
                        P.op("act", lambda e, o=vs, i=pv, pl=pl:
                             e.activation(out=o[:, pl * 193: pl * 193 + 64], in_=i[:, pl * 128: pl * 128 + 64], func=AF.Copy),
                             reads=[pvb], writes=[vsb])
                        P.op("act", lambda e, o=vs, i=pv, pl=pl:
                             e.activation(out=o[:, pl * 193 + 129: pl * 193 + 193], in_=i[:, pl * 128 + 64: pl * 128 + 128], func=AF.Copy),
                             reads=[pvb], writes=[vsb])
                    P.dma("sp", xv_own[tt // 8][(tt % 8) * 128:(tt % 8 + 1) * 128, vh * 386:(vh + 1) * 386], vs[:],
                          reads=[vsb], writes=[xvB[tt][vh]])
            if MSTOP <= 2:
                P.barrier(); return
            for g in range(4):
                w, wb_ = load_w(win_d[l, g], 1024)
                pu, pub = psG.next()
                for kt in range(KT):
                    mm(pu[:, 0:16], w[:, kt * 128:(kt + 1) * 128], hv_(kt, 3)[:, CW - 16:CW], kt == 0, kt == KT - 1,
                       [wb_, hB[kt][3]], pub)
                P.op("act", lambda e, i=pu, o=xs_sb[:, 64 + g * 16: 64 + (g + 1) * 16]:
                     e.activation(out=o, in_=i[:, 0:16], func=AF.Copy), reads=[pub], writes=[xsB])
            P.op("dve", lambda e: e.tensor_scalar(xs_sb[0:64, 0:64], xs_sb[0:64, 0:64], 1.0 / 256.0, None, ALU.mult),
                 reads=[], writes=[xsB])
            P.dma("sp", xs_own[0:128, :], xs_sb[:], reads=[xsB], writes=[xsoB])
            if MSTOP <= 3:
                P.barrier(); return
            P.collective(xs_own_f.ap().opt(), xs_all_f.ap().opt(), reads=[xsoB], writes=[xsaB])
            P.dma("sp", xsp_sb[:], xs_all[0:128, :], reads=[xsaB], writes=[xspB])
            pwt, pwb = pw_sb[:], pwB
            P.dma("pool", pwt, pw_d[l], reads=[], writes=[pwB])
            pre_w = [load_w(win_d[l, g], 1024) for g in range(3)]
            for i in range(2):
                P.collective(xv_own_f[i].ap().opt(), xv_all_f[i].ap().opt(),
                             reads=[b for r in xvB[8 * i:8 * i + 8] for b in r], writes=[xvaB[i]])
            P.op("dve", lambda e: e.tensor_copy(kb_all[:, :, 0:8], xsp_sb[0:64, 0:64].rearrange("p (a b) -> p a b", a=8)),
                 reads=[xspB], writes=[kbB])
            P.op("dve", lambda e: e.tensor_copy(kb_all[:, :, 8:16], xs_sb[0:64, 0:64].rearrange("p (a b) -> p a b", a=8)),
                 reads=[xsB], writes=[kbB])
            if MSTOP <= 4:
                P.barrier(); return
            for g in range(4):
                wwin = 2 ** (g + 1)
                w, wb_ = pre_w[g] if g < 3 else load_w(win_d[l, g], 1024)
                ubuf = ubB[g % 2]
                ui = g % 2
                P.op("dve", lambda e, o=ub_(ui, 0, 16), i=xsp_sb[:, 64 + g * 16: 64 + (g + 1) * 16]:
                     e.tensor_scalar(o, i, hasprev, None, ALU.mult), reads=[xspB, cfB], writes=[ubuf])
                for c in range(NCH):
                    pu, pub = psG.next()
                    for kt in range(KT):
                        mm(pu[:], w[:, kt * 128:(kt + 1) * 128], hv_(kt, c), kt == 0, kt == KT - 1,
                           [wb_, hB[kt][c]], pub)
                    P.op("act", lambda e, i=pu, o=ub_(ui, 16 + c * CW, 16 + (c + 1) * CW):
                         e.activation(out=o, in_=i[:], func=AF.Copy), reads=[pub], writes=[ubuf])
                for c in range(NCH):
                    b0 = c * CW
                    cur = (lambda ui_, b0_: (lambda a, b_: ub_(ui_, b0_ + a, b0_ + b_)))(ui, b0)
                    curb = ubuf
                    sh = 1
                    while sh < wwin:
                        nt, ntb = tfR.next()
                        P.op("dve", lambda e, o=nt, hi=cur(sh, 528), lo=cur(0, 528 - sh), sh=sh:
                             e.tensor_tensor(o[:, sh:528], hi, lo, ALU.add), reads=[curb], writes=[ntb])
                        cur = (lambda nt_: (lambda a, b_: nt_[:, a:b_]))(nt)
                        curb = ntb
                        sh *= 2
                    if c == 0:
                        P.op("dve", lambda e, o=cur(16, 32), cr=corr_(g): e.tensor_tensor(o, o, cr, ALU.mult),
                             reads=[cfB], writes=[curb])
                    dtile, db = ptR.next()
                    P.op("dve", lambda e, o=dtile, s=cur(16, 528), u=ub_(ui, 16 + c * CW, 16 + (c + 1) * CW), sc=1.0 / wwin:
                         e.scalar_tensor_tensor(o, s, sc, u, ALU.mult, ALU.subtract),
                         reads=[curb, ubuf], writes=[db])
                    py, pyb = psG.next()
                    mm(py[:], pwt[:, g * 128:(g + 1) * 128], dtile, True, True, [pwb, db], pyb)
                    P.op("act", lambda e, i=py, o=yp_(g, c), s=smc(l, 40 + g):
                         e.activation(out=o, in_=i[:], func=AF.Copy, scale=s), reads=[pyb, smB], writes=[ypB[g][c]])
            if MSTOP <= 5:
                P.barrier(); return
            wqs = {}

            def prologue_g(c, head, out):
                pair, hh = head // 2, head % 2
                if hh == 0:
                    wqs["w"] = load_w(win_d[l, 4 + pair], 1024)
                wq, wqb = wqs["w"]
                pq, pqb = psG.next()
                for kt in range(KT):
                    mm(pq[0:64, :], wq[:, kt * 128 + hh * 64: kt * 128 + hh * 64 + 64], hv_(kt, c),
                       kt == 0, kt == KT - 1, [wqb, hB[kt][c]], pqb)
                yield
                o2 = {}
                yield from normrope_g(l, pq, pqb, 44, c, kstR, o2)
                qf, qfb = o2["t"], o2["b"]
                yield
                qa, qab = qaR.next()
                P.op("act", lambda e, o=qa, i=qf: e.activation(out=o[0:64, :], in_=i[0:64, 0:CW], func=AF.Copy),
                     reads=[qfb], writes=[qab])
                pg, pgb = psG.next()
                for qt in range(4):
                    mm(pg[:, qt * 16:(qt + 1) * 16], qf[0:64, qt * 128:(qt + 1) * 128], kb_all[:, head, :],
                       True, True, [qfb, kbB], pgb)
                yield
                P.op("dve", lambda e, i=pg, g=GM_(c): e.tensor_tensor(gs_t[:], i[:, 0:64], g, ALU.add),
                     reads=[pgb, cfB], writes=[gsB])
                for qt in range(4):
                    P.op("dve", lambda e, qt=qt: e.max(top8[:, qt, :], gs_t[:, qt * 16:(qt + 1) * 16]),
                         reads=[gsB], writes=[top8B])
                yield
                for qt in range(4):
                    P.op("dve", lambda e, qt=qt: e.tensor_scalar(sel_t[:, qt * 16:(qt + 1) * 16],
                                                                gs_t[:, qt * 16:(qt + 1) * 16],
                                                                top8[:, qt, 2:3], None, ALU.is_ge),
                         reads=[gsB, top8B], writes=[selB])
                P.op("dve", lambda e: e.scalar_tensor_tensor(sel_t[:], gs_t[:], -1e29, sel_t[:], ALU.is_gt, ALU.mult),
                     reads=[gsB], writes=[selB])
                P.op("dve", lambda e, o=OWN_(c): e.tensor_tensor(sel_t[:], sel_t[:], o, ALU.max),
                     reads=[cfB], writes=[selB])
                P.op("dve", lambda e: e.tensor_scalar(biasm[:, :, 64:80], sel_t[:].rearrange("p (a b) -> p a b", a=4),
                                                      -1.0, -NEGB, ALU.add, ALU.mult),
                     reads=[selB], writes=[biasB])
                yield
                pb_, pbb = psG.next()
                for qt in range(4):
                    mm(pb_[0:80, qt * 128:(qt + 1) * 128], biasm[:, qt, :], ident, True, True, [biasB, cbB], pbb)
                yield
                P.op("act", lambda e, o=qa, i=pb_: e.activation(out=o[64:80, :], in_=i[64:80, :], func=AF.Copy),
                     reads=[pbb], writes=[qab])
                out["qa"] = qa; out["qab"] = qab

            def attend(c, head, qa, qab, nxt):
                    pair, hh = head // 2, head % 2
                    pieces = []
                    for pc in range(2):
                        pieces.append((xk_all, xv_all[pc], xkaB, xvaB[pc], pc * 1024, pc * 1024, 8, None))
                    n_own = 4 * (c + 1)
                    for op_ in range(2):
                        nt_ = min(8, n_own - op_ * 8)
                        if nt_ > 0:
                            pieces.append((xk_own, xv_own[op_], None, None, op_ * 1024, T + op_ * 1024, nt_, op_ * 8))
                    po, pob = psO.next()
                    VW = 65 if hh == 0 else 128
                    vc0 = pair * 193 + (0 if hh == 0 else 65)
                    total_tiles = sum(p[6] for p in pieces)
                    tiles = []
                    first = {}
                    for pi, pcs in enumerate(pieces):
                        own0, ntl = pcs[7], pcs[6]
                        for i in range(ntl):
                            q0, diag = 0, False
                            if own0 is not None and own0 + i >= 4 * c:
                                q0, diag = (own0 + i - 4 * c) * 128, True
                            tiles.append((pi, i, q0, diag))
                    kslot, vslot = {}, {}

                    def load_k(pi, head=head):
                        sk, sv, skb, svb, k0, s0, ntl, own0 = pieces[pi]
                        nk = ntl * 128
                        ktEbuf = ktEB[ktR.i]
                        kt_t, ktbuf = ktR.next()
                        kreads = [xkB[head][cc] for cc in range(NCH)] if skb is None else [skb]
                        P.dma("sp", kt_t[0:64, 0:nk], sk[head * 64:(head + 1) * 64, k0:k0 + nk], reads=kreads, writes=[ktbuf])
                        P.dma("sp", kt_t[64:80, 0:nk], E_d[:, s0:s0 + nk], reads=[], writes=[ktEbuf])
                        kslot[pi] = (kt_t, ktbuf, ktEbuf)

                    def load_v(pi, pair=pair, VW=VW, vc0=vc0):
                        sk, sv, skb, svb, k0, s0, ntl, own0 = pieces[pi]
                        nk = ntl * 128
                        vb_t, vbbuf = vbR.next()
                        vreads = [xvB[tt][pair // 2] for tt in range(k0 // 128, k0 // 128 + ntl)] if svb is None else [svb]
                        P.dma("sp", vb_t[:, 0:ntl, 0:VW],
                              sv[0:nk, vc0:vc0 + VW].rearrange("(kt p) c -> p kt c", p=128),
                              reads=vreads, writes=[vbbuf])
                        vslot[pi] = (vb_t, vbbuf)

                    for pi in range(min(2, len(pieces))):
                        load_k(pi); load_v(pi)
                    LA = 3
                    stash = {}
                    n_t = len(tiles)
                    for t in range(n_t + LA):
                        if nxt is not None and t >= 3:
                            next(nxt, None)
                        if t < n_t:
                            pi, i, q0, diag = tiles[t]
                            if i == 0 and pi >= 1 and pi + 1 < len(pieces):
                                load_k(pi + 1)
                            kt_t, ktbuf, ktEbuf = kslot[pi]
                            pss, pssb = psS.next()
                            mm(pss[:, q0:CW], kt_t[0:80, i * 128:(i + 1) * 128], qa[0:80, q0:CW], True, True,
                               [ktbuf, ktEbuf, qab], pssb)
                            pt, ptb = ptR.next()
                            P.op("act", lambda e, o=pt, i_=pss, q0=q0:
                                 e.activation(out=o[:, q0:CW], in_=i_[:, q0:CW], func=AF.Exp, scale=0.125),
                                 reads=[pssb], writes=[ptb])
                            if diag:
                                P.op("dve", lambda e, o=pt, q0=q0:
                                     e.tensor_tensor(o[:, q0:q0 + 128], o[:, q0:q0 + 128], tri, ALU.mult),
                                     reads=[cbB], writes=[ptb])
                            stash[t] = (pt, ptb)
                        tp = t - LA
                        if tp >= 0:
                            pi, i, q0, diag = tiles[tp]
                            if i == 0 and pi >= 1 and pi + 1 < len(pieces):
                                load_v(pi + 1)
                            vb_t, vbbuf = vslot[pi]
                            pt, ptb = stash.pop(tp)
                            P.op("pe", lambda e, o=po, v=vb_t, i=i, VW=VW, p_=pt, q0=q0, s=(tp == 0), t_=(tp == n_t - 1):
                                 e.matmul(o[0:VW, q0:CW], v[:, i, 0:VW], p_[:, q0:CW], start=s, stop=t_, skip_group_check=True),
                                 reads=[vbbuf, ptb], writes=[pob])
                    dr = 64 if hh == 0 else 0
                    r0, r1 = (0, 64) if hh == 0 else (64, 128)
                    if nxt is not None:
                        for _ in nxt:
                            pass
                    rd, rdb = tfR.next()
                    P.op("act", lambda e, o=rd, i=po, dr=dr: e.activation(out=o[dr:dr + 1, 0:CW], in_=i[dr:dr + 1, :], func=AF.Ln),
                         reads=[pob], writes=[rdb])
                    P.op("act", lambda e, o=rd, dr=dr: e.activation(out=o[dr:dr + 1, 0:CW], in_=o[dr:dr + 1, 0:CW], func=AF.Exp, scale=-1.0),
                         reads=[], writes=[rdb])
                    pbc, pbcb = psG.next()
                    mm(pbc[0:r1, :], onesf[dr:dr + 1, 0:r1], rd[dr:dr + 1, 0:CW], True, True, [rdb, cfB], pbcb)
                    on, onb = tfR.next()
                    P.op("act", lambda e, o=on, i=po, r0=r0, r1=r1: e.activation(out=o[r0:r1, 0:CW], in_=i[r0:r1, :], func=AF.Copy),
                         reads=[pob], writes=[onb])
                    P.op("dve", lambda e, o=ya_(pair, c, r0, r1), a=on, b=pbc, r0=r0, r1=r1:
                         e.tensor_tensor(o, a[r0:r1, 0:CW], b[r0:r1, :], ALU.mult),
                         reads=[onb, pbcb], writes=[yaB[pair][c]])

            order = [(c, head) for c in range(NCH) for head in range(8)]
            st_cur = {}
            for _ in prologue_g(order[0][0], order[0][1], st_cur):
                pass
            for idx, (c, head) in enumerate(order):
                st_nxt = {}
                nxt = prologue_g(order[idx + 1][0], order[idx + 1][1], st_nxt) if idx + 1 < len(order) else None
                attend(c, head, st_cur["qa"], st_cur["qab"], nxt)
                st_cur = st_nxt
            if MSTOP <= 6:
                P.barrier(); return
            P.barrier()
            for m in range(8):
                w0, w0b = load_w(win_d[l, 16 + m], 1024)
                w1, w1b = load_w(win_d[l, 24 + m], 1024)
                wpa, wpab = stgR.next()
                P.dma("pool", wpa[:, 0:512], wbp_d[l, m], reads=[], writes=[wpab])
                P.dma("pool", wpa[:, 512:1024], wba_d[l, m], reads=[], writes=[wpab])
                for c in range(NCH):
                    g0, g0b = psG.next()
                    for kt in range(KT):
                        mm(g0[:], w0[:, kt * 128:(kt + 1) * 128], hv_(kt, c), kt == 0, kt == KT - 1, [w0b, hB[kt][c]], g0b)
                    ta, tab = tfR.next()
                    P.op("act", lambda e, o=ta, i=g0, b=smc(l, 24 + m): e.activation(out=o[:, 0:CW], in_=i[:], func=AF.Sigmoid, bias=b),
                         reads=[g0b, smB], writes=[tab])
                    g1, g1b = psG.next()
                    for kt in range(KT):
                        mm(g1[:], w1[:, kt * 128:(kt + 1) * 128], hv_(kt, c), kt == 0, kt == KT - 1, [w1b, hB[kt][c]], g1b)
                    tb, tbb = tfR.next()
                    P.op("act", lambda e, o=tb, i=g1, b=smc(l, 32 + m): e.activation(out=o[:, 0:CW], in_=i[:], func=AF.Sigmoid, bias=b),
                         reads=[g1b, smB], writes=[tbb])
                    bp, bpb = psG.next()
                    for kk in range(4):
                        mm(bp[:], wpa[:, kk * 128:(kk + 1) * 128], yp_(kk, c), kk == 0, kk == 3, [wpab, ypB[kk][c]], bpb)
                    P.op("dve", lambda e, a=ta, i=bp: e.tensor_tensor(a[:, 0:CW], a[:, 0:CW], i[:], ALU.mult),
                         reads=[bpb], writes=[tab])
                    ba, bab = psG.next()
                    for kk in range(4):
                        mm(ba[:], wpa[:, 512 + kk * 128: 512 + (kk + 1) * 128], ya_(kk, c, 0, 128), kk == 0, kk == 3, [wpab, yaB[kk][c]], bab)
                    P.op("dve", lambda e, a=tb, i=ba: e.tensor_tensor(a[:, 0:CW], a[:, 0:CW], i[:], ALU.mult),
                         reads=[bab], writes=[tbb])
                    P.op("dve", lambda e, o=mg_(m, c), a=ta, b=tb: e.tensor_tensor(o, a[:, 0:CW], b[:, 0:CW], ALU.add),
                         reads=[tab, tbb], writes=[mgB[m][c]])
            for m2 in range(8):
                w, wb_ = load_w(wout_d[l, m2], 1024)
                for c in range(NCH):
                    po, pob = psG.next()
                    for m in range(8):
                        mm(po[:], w[:, m * 128:(m + 1) * 128], mg_(m, c), m == 0, m == 7, [wb_, mgB[m][c]], pob)
                    P.op("dve", lambda e, o=xv_(m2, c), i=po: e.tensor_tensor(o, i[:], o, ALU.add),
                         reads=[pob], writes=[xB[m2][c]])
            P.barrier()

        ph = 0
        for l in range(L):
            for f in (lambda: emit_ffn(l, 0), lambda: emit_mixer(l), lambda: emit_ffn(l, 1)):
                if STOP is None or ph < STOP:
                    f()
                ph += 1
            P.barrier()
        for kt in range(KT):
            P.dma("sp", outT[kt * 128:(kt + 1) * 128, :], x_sb[:, kt * T:(kt + 1) * T], reads=xB[kt], writes=[])
        P.barrier()

        with nc.Block() as block:
            @block.tensor
            def _(e):
                P.replay("pe", e)

            @block.scalar
            def _(e):
                P.replay("act", e)

            @block.vector
            def _(e):
                P.replay("dve", e)

            @block.gpsimd
            def _(e):
                P.replay("pool", e)

            @block.sync
            def _(e):
                P.replay("sp", e)
    return nc


def _prep_shared(inp):
    f = np.float32
    gu = np.empty((L, 2, NJ, 128, 2048), f)
    dn = np.zeros((L, 2, 3, 8, 128, 1024), f)
    for wi, (ngu, ndn) in enumerate((("ffn1_w_gate_up", "ffn1_w_down"), ("ffn2_w_gate_up", "ffn2_w_down"))):
        W = np.asarray(inp[ngu], f)
        A = W[:, :, :FF].reshape(L, KT, 128, NJ, 128)
        B = W[:, :, FF:].reshape(L, KT, 128, NJ, 128)
        t = np.stack([A, B], axis=4)
        gu[:, wi] = t.transpose(0, 3, 2, 1, 4, 5).reshape(L, NJ, 128, 2048)
        Wd = np.asarray(inp[ndn], f).reshape(L, NJ, 128, 8, 128)
        for part, (j0, nk) in enumerate(PARTS):
            t = Wd[:, j0:j0 + nk].transpose(0, 3, 2, 1, 4)
            dn[:, wi, part, :, :, :nk * 128] = t.reshape(L, 8, 128, nk * 128)
    Win = np.asarray(inp["w_in"], f)
    win = Win.reshape(L, KT, 128, 32, 128).transpose(0, 3, 2, 1, 4).reshape(L, 32, 128, 1024)
    Wv = Win[:, :, 1536:2048].reshape(L, KT, 128, 2, 256)
    wv = Wv.transpose(0, 3, 2, 1, 4).reshape(L, 2, 128, 2048)
    def br(name):
        W = np.asarray(inp[name], f).reshape(L, 4, 128, 8, 128)
        return W.transpose(0, 3, 2, 1, 4).reshape(L, 8, 128, 512)
    wout = np.asarray(inp["w_out"], f).reshape(L, KT, 128, 8, 128).transpose(0, 3, 2, 1, 4).reshape(L, 8, 128, 1024)
    pw = np.asarray(inp["pool_w"], f).transpose(0, 2, 1, 3).reshape(L, 128, 512)
    sm = np.zeros((128, L * NS), f)
    for l in range(L):
        o = l * NS
        sm[:, o + 0:o + 8] = np.asarray(inp["ffn1_norm"], f)[l].reshape(8, 128).T
        sm[:, o + 8:o + 16] = np.asarray(inp["mix_norm"], f)[l].reshape(8, 128).T
        sm[:, o + 16:o + 24] = np.asarray(inp["ffn2_norm"], f)[l].reshape(8, 128).T
        sm[:, o + 24:o + 40] = np.asarray(inp["b_gate"], f)[l].reshape(16, 128).T
        sm[:, o + 40:o + 44] = np.asarray(inp["pool_scale"], f)[l].reshape(4, 128).T
        sm[0:64, o + 44] = np.asarray(inp["q_norm"], f)[l]
        sm[0:64, o + 45] = np.asarray(inp["k_norm"], f)[l]
    return dict(gu=np.ascontiguousarray(gu), dn=dn, win=np.ascontiguousarray(win), wv=np.ascontiguousarray(wv),
                wbp=np.ascontiguousarray(br("w_branch_pool")), wba=np.ascontiguousarray(br("w_branch_attn")),
                wout=np.ascontiguousarray(wout), pw=np.ascontiguousarray(pw), smalls=sm)


def _consts(half):
    f = np.float32
    cf = np.zeros((128, NCF), f)
    for m in range(64):
        if m < 32:
            cf[m + 32, m] = -1.0
        else:
            cf[m - 32, m] = 1.0
    GM = np.full((16, 16), -1e30, f)
    OWN = np.zeros((16, 16), f)
    for qt in range(16):
        sbq = 8 + qt // 2
        lo = 0 if half == 1 else 8
        GM[qt, lo:sbq] = 0.0
        OWN[qt, sbq] = 1.0
    cf[:, 64:320] = GM.reshape(1, 256)
    cf[:, 320:576] = OWN.reshape(1, 256)
    corr = np.ones((4, 16), f)
    if half == 0:
        for g in range(4):
            w = 2 ** (g + 1)
            for t in range(16):
                corr[g, t] = w / min(t + 1, w)
    cf[:, 576:640] = corr.reshape(1, 64)
    cf[:, 640] = float(half)
    cf[:, 641:769] = 1.0
    cf[:, 769] = EPS
    cb = np.zeros((128, NCB), f)
    cb[:, 0:128] = np.eye(128, dtype=f)
    cb[:, 128:256] = np.triu(np.ones((128, 128), f))
    cb[:, 256:384] = 1.0 / 1024.0
    cb[0:64, 384:448] = 1.0 / 64.0
    hd = 32
    inv_freq = (1.0 / (np.float32(10000.0) ** (np.arange(hd, dtype=f) * f(2.0 / 64)))).astype(f)
    pos = (np.arange(T, dtype=f) + f(half * T)).astype(f)
    ang = (pos[:, None] * inv_freq[None, :]).astype(f)
    cosv = np.cos(ang).astype(f).T
    sinv = np.sin(ang).astype(f).T
    cs = np.zeros((64, 2 * T), f)
    cs[0:32, 0:T] = cosv; cs[32:64, 0:T] = cosv
    cs[0:32, T:] = sinv; cs[32:64, T:] = sinv
    E = np.zeros((16, 4096), f)
    for j in range(16):
        E[j, j * 256:(j + 1) * 256] = 1.0
    return dict(cf32=cf, cb16=cb.astype(ml_dtypes.bfloat16), cs=cs, E=E.astype(ml_dtypes.bfloat16))


def kernel(**inputs):
    x = np.asarray(inputs["x"], np.float32)
    shared = _prep_shared(inputs)
    consts = [_consts(0), _consts(1)]
    in_maps = []
    for core in range(8):
        b, half = core // 2, core % 2
        m = dict(shared)
        m.update(consts[half])
        m["xT"] = np.ascontiguousarray(x[b, half * T:(half + 1) * T, :].T)
        in_maps.append(m)
    nc = build_program()
    res = run_bass_kernel_spmd(nc, in_maps, core_ids=list(range(8)))
    out = np.empty_like(x)
    for core in range(8):
        b, half = core // 2, core % 2
        out[b, half * T:(half + 1) * T, :] = np.asarray(res.results[core]["outT"], np.float32).T
    return out
```

```python
import numpy as np
import ml_dtypes
from contextlib import ExitStack
import concourse.bass as bass
import concourse.mybir as mybir
from concourse.bass_utils import run_bass_kernel_spmd

F32 = mybir.dt.float32
BF16 = mybir.dt.bfloat16
AF = mybir.ActivationFunctionType
ALU = mybir.AluOpType
AX = mybir.AxisListType

L = 2
D = 1024
KT = 8
T = 2048
NCH = 4
CW = 512
FF = 2816
NJ = 22
PARTS = [(0, 8), (8, 7), (15, 7)]
EPS = 1e-6
NS = 48
VROW = 1024
NCF = 770
NCB = 448
NEGB = -30000.0
MSTOP = 99
STOP = None

class Tok:
    __slots__ = ("sem", "val", "eng")
    def __init__(self, sem, val, eng):
        self.sem = sem; self.val = val; self.eng = eng


class Buf:
    __slots__ = ("w", "r", "name", "aux")
    def __init__(self, name=""):
        self.w = None; self.r = {}; self.name = name


ENGS = ("pe", "act", "dve", "pool", "sp")


class Prog:
    def __init__(self, esem, rings, ccsem):
        self.esem = esem
        self.rings = rings
        self.ccsem = ccsem
        self.ops = {e: [] for e in ENGS}
        self.cnt = {e: 0 for e in esem}
        self.known = {e: {} for e in ENGS}
        self.dma_tot = {}
        self.sem_of = {}
        for q in rings:
            for s in rings[q]:
                self.dma_tot[id(s)] = 0
                self.sem_of[id(s)] = s
        self.ring_i = {q: 0 for q in rings}
        self.cc_tot = 0

    def _need(self, eng, toks):
        waits = {}
        for t in toks:
            if t is None:
                continue
            if eng == "pe" and t.eng == "pe":
                continue
            k = id(t.sem)
            cur = waits.get(k)
            if cur is None or cur.val < t.val:
                waits[k] = t
        out = []
        kn = self.known[eng]
        for k, t in waits.items():
            if kn.get(k, 0) >= t.val:
                continue
            kn[k] = t.val
            out.append((t.sem, t.val))
        return out

    def _toks(self, reads, writes):
        toks = []
        for b in reads:
            toks.append(b.w)
        for b in writes:
            toks.append(b.w)
            toks.extend(b.r.values())
        return toks

    def _update(self, tok, reads, writes):
        for b in writes:
            b.w = tok; b.r = {}
        for b in reads:
            if b.w is tok:
                continue
            b.r[id(tok.sem)] = tok

    def op(self, eng, fn, reads=(), writes=()):
        waits = self._need(eng, self._toks(reads, writes))
        self.cnt[eng] += 1
        tok = Tok(self.esem[eng], self.cnt[eng], eng)
        self.ops[eng].append((waits, fn, tok.sem, 1))
        self._update(tok, reads, writes)
        return tok

    def dma(self, q, out_ap, in_ap, reads=(), writes=()):
        ring = self.rings[q]
        i = self.ring_i[q]
        self.ring_i[q] = (i + 1) % len(ring)
        sem = ring[i]
        toks = self._toks(reads, writes)
        if self.dma_tot[id(sem)] > 0:
            toks.append(Tok(sem, self.dma_tot[id(sem)], "dma"))
        waits = self._need(q, toks)
        self.dma_tot[id(sem)] += 16
        tok = Tok(sem, self.dma_tot[id(sem)], "dma")
        self.ops[q].append((waits, (lambda e, o=out_ap, s=in_ap: e.dma_start(out=o, in_=s)), sem, 16))
        self._update(tok, reads, writes)
        return tok

    def collective(self, in_ap, out_ap, reads=(), writes=()):
        waits = self._need("pool", self._toks(reads, writes))
        self.cc_tot += 1
        tok = Tok(self.ccsem, self.cc_tot, "cc")

        def fn(e, i=in_ap, o=out_ap):
            return e.collective_compute("AllGather", ALU.bypass,
                                        replica_groups=[[0, 1], [2, 3], [4, 5], [6, 7]],
                                        ins=[i], outs=[o])
        self.ops["pool"].append((waits, fn, self.ccsem, 1))
        self.ops["pool"].append(([(self.ccsem, self.cc_tot)], None, None, 0))
        self.known["pool"][id(self.ccsem)] = self.cc_tot
        self._update(tok, reads, writes)
        return tok

    def barrier(self):
        toks = [Tok(self.esem[e], self.cnt[e], e) for e in self.esem if self.cnt[e] > 0]
        for k, tot in self.dma_tot.items():
            if tot > 0:
                toks.append(Tok(self.sem_of[k], tot, "dma"))
        if self.cc_tot > 0:
            toks.append(Tok(self.ccsem, self.cc_tot, "cc"))
        for e in ENGS:
            waits = {}
            for t in toks:
                if t.eng == e:
                    continue
                waits[id(t.sem)] = t
            out = []
            kn = self.known[e]
            for k, t in waits.items():
                if kn.get(k, 0) >= t.val:
                    continue
                kn[k] = t.val
                out.append((t.sem, t.val))
            if out:
                self.ops[e].append((out, None, None, 0))

    def replay(self, eng, e):
        for waits, fn, sem, amt in self.ops[eng]:
            for (s, v) in waits:
                e.wait_ge(s, v)
            if fn is not None:
                fn(e).then_inc(sem, amt)


class Ring:
    def __init__(self, aps, name, bufs=None):
        self.aps = aps
        self.bufs = bufs if bufs is not None else [Buf(f"{name}{i}") for i in range(len(aps))]
        self.i = 0

    def next(self):
        i = self.i
        self.i = (i + 1) % len(self.aps)
        return self.aps[i], self.bufs[i]


def build_program():
    nc = bass.Bass("TRN2", target_bir_lowering=False)
    dt = nc.dram_tensor
    xT = dt("xT", [D, T], F32, kind="ExternalInput")
    outT = dt("outT", [D, T], F32, kind="ExternalOutput")
    gu_d = dt("gu", [L, 2, NJ, 128, 2048], F32, kind="ExternalInput")
    dn_d = dt("dn", [L, 2, 3, 8, 128, 1024], F32, kind="ExternalInput")
    win_d = dt("win", [L, 32, 128, 1024], F32, kind="ExternalInput")
    wv_d = dt("wv", [L, 2, 128, 2048], F32, kind="ExternalInput")
    wbp_d = dt("wbp", [L, 8, 128, 512], F32, kind="ExternalInput")
    wba_d = dt("wba", [L, 8, 128, 512], F32, kind="ExternalInput")
    wout_d = dt("wout", [L, 8, 128, 1024], F32, kind="ExternalInput")
    pw_d = dt("pw", [L, 128, 512], F32, kind="ExternalInput")
    sm_d = dt("smalls", [128, L * NS], F32, kind="ExternalInput")
    cf_d = dt("cf32", [128, NCF], F32, kind="ExternalInput")
    cb_d = dt("cb16", [128, NCB], BF16, kind="ExternalInput")
    cs_d = dt("cs", [64, 2 * T], F32, kind="ExternalInput")
    E_d = dt("E", [16, 4096], BF16, kind="ExternalInput")
    xk_own_f = dt("xk_own", [4096, 128], F32)
    xk_all_f = dt("xk_all", [8192, 128], F32)
    xv_own_f = [dt(f"xv{i}_own", [4096, 128], F32) for i in range(2)]
    xv_all_f = [dt(f"xv{i}_all", [8192, 128], F32) for i in range(2)]
    xs_own_f = dt("xs_own", [4096, 128], F32)
    xs_all_f = dt("xs_all", [8192, 128], F32)
    xk_own = xk_own_f.bitcast(BF16).reshape([512, T])
    xk_all = xk_all_f.bitcast(BF16).reshape([1024, T])
    xv_own = [t.bitcast(BF16).reshape([1024, VROW]) for t in xv_own_f]
    xv_all = [t.bitcast(BF16).reshape([2048, VROW]) for t in xv_all_f]
    xs_own = xs_own_f
    xs_all = xs_all_f

    with ExitStack() as st:
        def sb(name, shape, dtype):
            return st.enter_context(nc.sbuf_tensor(name, shape, dtype))

        x_sb = sb("x_sb", [128, KT * T], F32)
        h_sb = sb("h_sb", [128, KT * T], BF16)
        R1 = sb("R1", [128, 8 * T], BF16)
        R2 = sb("R2", [128, 8256], F32)
        ktb = [sb(f"ktb{i}", [80, 1024], BF16) for i in range(2)]
        vb = [sb(f"vb{i}", [128, 8, 128], BF16) for i in range(2)]
        stg = [sb(f"stg{i}", [128, 2048], BF16) for i in range(3)]
        pw_sb = sb("pw_sb", [128, 512], BF16)
        tf_t = [sb(f"tf{i}", [128, 528], F32) for i in range(4)]
        pt_t = [sb(f"pt{i}", [128, CW], BF16) for i in range(5)]
        qa_t = [sb(f"qa{i}", [80, CW], BF16) for i in range(2)]
        vst_t = [sb(f"vst{i}", [128, 386], BF16) for i in range(2)]
        kst_t = [sb(f"kst{i}", [64, CW], BF16) for i in range(2)]
        biasm = sb("biasm", [128, 4, 80], BF16)
        gs_t = sb("gs_t", [128, 64], F32)
        sel_t = sb("sel_t", [128, 64], F32)
        top8 = sb("top8", [128, 4, 8], F32)
        xs_sb = sb("xs_sb", [128, 128], F32)
        xsp_sb = sb("xsp_sb", [128, 128], F32)
        kb_all = sb("kb_all", [64, 8, 16], F32)
        sm_sb = sb("sm_sb", [128, L * NS], F32)
        cf_sb = sb("cf_sb", [128, NCF], F32)
        cb_sb = sb("cb_sb", [128, NCB], BF16)
        psb = [st.enter_context(nc.psum_tensor(f"ps{i}", [128, CW], F32)) for i in range(8)]

        sem = lambda n: st.enter_context(nc.semaphore(n))
        esem = {"pe": sem("s_pe"), "act": sem("s_act"), "dve": sem("s_dve")}
        rings = {"sp": [sem(f"s_sp{i}") for i in range(8)],
                 "pool": [sem(f"s_pl{i}") for i in range(8)]}
        ccsem = sem("s_cc")
        P = Prog(esem, rings, ccsem)

        def xv_(kt, c): return x_sb[:, kt * T + c * CW: kt * T + (c + 1) * CW]
        def hv_(kt, c): return h_sb[:, kt * T + c * CW: kt * T + (c + 1) * CW]
        xB = [[Buf() for _ in range(NCH)] for _ in range(KT)]
        hB = [[Buf() for _ in range(NCH)] for _ in range(KT)]
        def hff_(jj, c): return R1[:, jj * T + c * CW: jj * T + (c + 1) * CW]
        hffB = [[Buf() for _ in range(NCH)] for _ in range(8)]
        def yp_(g, c): return R1[:, g * T + c * CW: g * T + (c + 1) * CW]
        ypB = [[Buf() for _ in range(NCH)] for _ in range(4)]
        def ya_(p, c, r0, r1): return R1[r0:r1, (4 + p) * T + c * CW: (4 + p) * T + (c + 1) * CW]
        yaB = [[Buf() for _ in range(NCH)] for _ in range(4)]
        R2b = R2[:, 0:8192].bitcast(BF16)
        def mg_(m, c): return R2b[:, m * T + c * CW: m * T + (c + 1) * CW]
        mgB = [[Buf() for _ in range(NCH)] for _ in range(8)]
        cosB = Buf("cos"); sinB = Buf("sin")
        def cos_(c): return R2[0:64, c * CW:(c + 1) * CW]
        def sin_(c): return R2[0:64, T + c * CW: T + (c + 1) * CW]
        UB0 = 4096
        def ub_(i, a, b): return R2[:, UB0 + i * 2080 + a: UB0 + i * 2080 + b]
        ubB = [Buf("ub0"), Buf("ub1")]

        stgR = Ring([s[:] for s in stg], "stg")
        tfR = Ring([s[:] for s in tf_t], "tf")
        ptR = Ring([s[:] for s in pt_t], "pt")
        sqR = ptR
        qaR = Ring(qa_t, "qa")
        vstR = Ring(vst_t, "vst")
        kstR = Ring(kst_t, "kst")
        ktR = Ring(ktb, "ktb")
        ktEB = [Buf("ktE0"), Buf("ktE1")]
        vbR = Ring(vb, "vb")
        psBufs = [Buf(f"psum{i}") for i in range(8)]
        psS = Ring(psb[0:3], "psS", psBufs[0:3])
        psO = Ring(psb[3:5], "psO", psBufs[3:5])
        psG = Ring(psb[5:8], "psG", psBufs[5:8])
        psA = Ring(psb[0:8], "psA", psBufs[0:8])
        rstdB = tfR.bufs[0]; pwB = Buf("pw")
        rstd_ap = tf_t[0][:, 0:CW]
        smB = Buf("sm"); cfB = Buf("cf"); cbB = Buf("cb")
        biasB = Buf("biasm"); gsB = Buf("gs"); selB = Buf("sel"); top8B = Buf("top8")
        xsB = Buf("xs"); xspB = Buf("xsp"); kbB = Buf("kb_all")
        xkB = [[Buf() for _ in range(NCH)] for _ in range(8)]
        xvB = [[Buf() for _ in range(2)] for _ in range(16)]
        xsoB = Buf("xs_own")
        xkaB = Buf("xk_all"); xvaB = [Buf("xv0_all"), Buf("xv1_all")]; xsaB = Buf("xs_all")

        ident = cb_sb[:, 0:128]
        tri = cb_sb[:, 128:256]
        onesm = cb_sb[:, 256:384]
        ones64 = cb_sb[0:64, 384:448]
        rotT = cf_sb[0:64, 0:64]
        def GM_(c): return cf_sb[:, 64 + c * 64: 64 + (c + 1) * 64]
        def OWN_(c): return cf_sb[:, 320 + c * 64: 320 + (c + 1) * 64]
        def corr_(g): return cf_sb[:, 576 + g * 16: 576 + (g + 1) * 16]
        hasprev = cf_sb[:, 640:641]
        onesf = cf_sb[:, 641:769]
        eps_ap = cf_sb[:, 769:770]
        def smc(l, col, rows=128): return sm_sb[0:rows, l * NS + col: l * NS + col + 1]

        mm_count = [0]

        def mm(out, lhsT, rhs, start, stop, reads, wbuf):
            P.op("pe", lambda e, o=out, a=lhsT, b=rhs, s=start, t=stop:
                 e.matmul(o, a, b, start=s, stop=t), reads=reads, writes=[wbuf])

        def load_w(src_ap, ncols):
            ap, b = stgR.next()
            P.dma("pool", ap[:, 0:ncols], src_ap, reads=[], writes=[b])
            return ap, b

        P.dma("sp", sm_sb[:], sm_d[:, :], writes=[smB])
        P.dma("sp", cf_sb[:], cf_d[:, :], writes=[cfB])
        P.dma("sp", cb_sb[:], cb_d[:, :], writes=[cbB])
        for kt in range(KT):
            P.dma("sp", x_sb[:, kt * T:(kt + 1) * T], xT[kt * 128:(kt + 1) * 128, :], writes=xB[kt])
        P.op("dve", lambda e: e.memset(biasm[:], 0.0), writes=[biasB])
        for i in range(2):
            a, b = vstR.next()
            P.op("dve", lambda e, a=a: e.memset(a[:], 0.0), writes=[b])
            for pl in range(2):
                P.op("dve", lambda e, a=a, pl=pl: e.memset(a[:, pl * 193 + 64: pl * 193 + 66], 1.0), writes=[b])
        P.op("dve", lambda e: e.memset(xs_sb[:], 0.0), writes=[xsB])

        def emit_norm(l, gcol):
            for c in range(NCH):
                pa, pb_ = psG.next()
                for kt in range(KT):
                    sq, sqb = sqR.next()
                    P.op("act", lambda e, o=sq, i=xv_(kt, c): e.activation(out=o, in_=i, func=AF.Square),
                         reads=[xB[kt][c]], writes=[sqb])
                    mm(pa[:], onesm, sq, kt == 0, kt == KT - 1, [sqb, cbB], pb_)
                P.op("act", lambda e, i=pa: e.activation(out=rstd_ap, in_=i[:], func=AF.Ln, bias=eps_ap),
                     reads=[pb_, cfB], writes=[rstdB])
                P.op("act", lambda e: e.activation(out=rstd_ap, in_=rstd_ap, func=AF.Exp, scale=-0.5),
                     reads=[], writes=[rstdB])
                for kt in range(KT):
                    P.op("dve", lambda e, o=hv_(kt, c), i=xv_(kt, c), g=smc(l, gcol + kt):
                         e.scalar_tensor_tensor(o, i, g, rstd_ap, ALU.mult, ALU.mult),
                         reads=[xB[kt][c], rstdB, smB], writes=[hB[kt][c]])

        def emit_ffn(l, which):
            emit_norm(l, 0 if which == 0 else 16)
            for part, (j0, nk) in enumerate(PARTS):
                for jj in range(nk):
                    j = j0 + jj
                    w, wb_ = load_w(gu_d[l, which, j], 2048)
                    for c in range(NCH):
                        pa, pab = psG.next()
                        for kt in range(KT):
                            mm(pa[:], w[:, kt * 256: kt * 256 + 128], hv_(kt, c), kt == 0, kt == KT - 1,
                               [wb_, hB[kt][c]], pab)
                        pb2, pbb = psG.next()
                        for kt in range(KT):
                            mm(pb2[:], w[:, kt * 256 + 128: kt * 256 + 256], hv_(kt, c), kt == 0, kt == KT - 1,
                               [wb_, hB[kt][c]], pbb)
                        tf, tfb = tfR.next()
                        P.op("act", lambda e, o=tf, i=pa: e.activation(out=o[:, 0:CW], in_=i[:], func=AF.Silu),
                             reads=[pab], writes=[tfb])
                        P.op("dve", lambda e, o=hff_(jj, c), a=tf, b=pb2:
                             e.tensor_tensor(o, a[:, 0:CW], b[:], ALU.mult),
                             reads=[tfb, pbb], writes=[hffB[jj][c]])
                for m in range(8):
                    w, wb_ = load_w(dn_d[l, which, part, m][:, 0:nk * 128], nk * 128)
                    for c in range(NCH):
                        po, pob = psG.next()
                        for kk in range(nk):
                            mm(po[:], w[:, kk * 128:(kk + 1) * 128], hff_(kk, c), kk == 0, kk == nk - 1,
                               [wb_, hffB[kk][c]], pob)
                        P.op("dve", lambda e, o=xv_(m, c), i=po:
                             e.scalar_tensor_tensor(o, i[:], 0.5, o, ALU.mult, ALU.add),
                             reads=[pob], writes=[xB[m][c]])

        def normrope_g(l, pin, pinb, gcol, c, sqring, out, psr=None):
            psr = psr or psG
            sq, sqb = sqring.next()
            P.op("act", lambda e, o=sq, i=pin: e.activation(out=o[0:64, :], in_=i[0:64, :], func=AF.Square),
                 reads=[pinb], writes=[sqb])
            yield
            ps_, psb_ = psr.next()
            mm(ps_[0:64, :], ones64, sq[0:64, :], True, True, [sqb, cbB], psb_)
            yield
            t2, t2b = tfR.next()
            P.op("act", lambda e, o=t2, i=ps_: e.activation(out=o[0:64, 0:CW], in_=i[0:64, :], func=AF.Ln, bias=eps_ap[0:64, :]),
                 reads=[psb_, cfB], writes=[t2b])
            P.op("act", lambda e, o=t2: e.activation(out=o[0:64, 0:CW], in_=o[0:64, 0:CW], func=AF.Exp, scale=-0.5),
                 reads=[], writes=[t2b])
            yield
            t3, t3b = tfR.next()
            P.op("dve", lambda e, o=t3, i=pin, r=t2, g=smc(l, gcol, 64):
                 e.scalar_tensor_tensor(o[0:64, 0:CW], i[0:64, :], g, r[0:64, 0:CW], ALU.mult, ALU.mult),
                 reads=[pinb, t2b, smB], writes=[t3b])
            yield
            pr, prb = psr.next()
            mm(pr[0:64, :], rotT, t3[0:64, 0:CW], True, True, [t3b, cfB], prb)
            yield
            P.op("dve", lambda e, o=t2, i=pr, s=sin_(c): e.tensor_tensor(o[0:64, 0:CW], i[0:64, :], s, ALU.mult),
                 reads=[prb, sinB], writes=[t2b])
            P.op("dve", lambda e, o=t3, s=cos_(c): e.tensor_tensor(o[0:64, 0:CW], o[0:64, 0:CW], s, ALU.mult),
                 reads=[cosB], writes=[t3b])
            yield
            P.op("dve", lambda e, o=t3, a=t2: e.tensor_tensor(o[0:64, 0:CW], o[0:64, 0:CW], a[0:64, 0:CW], ALU.add),
                 reads=[t2b], writes=[t3b])
            out["t"] = t3; out["b"] = t3b

        def normrope(l, pin, pinb, gcol, c):
            out = {}
            for _ in normrope_g(l, pin, pinb, gcol, c, sqR, out):
                pass
            return out["t"], out["b"]

        def emit_mixer(l):
            emit_norm(l, 8)
            P.barrier()
            P.dma("sp", R2[0:64, 0:T], cs_d[:, 0:T], writes=[cosB])
            P.dma("sp", R2[0:64, T:2 * T], cs_d[:, T:2 * T], writes=[sinB])
            def kchain_g(head, c, w, wb_):
                hh = head % 2
                pk, pkb = psA.next()
                for kt in range(KT):
                    mm(pk[0:64, :], w[:, kt * 128 + hh * 64: kt * 128 + hh * 64 + 64], hv_(kt, c),
                       kt == 0, kt == KT - 1, [wb_, hB[kt][c]], pkb)
                yield
                o2 = {}
                yield from normrope_g(l, pk, pkb, 45, c, sqR, o2, psA)
                kf, kfb = o2["t"], o2["b"]
                yield
                ks, ksb = kstR.next()
                P.op("act", lambda e, o=ks, i=kf: e.activation(out=o[:], in_=i[0:64, 0:CW], func=AF.Copy),
                     reads=[kfb], writes=[ksb])
                P.dma("sp", xk_own[head * 64:(head + 1) * 64, c * CW:(c + 1) * CW], ks[:],
                      reads=[ksb], writes=[xkB[head][c]])
                P.op("dve", lambda e, i=kf, o=xs_sb[0:64, head * 8 + 2 * c: head * 8 + 2 * c + 2]:
                     e.reduce_sum(o, i[0:64, 0:CW].rearrange("p (a b) -> p a b", a=2), AX.X),
                     reads=[kfb], writes=[xsB])

            for tp in range(4):
                w, wb_ = load_w(win_d[l, 8 + tp], 1024)
                todo = [kchain_g(2 * tp + hh, c, w, wb_) for hh in range(2) for c in range(NCH)]
                active = []
                while todo or active:
                    while todo and len(active) < 2:
                        active.append(todo.pop(0))
                    for g_ in list(active):
                        try:
                            next(g_)
                        except StopIteration:
                            active.remove(g_)
            if MSTOP <= 1:
                P.barrier(); return
            for g in range(4):
                w, wb_ = load_w(win_d[l, g], 1024)
                pu, pub = psG.next()
                for kt in range(KT):
                    mm(pu[:, 0:16], w[:, kt * 128:(kt + 1) * 128], hv_(kt, 3)[:, CW - 16:CW], kt == 0, kt == KT - 1,
                       [wb_, hB[kt][3]], pub)
                P.op("act", lambda e, i=pu, o=xs_sb[:, 64 + g * 16: 64 + (g + 1) * 16]:
                     e.activation(out=o, in_=i[:, 0:16], func=AF.Copy), reads=[pub], writes=[xsB])
            P.op("dve", lambda e: e.tensor_scalar(xs_sb[0:64, 0:64], xs_sb[0:64, 0:64], 1.0 / 256.0, None, ALU.mult),
                 reads=[], writes=[xsB])
            P.dma("sp", xs_own[0:128, :], xs_sb[:], reads=[xsB], writes=[xsoB])
            pre_v = [load_w(wv_d[l, vh], 2048) for vh in range(2)]
            P.collective(xk_own_f.ap().opt(), xk_all_f.ap().opt(),
                         reads=[b for r in xkB for b in r], writes=[xkaB])
            P.collective(xs_own_f.ap().opt(), xs_all_f.ap().opt(), reads=[xsoB], writes=[xsaB])
            for vh in range(2):
                w, wb_ = pre_v[vh]
                for tt in range(16):
                    c, o4 = tt // 4, (tt % 4) * 128
                    pv, pvb = psG.next()
                    for kt in range(KT):
                        mm(pv[:, 0:256], hv_(kt, c)[:, o4:o4 + 128], w[:, kt * 256:(kt + 1) * 256],
                           kt == 0, kt == KT - 1, [wb_, hB[kt][c]], pvb)
                    vs, vsb = vstR.next()
                    for pl in range(2):
                        P.op("act", lambda e, o=vs, i=pv, pl=pl:
                             e.activation(out=o[:, pl * 193: pl * 193 + 64], in_=i[:, pl * 128: pl * 128 + 64], func=AF.Copy),
                             reads=[pvb], writes=[vsb])
                        P.op("act", lambda e, o=vs, i=pv, pl=pl:
                             e.activation(out=o[:, pl * 193 + 129: pl * 193 + 193], in_=i[:, pl * 128 + 64: pl * 128 + 128], func=AF.Copy),
                             reads=[pvb], writes=[vsb])
                    P.dma("sp", xv_own[tt // 8][(tt % 8) * 128:(tt % 8 + 1) * 128, vh * 386:(vh + 1) * 386], vs[:],
                          reads=[vsb], writes=[xvB[tt][vh]])
            if MSTOP <= 2:
                P.barrier(); return
            if MSTOP <= 3:
                P.barrier(); return
            P.dma("sp", xsp_sb[:], xs_all[0:128, :], reads=[xsaB], writes=[xspB])
            pwt, pwb = pw_sb[:], pwB
            P.dma("pool", pwt, pw_d[l], reads=[], writes=[pwB])
            pre_w = [load_w(win_d[l, g], 1024) for g in range(3)]
            for i in range(2):
                P.collective(xv_own_f[i].ap().opt(), xv_all_f[i].ap().opt(),
                             reads=[b for r in xvB[8 * i:8 * i + 8] for b in r], writes=[xvaB[i]])
            P.op("dve", lambda e: e.tensor_copy(kb_all[:, :, 0:8], xsp_sb[0:64, 0:64].rearrange("p (a b) -> p a b", a=8)),
                 reads=[xspB], writes=[kbB])
            P.op("dve", lambda e: e.tensor_copy(kb_all[:, :, 8:16], xs_sb[0:64, 0:64].rearrange("p (a b) -> p a b", a=8)),
                 reads=[xsB], writes=[kbB])
            if MSTOP <= 4:
                P.barrier(); return
            for g in range(4):
                wwin = 2 ** (g + 1)
                w, wb_ = pre_w[g] if g < 3 else load_w(win_d[l, g], 1024)
                ubuf = ubB[g % 2]
                ui = g % 2
                P.op("dve", lambda e, o=ub_(ui, 0, 16), i=xsp_sb[:, 64 + g * 16: 64 + (g + 1) * 16]:
                     e.tensor_scalar(o, i, hasprev, None, ALU.mult), reads=[xspB, cfB], writes=[ubuf])
                for c in range(NCH):
                    pu, pub = psG.next()
                    for kt in range(KT):
                        mm(pu[:], w[:, kt * 128:(kt + 1) * 128], hv_(kt, c), kt == 0, kt == KT - 1,
                           [wb_, hB[kt][c]], pub)
                    P.op("act", lambda e, i=pu, o=ub_(ui, 16 + c * CW, 16 + (c + 1) * CW):
                         e.activation(out=o, in_=i[:], func=AF.Copy), reads=[pub], writes=[ubuf])
                for c in range(NCH):
                    b0 = c * CW
                    cur = (lambda ui_, b0_: (lambda a, b_: ub_(ui_, b0_ + a, b0_ + b_)))(ui, b0)
                    curb = ubuf
                    sh = 1
                    while sh < wwin:
                        nt, ntb = tfR.next()
                        P.op("dve", lambda e, o=nt, hi=cur(sh, 528), lo=cur(0, 528 - sh), sh=sh:
                             e.tensor_tensor(o[:, sh:528], hi, lo, ALU.add), reads=[curb], writes=[ntb])
                        cur = (lambda nt_: (lambda a, b_: nt_[:, a:b_]))(nt)
                        curb = ntb
                        sh *= 2
                    if c == 0:
                        P.op("dve", lambda e, o=cur(16, 32), cr=corr_(g): e.tensor_tensor(o, o, cr, ALU.mult),
                             reads=[cfB], writes=[curb])
                    dtile, db = ptR.next()
                    P.op("dve", lambda e, o=dtile, s=cur(16, 528), u=ub_(ui, 16 + c * CW, 16 + (c + 1) * CW), sc=1.0 / wwin:
                         e.scalar_tensor_tensor(o, s, sc, u, ALU.mult, ALU.subtract),
                         reads=[curb, ubuf], writes=[db])
                    py, pyb = psG.next()
                    mm(py[:], pwt[:, g * 128:(g + 1) * 128], dtile, True, True, [pwb, db], pyb)
                    P.op("act", lambda e, i=py, o=yp_(g, c), s=smc(l, 40 + g):
                         e.activation(out=o, in_=i[:], func=AF.Copy, scale=s), reads=[pyb, smB], writes=[ypB[g][c]])
            if MSTOP <= 5:
                P.barrier(); return
            wqs = {}

            def prologue_g(c, head, out):
                pair, hh = head // 2, head % 2
                if hh == 0:
                    wqs["w"] = load_w(win_d[l, 4 + pair], 1024)
                wq, wqb = wqs["w"]
                pq, pqb = psG.next()
                for kt in range(KT):
                    mm(pq[0:64, :], wq[:, kt * 128 + hh * 64: kt * 128 + hh * 64 + 64], hv_(kt, c),
                       kt == 0, kt == KT - 1, [wqb, hB[kt][c]], pqb)
                yield
                o2 = {}
                yield from normrope_g(l, pq, pqb, 44, c, kstR, o2)
                qf, qfb = o2["t"], o2["b"]
                yield
                qa, qab = qaR.next()
                P.op("act", lambda e, o=qa, i=qf: e.activation(out=o[0:64, :], in_=i[0:64, 0:CW], func=AF.Copy),
                     reads=[qfb], writes=[qab])
                pg, pgb = psG.next()
                for qt in range(4):
                    mm(pg[:, qt * 16:(qt + 1) * 16], qf[0:64, qt * 128:(qt + 1) * 128], kb_all[:, head, :],
                       True, True, [qfb, kbB], pgb)
                yield
                P.op("dve", lambda e, i=pg, g=GM_(c): e.tensor_tensor(gs_t[:], i[:, 0:64], g, ALU.add),
                     reads=[pgb, cfB], writes=[gsB])
                for qt in range(4):
                    P.op("dve", lambda e, qt=qt: e.max(top8[:, qt, :], gs_t[:, qt * 16:(qt + 1) * 16]),
                         reads=[gsB], writes=[top8B])
                yield
                for qt in range(4):
                    P.op("dve", lambda e, qt=qt: e.tensor_scalar(sel_t[:, qt * 16:(qt + 1) * 16],
                                                                gs_t[:, qt * 16:(qt + 1) * 16],
                                                                top8[:, qt, 2:3], None, ALU.is_ge),
                         reads=[gsB, top8B], writes=[selB])
                P.op("dve", lambda e: e.scalar_tensor_tensor(sel_t[:], gs_t[:], -1e29, sel_t[:], ALU.is_gt, ALU.mult),
                     reads=[gsB], writes=[selB])
                P.op("dve", lambda e, o=OWN_(c): e.tensor_tensor(sel_t[:], sel_t[:], o, ALU.max),
                     reads=[cfB], writes=[selB])
                P.op("dve", lambda e: e.tensor_scalar(biasm[:, :, 64:80], sel_t[:].rearrange("p (a b) -> p a b", a=4),
                                                      -1.0, -NEGB, ALU.add, ALU.mult),
                     reads=[selB], writes=[biasB])
                yield
                pb_, pbb = psG.next()
                for qt in range(4):
                    mm(pb_[0:80, qt * 128:(qt + 1) * 128], biasm[:, qt, :], ident, True, True, [biasB, cbB], pbb)
                yield
                P.op("act", lambda e, o=qa, i=pb_: e.activation(out=o[64:80, :], in_=i[64:80, :], func=AF.Copy),
                     reads=[pbb], writes=[qab])
                out["qa"] = qa; out["qab"] = qab

            def attend(c, head, qa, qab, nxt):
                    pair, hh = head // 2, head % 2
                    pieces = []
                    for pc in range(2):
                        pieces.append((xk_all, xv_all[pc], xkaB, xvaB[pc], pc * 1024, pc * 1024, 8, None))
                    n_own = 4 * (c + 1)
                    for op_ in range(2):
                        nt_ = min(8, n_own - op_ * 8)
                        if nt_ > 0:
                            pieces.append((xk_own, xv_own[op_], None, None, op_ * 1024, T + op_ * 1024, nt_, op_ * 8))
                    po, pob = psO.next()
                    VW = 65 if hh == 0 else 128
                    vc0 = pair * 193 + (0 if hh == 0 else 65)
                    total_tiles = sum(p[6] for p in pieces)
                    tiles = []
                    first = {}
                    for pi, pcs in enumerate(pieces):
                        own0, ntl = pcs[7], pcs[6]
                        for i in range(ntl):
                            q0, diag = 0, False
                            if own0 is not None and own0 + i >= 4 * c:
                                q0, diag = (own0 + i - 4 * c) * 128, True
                            tiles.append((pi, i, q0, diag))
                    kslot, vslot = {}, {}

                    def load_k(pi, head=head):
                        sk, sv, skb, svb, k0, s0, ntl, own0 = pieces[pi]
                        nk = ntl * 128
                        ktEbuf = ktEB[ktR.i]
                        kt_t, ktbuf = ktR.next()
                        kreads = [xkB[head][cc] for cc in range(NCH)] if skb is None else [skb]
                        P.dma("sp", kt_t[0:64, 0:nk], sk[head * 64:(head + 1) * 64, k0:k0 + nk], reads=kreads, writes=[ktbuf])
                        P.dma("sp", kt_t[64:80, 0:nk], E_d[:, s0:s0 + nk], reads=[], writes=[ktEbuf])
                        kslot[pi] = (kt_t, ktbuf, ktEbuf)

                    def load_v(pi, pair=pair, VW=VW, vc0=vc0):
                        sk, sv, skb, svb, k0, s0, ntl, own0 = pieces[pi]
                        nk = ntl * 128
                        vb_t, vbbuf = vbR.next()
                        vreads = [xvB[tt][pair // 2] for tt in range(k0 // 128, k0 // 128 + ntl)] if svb is None else [svb]
                        P.dma("sp", vb_t[:, 0:ntl, 0:VW],
                              sv[0:nk, vc0:vc0 + VW].rearrange("(kt p) c -> p kt c", p=128),
                              reads=vreads, writes=[vbbuf])
                        vslot[pi] = (vb_t, vbbuf)

                    for pi in range(min(2, len(pieces))):
                        load_k(pi); load_v(pi)
                    LA = 3
                    stash = {}
                    n_t = len(tiles)
                    for t in range(n_t + LA):
                        if nxt is not None and t >= 3:
                            next(nxt, None)
                        if t < n_t:
                            pi, i, q0, diag = tiles[t]
                            if i == 0 and pi >= 1 and pi + 1 < len(pieces):
                                load_k(pi + 1)
                            kt_t, ktbuf, ktEbuf = kslot[pi]
                            pss, pssb = psS.next()
                            mm(pss[:, q0:CW], kt_t[0:80, i * 128:(i + 1) * 128], qa[0:80, q0:CW], True, True,
                               [ktbuf, ktEbuf, qab], pssb)
                            pt, ptb = ptR.next()
                            P.op("act", lambda e, o=pt, i_=pss, q0=q0:
                                 e.activation(out=o[:, q0:CW], in_=i_[:, q0:CW], func=AF.Exp, scale=0.125),
                                 reads=[pssb], writes=[ptb])
                            if diag:
                                P.op("dve", lambda e, o=pt, q0=q0:
                                     e.tensor_tensor(o[:, q0:q0 + 128], o[:, q0:q0 + 128], tri, ALU.mult),
                                     reads=[cbB], writes=[ptb])
                            stash[t] = (pt, ptb)
                        tp = t - LA
                        if tp >= 0:
                            pi, i, q0, diag = tiles[tp]
                            if i == 0 and pi >= 1 and pi + 1 < len(pieces):
                                load_v(pi + 1)
                            vb_t, vbbuf = vslot[pi]
                            pt, ptb = stash.pop(tp)
                            P.op("pe", lambda e, o=po, v=vb_t, i=i, VW=VW, p_=pt, q0=q0, s=(tp == 0), t_=(tp == n_t - 1):
                                 e.matmul(o[0:VW, q0:CW], v[:, i, 0:VW], p_[:, q0:CW], start=s, stop=t_, skip_group_check=True),
                                 reads=[vbbuf, ptb], writes=[pob])
                    dr = 64 if hh == 0 else 0
                    r0, r1 = (0, 64) if hh == 0 else (64, 128)
                    if nxt is not None:
                        for _ in nxt:
                            pass
                    rd, rdb = tfR.next()
                    P.op("act", lambda e, o=rd, i=po, dr=dr: e.activation(out=o[dr:dr + 1, 0:CW], in_=i[dr:dr + 1, :], func=AF.Ln),
                         reads=[pob], writes=[rdb])
                    P.op("act", lambda e, o=rd, dr=dr: e.activation(out=o[dr:dr + 1, 0:CW], in_=o[dr:dr + 1, 0:CW], func=AF.Exp, scale=-1.0),
                         reads=[], writes=[rdb])
                    pbc, pbcb = psG.next()
                    mm(pbc[0:r1, :], onesf[dr:dr + 1, 0:r1], rd[dr:dr + 1, 0:CW], True, True, [rdb, cfB], pbcb)
                    on, onb = tfR.next()
                    P.op("act", lambda e, o=on, i=po, r0=r0, r1=r1: e.activation(out=o[r0:r1, 0:CW], in_=i[r0:r1, :], func=AF.Copy),
                         reads=[pob], writes=[onb])
                    P.op("dve", lambda e, o=ya_(pair, c, r0, r1), a=on, b=pbc, r0=r0, r1=r1:
                         e.tensor_tensor(o, a[r0:r1, 0:CW], b[r0:r1, :], ALU.mult),
                         reads=[onb, pbcb], writes=[yaB[pair][c]])

            order = [(c, head) for c in range(NCH) for head in range(8)]
            st_cur = {}
            for _ in prologue_g(order[0][0], order[0][1], st_cur):
                pass
            for idx, (c, head) in enumerate(order):
                st_nxt = {}
                nxt = prologue_g(order[idx + 1][0], order[idx + 1][1], st_nxt) if idx + 1 < len(order) else None
                attend(c, head, st_cur["qa"], st_cur["qab"], nxt)
                st_cur = st_nxt
            if MSTOP <= 6:
                P.barrier(); return
            P.barrier()
            for m in range(8):
                w0, w0b = load_w(win_d[l, 16 + m], 1024)
                w1, w1b = load_w(win_d[l, 24 + m], 1024)
                wpa, wpab = stgR.next()
                P.dma("pool", wpa[:, 0:512], wbp_d[l, m], reads=[], writes=[wpab])
                P.dma("pool", wpa[:, 512:1024], wba_d[l, m], reads=[], writes=[wpab])
                for c in range(NCH):
                    g0, g0b = psG.next()
                    for kt in range(KT):
                        mm(g0[:], w0[:, kt * 128:(kt + 1) * 128], hv_(kt, c), kt == 0, kt == KT - 1, [w0b, hB[kt][c]], g0b)
                    ta, tab = tfR.next()
                    P.op("act", lambda e, o=ta, i=g0, b=smc(l, 24 + m): e.activation(out=o[:, 0:CW], in_=i[:], func=AF.Sigmoid, bias=b),
                         reads=[g0b, smB], writes=[tab])
                    g1, g1b = psG.next()
                    for kt in range(KT):
                        mm(g1[:], w1[:, kt * 128:(kt + 1) * 128], hv_(kt, c), kt == 0, kt == KT - 1, [w1b, hB[kt][c]], g1b)
                    tb, tbb = tfR.next()
                    P.op("act", lambda e, o=tb, i=g1, b=smc(l, 32 + m): e.activation(out=o[:, 0:CW], in_=i[:], func=AF.Sigmoid, bias=b),
                         reads=[g1b, smB], writes=[tbb])
                    bp, bpb = psG.next()
                    for kk in range(4):
                        mm(bp[:], wpa[:, kk * 128:(kk + 1) * 128], yp_(kk, c), kk == 0, kk == 3, [wpab, ypB[kk][c]], bpb)
                    P.op("dve", lambda e, a=ta, i=bp: e.tensor_tensor(a[:, 0:CW], a[:, 0:CW], i[:], ALU.mult),
                         reads=[bpb], writes=[tab])
                    ba, bab = psG.next()
                    for kk in range(4):
                        mm(ba[:], wpa[:, 512 + kk * 128: 512 + (kk + 1) * 128], ya_(kk, c, 0, 128), kk == 0, kk == 3, [wpab, yaB[kk][c]], bab)
                    P.op("dve", lambda e, a=tb, i=ba: e.tensor_tensor(a[:, 0:CW], a[:, 0:CW], i[:], ALU.mult),
                         reads=[bab], writes=[tbb])
                    P.op("dve", lambda e, o=mg_(m, c), a=ta, b=tb: e.tensor_tensor(o, a[:, 0:CW], b[:, 0:CW], ALU.add),
                         reads=[tab, tbb], writes=[mgB[m][c]])
            for m2 in range(8):
                w, wb_ = load_w(wout_d[l, m2], 1024)
                for c in range(NCH):
                    po, pob = psG.next()
                    for m in range(8):
                        mm(po[:], w[:, m * 128:(m + 1) * 128], mg_(m, c), m == 0, m == 7, [wb_, mgB[m][c]], pob)
                    P.op("dve", lambda e, o=xv_(m2, c), i=po: e.tensor_tensor(o, i[:], o, ALU.add),
                         reads=[pob], writes=[xB[m2][c]])
            P.barrier()

        ph = 0
        for l in range(L):
            for f in (lambda: emit_ffn(l, 0), lambda: emit_mixer(l), lambda: emit_ffn(l, 1)):
                if STOP is None or ph < STOP:
                    f()
                ph += 1
            P.barrier()
        for kt in range(KT):
            P.dma("sp", outT[kt * 128:(kt + 1) * 128, :], x_sb[:, kt * T:(kt + 1) * T], reads=xB[kt], writes=[])
        P.barrier()

        with nc.Block() as block:
            @block.tensor
            def _(e):
                P.replay("pe", e)

            @block.scalar
            def _(e):
                P.replay("act", e)

            @block.vector
            def _(e):
                P.replay("dve", e)

            @block.gpsimd
            def _(e):
                P.replay("pool", e)

            @block.sync
            def _(e):
                P.replay("sp", e)
    return nc


def _prep_shared(inp):
    f = np.float32
    gu = np.empty((L, 2, NJ, 128, 2048), f)
    dn = np.zeros((L, 2, 3, 8, 128, 1024), f)
    for wi, (ngu, ndn) in enumerate((("ffn1_w_gate_up", "ffn1_w_down"), ("ffn2_w_gate_up", "ffn2_w_down"))):
        W = np.asarray(inp[ngu], f)
        A = W[:, :, :FF].reshape(L, KT, 128, NJ, 128)
        B = W[:, :, FF:].reshape(L, KT, 128, NJ, 128)
        t = np.stack([A, B], axis=4)
        gu[:, wi] = t.transpose(0, 3, 2, 1, 4, 5).reshape(L, NJ, 128, 2048)
        Wd = np.asarray(inp[ndn], f).reshape(L, NJ, 128, 8, 128)
        for part, (j0, nk) in enumerate(PARTS):
            t = Wd[:, j0:j0 + nk].transpose(0, 3, 2, 1, 4)
            dn[:, wi, part, :, :, :nk * 128] = t.reshape(L, 8, 128, nk * 128)
    Win = np.asarray(inp["w_in"], f)
    win = Win.reshape(L, KT, 128, 32, 128).transpose(0, 3, 2, 1, 4).reshape(L, 32, 128, 1024)
    Wv = Win[:, :, 1536:2048].reshape(L, KT, 128, 2, 256)
    wv = Wv.transpose(0, 3, 2, 1, 4).reshape(L, 2, 128, 2048)
    def br(name):
        W = np.asarray(inp[name], f).reshape(L, 4, 128, 8, 128)
        return W.transpose(0, 3, 2, 1, 4).reshape(L, 8, 128, 512)
    wout = np.asarray(inp["w_out"], f).reshape(L, KT, 128, 8, 128).transpose(0, 3, 2, 1, 4).reshape(L, 8, 128, 1024)
    pw = np.asarray(inp["pool_w"], f).transpose(0, 2, 1, 3).reshape(L, 128, 512)
    sm = np.zeros((128, L * NS), f)
    for l in range(L):
        o = l * NS
        sm[:, o + 0:o + 8] = np.asarray(inp["ffn1_norm"], f)[l].reshape(8, 128).T
        sm[:, o + 8:o + 16] = np.asarray(inp["mix_norm"], f)[l].reshape(8, 128).T
        sm[:, o + 16:o + 24] = np.asarray(inp["ffn2_norm"], f)[l].reshape(8, 128).T
        sm[:, o + 24:o + 40] = np.asarray(inp["b_gate"], f)[l].reshape(16, 128).T
        sm[:, o + 40:o + 44] = np.asarray(inp["pool_scale"], f)[l].reshape(4, 128).T
        sm[0:64, o + 44] = np.asarray(inp["q_norm"], f)[l]
        sm[0:64, o + 45] = np.asarray(inp["k_norm"], f)[l]
    return dict(gu=np.ascontiguousarray(gu), dn=dn, win=np.ascontiguousarray(win), wv=np.ascontiguousarray(wv),
                wbp=np.ascontiguousarray(br("w_branch_pool")), wba=np.ascontiguousarray(br("w_branch_attn")),
                wout=np.ascontiguousarray(wout), pw=np.ascontiguousarray(pw), smalls=sm)


def _consts(half):
    f = np.float32
    cf = np.zeros((128, NCF), f)
    for m in range(64):
        if m < 32:
            cf[m + 32, m] = -1.0
        else:
            cf[m - 32, m] = 1.0
    GM = np.full((16, 16), -1e30, f)
    OWN = np.zeros((16, 16), f)
    for qt in range(16):
        sbq = 8 + qt // 2
        lo = 0 if half == 1 else 8
        GM[qt, lo:sbq] = 0.0
        OWN[qt, sbq] = 1.0
    cf[:, 64:320] = GM.reshape(1, 256)
    cf[:, 320:576] = OWN.reshape(1, 256)
    corr = np.ones((4, 16), f)
    if half == 0:
        for g in range(4):
            w = 2 ** (g + 1)
            for t in range(16):
                corr[g, t] = w / min(t + 1, w)
    cf[:, 576:640] = corr.reshape(1, 64)
    cf[:, 640] = float(half)
    cf[:, 641:769] = 1.0
    cf[:, 769] = EPS
    cb = np.zeros((128, NCB), f)
    cb[:, 0:128] = np.eye(128, dtype=f)
    cb[:, 128:256] = np.triu(np.ones((128, 128), f))
    cb[:, 256:384] = 1.0 / 1024.0
    cb[0:64, 384:448] = 1.0 / 64.0
    hd = 32
    inv_freq = (1.0 / (np.float32(10000.0) ** (np.arange(hd, dtype=f) * f(2.0 / 64)))).astype(f)
    pos = (np.arange(T, dtype=f) + f(half * T)).astype(f)
    ang = (pos[:, None] * inv_freq[None, :]).astype(f)
    cosv = np.cos(ang).astype(f).T
    sinv = np.sin(ang).astype(f).T
    cs = np.zeros((64, 2 * T), f)
    cs[0:32, 0:T] = cosv; cs[32:64, 0:T] = cosv
    cs[0:32, T:] = sinv; cs[32:64, T:] = sinv
    E = np.zeros((16, 4096), f)
    for j in range(16):
        E[j, j * 256:(j + 1) * 256] = 1.0
    return dict(cf32=cf, cb16=cb.astype(ml_dtypes.bfloat16), cs=cs, E=E.astype(ml_dtypes.bfloat16))


def kernel(**inputs):
    x = np.asarray(inputs["x"], np.float32)
    shared = _prep_shared(inputs)
    consts = [_consts(0), _consts(1)]
    in_maps = []
    for core in range(8):
        b, half = core // 2, core % 2
        m = dict(shared)
        m.update(consts[half])
        m["xT"] = np.ascontiguousarray(x[b, half * T:(half + 1) * T, :].T)
        in_maps.append(m)
    nc = build_program()
    res = run_bass_kernel_spmd(nc, in_maps, core_ids=list(range(8)))
    out = np.empty_like(x)
    for core in range(8):
        b, half = core // 2, core % 2
        out[b, half * T:(half + 1) * T, :] = np.asarray(res.results[core]["outT"], np.float32).T
    return out
```
